# Optimizing a Trainium2 kernel written in Bass

```python
import math
import jax, jax.numpy as jnp
from jax import lax
import numpy as np

D_MODEL = 2048
BATCH = 1
SEQ = 8192
DEPTH = 1
DEC_BATCH = 1
DEC_SEQ = 16384
PAST_LEN = 128

DA_HEADS = 8
DA_QK_DIM = 64
DA_V_DIM = 2 * DA_QK_DIM
DA_ROT_DIM = DA_QK_DIM // 4
MLA_HEADS = 8
MLA_Q_RANK = 512
MLA_KV_RANK = 512
MLA_NOPE_DIM = 128
MLA_ROPE_DIM = 64
MLA_V_DIM = 128
ROPE_THETA = 500000.0
Q_BLOCK = 128
NORM_EPS = 1e-6
FFN_HIDDEN = -(-8 * D_MODEL // (3 * 256)) * 256
DA_Q_COLS = DA_HEADS * 2 * DA_QK_DIM
DA_K_COLS = DA_HEADS * 2 * DA_QK_DIM
DA_V_COLS = DA_HEADS * DA_V_DIM
GATE_COLS = 2 * D_MODEL
IN_COLS = DA_Q_COLS + DA_K_COLS + DA_V_COLS + MLA_Q_RANK + MLA_KV_RANK + MLA_ROPE_DIM + GATE_COLS
DA_OUT = DA_HEADS * DA_V_DIM
MLA_OUT = MLA_HEADS * MLA_V_DIM

kernel_name = "hybrid_diffattn_mla_gated_encoder"


def rmsnorm(x, g):
    xf = x.astype(jnp.float32)
    y = xf * lax.rsqrt(jnp.mean(xf * xf, axis=-1, keepdims=True) + NORM_EPS)
    return (y * g.astype(jnp.float32)).astype(x.dtype)


def rope(x, rot_dim):
    s = x.shape[1]
    pos = jnp.arange(s, dtype=jnp.float32)
    inv_freq = ROPE_THETA ** (-jnp.arange(0, rot_dim, 2, dtype=jnp.float32) / rot_dim)
    ang = pos[:, None] * inv_freq[None, :]
    ang = ang.reshape((s,) + (1,) * (x.ndim - 3) + (rot_dim // 2,))
    cos, sin = jnp.cos(ang), jnp.sin(ang)
    xr = x[..., :rot_dim].astype(jnp.float32)
    x1, x2 = xr[..., : rot_dim // 2], xr[..., rot_dim // 2:]
    rot = jnp.concatenate([x1 * cos - x2 * sin, x2 * cos + x1 * sin], axis=-1).astype(x.dtype)
    return jnp.concatenate([rot, x[..., rot_dim:]], axis=-1)


def to_blocks(t):
    b, s = t.shape[:2]
    return jnp.moveaxis(t.reshape((b, s // Q_BLOCK, Q_BLOCK) + t.shape[2:]), 1, 0)


def from_blocks(t):
    t = jnp.moveaxis(t, 0, 1)
    return t.reshape((t.shape[0], t.shape[1] * t.shape[2]) + t.shape[3:])


def softmax_f32(s, scale):
    return jax.nn.softmax(s.astype(jnp.float32) * scale, axis=-1)


def diff_attention(q, k, v, lam, subln_g, lambda_init):
    b, s = q.shape[:2]
    scale = DA_QK_DIM ** -0.5
    k1, k2 = k[..., 0, :], k[..., 1, :]

    def block(qb):
        a1 = softmax_f32(jnp.einsum('bqhd,bkhd->bhqk', qb[..., 0, :], k1), scale)
        a2 = softmax_f32(jnp.einsum('bqhd,bkhd->bhqk', qb[..., 1, :], k2), scale)
        a = (a1 - lam * a2).astype(v.dtype)
        return jnp.einsum('bhqk,bkhd->bqhd', a, v)

    o = from_blocks(lax.map(block, to_blocks(q)))
    o = rmsnorm(o, subln_g) * (1.0 - lambda_init)
    return o.reshape(b, s, DA_OUT)


def mla_attention(c_q, c_kv, k_rope, q_norm_g, w_q_b, kv_norm_g, w_kv_b):
    b, s = c_q.shape[:2]
    q = (rmsnorm(c_q, q_norm_g) @ w_q_b).reshape(b, s, MLA_HEADS, MLA_NOPE_DIM + MLA_ROPE_DIM)
    q_nope, q_rope = q[..., :MLA_NOPE_DIM], rope(q[..., MLA_NOPE_DIM:], MLA_ROPE_DIM)
    kv = (rmsnorm(c_kv, kv_norm_g) @ w_kv_b).reshape(b, s, MLA_HEADS, MLA_NOPE_DIM + MLA_V_DIM)
    k_nope, v = kv[..., :MLA_NOPE_DIM], kv[..., MLA_NOPE_DIM:]
    k_r = rope(k_rope[:, :, None, :], MLA_ROPE_DIM)[:, :, 0, :]
    scale = (MLA_NOPE_DIM + MLA_ROPE_DIM) ** -0.5

    def block(qs):
        qn, qr = qs
        sc = jnp.einsum('bqhd,bkhd->bhqk', qn, k_nope) + jnp.einsum('bqhd,bkd->bhqk', qr, k_r)
        p = softmax_f32(sc, scale).astype(v.dtype)
        return jnp.einsum('bhqk,bkhd->bqhd', p, v)

    o = from_blocks(lax.map(block, (to_blocks(q_nope), to_blocks(q_rope))))
    return o.reshape(b, s, MLA_OUT)


def encoder_layer(x, lambda_init, attn_norm_g, w_in, da_lambda_q1, da_lambda_k1, da_lambda_q2,
                  da_lambda_k2, da_subln_g, mla_q_norm_g, mla_w_q_b, mla_kv_norm_g, mla_w_kv_b,
                  w_branch_da, w_branch_mla, w_out, ffn_norm_g, w_gate, w_up, w_down):
    b, s, _ = x.shape
    h = rmsnorm(x, attn_norm_g)
    proj = h @ w_in
    cuts = np.cumsum([DA_Q_COLS, DA_K_COLS, DA_V_COLS, MLA_Q_RANK, MLA_KV_RANK, MLA_ROPE_DIM]).tolist()
    q_da, k_da, v_da, c_q, c_kv, k_rope, gate_logits = jnp.split(proj, cuts, axis=-1)
    q_da = rope(q_da.reshape(b, s, DA_HEADS, 2, DA_QK_DIM), DA_ROT_DIM)
    k_da = rope(k_da.reshape(b, s, DA_HEADS, 2, DA_QK_DIM), DA_ROT_DIM)
    v_da = v_da.reshape(b, s, DA_HEADS, DA_V_DIM)
    lam = (jnp.exp(jnp.sum(da_lambda_q1.astype(jnp.float32) * da_lambda_k1.astype(jnp.float32)))
           - jnp.exp(jnp.sum(da_lambda_q2.astype(jnp.float32) * da_lambda_k2.astype(jnp.float32)))
           + lambda_init)
    o_da = diff_attention(q_da, k_da, v_da, lam, da_subln_g, lambda_init)
    o_mla = mla_attention(c_q, c_kv, k_rope, mla_q_norm_g, mla_w_q_b, mla_kv_norm_g, mla_w_kv_b)
    gates = jax.nn.sigmoid(gate_logits.astype(jnp.float32)).astype(x.dtype)
    g_da, g_mla = gates[..., :D_MODEL], gates[..., D_MODEL:]
    merged = g_da * (o_da @ w_branch_da) + g_mla * (o_mla @ w_branch_mla)
    x = x + merged @ w_out
    h = rmsnorm(x, ffn_norm_g)
    x = x + (jax.nn.silu(h @ w_gate) * (h @ w_up)) @ w_down
    return x


def trunk(x, attn_norm_g, w_in, da_lambda_q1, da_lambda_k1, da_lambda_q2, da_lambda_k2, da_subln_g,
          mla_q_norm_g, mla_w_q_b, mla_kv_norm_g, mla_w_kv_b, w_branch_da, w_branch_mla, w_out,
          ffn_norm_g, w_gate, w_up, w_down, final_norm_g):
    for l in range(DEPTH):
        lambda_init = 0.8 - 0.6 * math.exp(-0.3 * l)
        x = encoder_layer(x, lambda_init, attn_norm_g[l], w_in[l], da_lambda_q1[l], da_lambda_k1[l],
                          da_lambda_q2[l], da_lambda_k2[l], da_subln_g[l], mla_q_norm_g[l], mla_w_q_b[l],
                          mla_kv_norm_g[l], mla_w_kv_b[l], w_branch_da[l], w_branch_mla[l], w_out[l],
                          ffn_norm_g[l], w_gate[l], w_up[l], w_down[l])
    return rmsnorm(x, final_norm_g)


def setup_inputs(seed: int = 0) -> dict:
    key = jax.random.key(seed)
    ks = jax.random.split(key, 24)

    def w(k, shape, fan_in):
        return jax.random.normal(k, shape, jnp.float32) * fan_in ** -0.5

    def gain(k, shape):
        return 1.0 + 0.02 * jax.random.normal(k, shape, jnp.float32)

    def lam(k):
        return 0.1 * jax.random.normal(k, (DEPTH, DA_QK_DIM), jnp.float32)

    return {
        "x_prompt": jax.random.normal(ks[0], (BATCH, SEQ, D_MODEL), jnp.float32),
        "x_sample": jax.random.normal(ks[1], (DEC_BATCH, DEC_SEQ, D_MODEL), jnp.float32),
        "attn_norm_g": gain(ks[2], (DEPTH, D_MODEL)),
        "w_in": w(ks[3], (DEPTH, D_MODEL, IN_COLS), D_MODEL),
        "da_lambda_q1": lam(ks[4]),
        "da_lambda_k1": lam(ks[5]),
        "da_lambda_q2": lam(ks[6]),
        "da_lambda_k2": lam(ks[7]),
        "da_subln_g": gain(ks[8], (DEPTH, DA_V_DIM)),
        "mla_q_norm_g": gain(ks[9], (DEPTH, MLA_Q_RANK)),
        "mla_w_q_b": w(ks[10], (DEPTH, MLA_Q_RANK, MLA_HEADS * (MLA_NOPE_DIM + MLA_ROPE_DIM)), MLA_Q_RANK),
        "mla_kv_norm_g": gain(ks[11], (DEPTH, MLA_KV_RANK)),
        "mla_w_kv_b": w(ks[12], (DEPTH, MLA_KV_RANK, MLA_HEADS * (MLA_NOPE_DIM + MLA_V_DIM)), MLA_KV_RANK),
        "w_branch_da": w(ks[13], (DEPTH, DA_OUT, D_MODEL), DA_OUT),
        "w_branch_mla": w(ks[14], (DEPTH, MLA_OUT, D_MODEL), MLA_OUT),
        "w_out": w(ks[15], (DEPTH, D_MODEL, D_MODEL), D_MODEL),
        "ffn_norm_g": gain(ks[16], (DEPTH, D_MODEL)),
        "w_gate": w(ks[17], (DEPTH, D_MODEL, FFN_HIDDEN), D_MODEL),
        "w_up": w(ks[18], (DEPTH, D_MODEL, FFN_HIDDEN), D_MODEL),
        "w_down": w(ks[19], (DEPTH, FFN_HIDDEN, D_MODEL), FFN_HIDDEN),
        "final_norm_g": gain(ks[20], (D_MODEL,)),
    }


def reference(x_prompt, x_sample, attn_norm_g, w_in, da_lambda_q1, da_lambda_k1, da_lambda_q2,
              da_lambda_k2, da_subln_g, mla_q_norm_g, mla_w_q_b, mla_kv_norm_g, mla_w_kv_b,
              w_branch_da, w_branch_mla, w_out, ffn_norm_g, w_gate, w_up, w_down, final_norm_g):
    params = (attn_norm_g, w_in, da_lambda_q1, da_lambda_k1, da_lambda_q2, da_lambda_k2, da_subln_g,
              mla_q_norm_g, mla_w_q_b, mla_kv_norm_g, mla_w_kv_b, w_branch_da, w_branch_mla, w_out,
              ffn_norm_g, w_gate, w_up, w_down, final_norm_g)
    y_prompt = trunk(x_prompt, *params)
    y_sample = trunk(x_sample, *params)
    return (y_prompt, y_sample)
```

```python
import contextlib
import math
import numpy as np
import concourse.bass as bass
import concourse.mybir as mybir
from concourse.bass_utils import run_bass_kernel_spmd

F32 = mybir.dt.float32
BF16 = mybir.dt.bfloat16
I32 = mybir.dt.int32
AF = mybir.ActivationFunctionType
ALU = mybir.AluOpType
AX = mybir.AxisListType

D = 2048
NH = 8
FF = 5632
INC = 8256
EPS = 1e-6
THETA = 500000.0
LAMBDA_INIT = 0.8 - 0.6 * math.exp(0.0)


class LSem:
    def __init__(self, S, name, step):
        self.S, self.name, self.step = S, name, step
        self.epoch = 28000 // step
        self.n = 0
        self.phys = []
        self.persist = False
        self.base = 0
        self.rank = {}

    def target_v(self, v):
        e = (v - 1) // self.epoch
        return self._phys(e), ((v - 1) % self.epoch + 1) * self.step

    def _phys(self, e):
        while len(self.phys) <= e:
            self.phys.append(self.S.nc.alloc_semaphore(name=f"{self.name}_{len(self.phys)}"))
        return self.phys[e]

    def next(self):
        self.n += 1
        return self.n

    def inc_target(self, n):
        return self._phys((n - 1) // self.epoch), self.step

    def wait_target(self, n):
        e = (n - 1) // self.epoch
        return self._phys(e), ((n - 1) % self.epoch + 1) * self.step


class Buf:
    def __init__(self, name=""):
        self.name = name
        self.writers = {}
        self.readers = {}


ENGS = ("pe", "act", "dve", "pool", "sp")


class Sched:
    def __init__(self, nc):
        self.nc = nc
        self.es = {e: LSem(self, "e_" + e, 1) for e in ("pe", "act", "dve", "pool")}
        self.prog = {e: [] for e in ENGS}
        self.seen = {e: {} for e in ENGS}
        self.dma_pool = []
        self.dma_all = []
        self.nops = 0

    def dsem(self, persist=False):
        if self.dma_pool and not persist:
            return self.dma_pool.pop()
        s = LSem(self, f"d{len(self.dma_all)}", 16)
        s.persist = persist
        self.dma_all.append(s)
        return s

    def dsem_free(self, s):
        self.dma_pool.append(s)

    def _wait(self, eng, lsem, n):
        seen = self.seen[eng]
        if seen.get(lsem, 0) >= n:
            return
        seen[lsem] = n
        self.prog[eng].append(("w", lsem, n))

    def op(self, eng, fn, reads=(), writes=(), dsem=None, cowrite=False):
        own = self.es.get(eng) if dsem is None else None
        deps = []
        for b in reads:
            for ls, n in b.writers.items():
                if ls is own and eng == "pe":
                    continue
                deps.append((ls, n))
        for b in writes:
            if not cowrite:
                for ls, n in b.writers.items():
                    if ls is own:
                        continue
                    deps.append((ls, n))
            for ls, n in b.readers.items():
                if ls is own:
                    continue
                deps.append((ls, n))
        for ls, n in deps:
            self._wait(eng, ls, n)
        ls = dsem if dsem is not None else own
        n = ls.next()
        self.prog[eng].append(("o", fn, ls, n))
        self.nops += 1
        for b in reads:
            if b.readers.get(ls, 0) < n:
                b.readers[ls] = n
        for b in writes:
            if cowrite:
                b.writers[ls] = n
            else:
                b.writers = {ls: n}
                b.readers = {}
        return (ls, n)

    def run_block(self, final=False):
        for s in self.dma_all:
            if s.n and (final or not s.persist):
                self._wait("sp", s, s.n)
        for e, s in self.es.items():
            if s.n:
                self._wait("sp", s, s.n)
        nc = self.nc
        prog = self.prog
        comp = set(self.es.values())
        lazy = {self.es["pe"]}
        waited = {ls: set() for ls in comp}
        for e in ENGS:
            for it in prog[e]:
                if it[0] == "w" and it[1] in comp:
                    waited[it[1]].add(it[2])
        for ls in comp:
            if ls in lazy:
                ws = sorted(waited[ls])
            else:
                ws = sorted(it[-1] for e in ENGS for it in prog[e] if it[0] == "o" and it[-2] is ls)
            ls.rank = {n: ls.base + i + 1 for i, n in enumerate(ws)}

        def replay(lst, e):
            for it in lst:
                ls, n = it[-2], it[-1]
                if it[0] == "w":
                    if ls in comp:
                        ph, val = ls.target_v(ls.rank[n])
                    else:
                        ph, val = ls.wait_target(n)
                    e.wait_ge(ph, val)
                else:
                    ins = it[1](e)
                    if ls in comp:
                        if n in ls.rank:
                            ph, _ = ls.target_v(ls.rank[n])
                            ins.then_inc(ph, 1)
                    else:
                        ph, amt = ls.inc_target(n)
                        ins.then_inc(ph, amt)

        with nc.Block() as block:
            @block.sync
            def _(e):
                replay(prog["sp"], e)

            @block.tensor
            def _(e):
                replay(prog["pe"], e)

            @block.scalar
            def _(e):
                replay(prog["act"], e)

            @block.vector
            def _(e):
                replay(prog["dve"], e)

            @block.gpsimd
            def _(e):
                replay(prog["pool"], e)
        self.prog = {e: [] for e in ENGS}
        for ls in comp:
            ls.base += len(ls.rank)
            ls.rank = {}
        for e in ENGS:
            for s in self.dma_all:
                if not s.persist:
                    self.seen[e][s] = s.n
            for s in self.es.values():
                self.seen[e][s] = s.n


class Ring:
    def __init__(self, S, tiles, dma=True):
        self.S = S
        self.slots = [(t, Buf(), S.dsem() if dma else None) for t in tiles]
        self.i = -1

    def next(self):
        self.i = (self.i + 1) % len(self.slots)
        return self.slots[self.i]

    def free(self):
        for t, b, d in self.slots:
            if d is not None:
                self.S.dsem_free(d)


def build_nc(NPo, NSo, stop=9):
    NQ = NPo + NSo
    SP, SS = 8 * NPo, 8 * NSo
    STOT = SP + SS
    NTK = STOT // 128
    NTQ = NQ // 128
    NT = NTK + NTQ
    NG = NQ // 512
    KC = 1024
    nc = bass.Bass("TRN2", target_bir_lowering=False)

    def din(name, shape):
        return nc.dram_tensor(name, shape, F32, kind="ExternalInput").ap()

    def dscr(name, shape, dt=BF16):
        return nc.dram_tensor(name, shape, dt, kind="Internal").ap()

    xall = din("xall", [STOT, D])
    xq = din("xq", [NQ, D])
    posk = din("posk", [128, NTK])
    posq = din("posq", [128, NTQ])
    ident_d = din("ident", [128, 128])
    g_attn_d = din("attn_norm_g", [1, D])
    w_in = din("w_in", [D, INC])
    lam_d = [din(n, [1, 64]) for n in ("da_lambda_q1", "da_lambda_k1", "da_lambda_q2", "da_lambda_k2")]
    g_sub_d = din("da_subln_g", [1, 128])
    g_q_d = din("mla_q_norm_g", [1, 512])
    w_qb = din("mla_w_q_b", [512, 1536])
    g_kv_d = din("mla_kv_norm_g", [1, 512])
    w_kvb = din("mla_w_kv_b", [512, 2048])
    w_bd = din("w_branch_da", [1024, D])
    w_bm = din("w_branch_mla", [1024, D])
    w_out = din("w_out", [D, D])
    g_ffn_d = din("ffn_norm_g", [1, D])
    w_gate = din("w_gate", [D, FF])
    w_up = din("w_up", [D, FF])
    w_down = din("w_down", [FF, D])
    g_fin_d = din("final_norm_g", [1, D])
    y = nc.dram_tensor("y", [NQ, D], F32, kind="ExternalOutput").ap()

    wb_in = dscr("wb_in", [D, INC])
    wb_qb = dscr("wb_qb", [512, 1536])
    wb_kvb = dscr("wb_kvb", [512, 2048])
    wb_bd = dscr("wb_bd", [1024, D])
    wb_bm = dscr("wb_bm", [1024, D])
    wb_out = dscr("wb_out", [D, D])
    wb_gate = dscr("wb_gate", [D, FF])
    wb_up = dscr("wb_up", [D, FF])
    wb_down = dscr("wb_down", [FF, D])
    rt = dscr("rt", [128, NT, 160], F32)
    KTda = dscr("KTda", [NH, 128, STOT])
    KTn = dscr("KTn", [NH, 128, STOT])
    KTr = dscr("KTr", [128, STOT])
    Vda = dscr("Vda", [NH, 128, NTK, 132])
    Vm = dscr("Vm", [NH, 128, NTK, 132])
    QTda = dscr("QTda", [NH, 128, NQ])
    QTn = dscr("QTn", [NH, 128, NQ])
    QTr = dscr("QTr", [4, 128, NQ])

    S = Sched(nc)
    B_rt, B_kv, B_q = Buf("rt"), Buf("kv"), Buf("q")

    def DMA(eng, out, in_, reads=(), writes=(), dsem=None, cowrite=False):
        return S.op(eng, lambda e: e.dma_start(out=out, in_=in_), reads, writes, dsem=dsem, cowrite=cowrite)

    def DMAS(eng, out, in_, n, reads=(), writes=(), dsem=None):
        sz = out.shape[1]
        step = (sz + n - 1) // n
        for a in range(0, sz, step):
            b_ = min(sz, a + step)
            DMA(eng, out[:, a:b_], in_[:, a:b_], reads, writes, dsem=dsem, cowrite=True)

    def MM(out, lhsT, rhs, start, stop, reads, writes, cowrite=True):
        return S.op("pe", lambda e: e.matmul(out, lhsT=lhsT, rhs=rhs, start=start, stop=stop), reads, writes, cowrite=cowrite)

    def TR(out, in_, reads, writes, cowrite=True):
        return S.op("pe", lambda e: e.transpose(out, in_, ident_b[:]), list(reads) + [B_id], writes, cowrite=cowrite)

    def ACT(out, in_, func, reads, writes, scale=1.0, bias=0.0, accum_out=None, cowrite=False):
        return S.op("act", lambda e: e.activation(out=out, in_=in_, func=func, bias=bias, scale=scale, accum_out=accum_out),
                    reads, writes, cowrite=cowrite)

    def TT(eng, out, in0, in1, op, reads, writes, cowrite=False):
        return S.op(eng, lambda e: e.tensor_tensor(out=out, in0=in0, in1=in1, op=op), reads, writes, cowrite=cowrite)

    def TS(eng, out, in0, s1, op0, reads, writes, s2=None, op1=ALU.bypass, cowrite=False):
        return S.op(eng, lambda e: e.tensor_scalar(out=out, in0=in0, scalar1=s1, scalar2=s2, op0=op0, op1=op1), reads, writes, cowrite=cowrite)

    def STT(out, in0, scalar, in1, op0, op1, reads, writes, cowrite=False):
        return S.op("dve", lambda e: e.scalar_tensor_tensor(out=out, in0=in0, scalar=scalar, in1=in1, op0=op0, op1=op1),
                    reads, writes, cowrite=cowrite)

    def CP(eng, out, in_, reads, writes, cowrite=False):
        if eng == "act":
            return S.op("act", lambda e: e.copy(out=out, in_=in_), reads, writes, cowrite=cowrite)
        return S.op(eng, lambda e: e.tensor_copy(out=out, in_=in_), reads, writes, cowrite=cowrite)

    def rstd_from_ss(ss, n, dim, reads_b, tmp, out, out_b):
        ACT(tmp, ss, AF.Ln, [reads_b], [out_b], scale=1.0 / dim, bias=eps_t[:, 0:1], cowrite=True)
        ACT(out, tmp, AF.Exp, [out_b], [out_b], scale=-0.5, cowrite=True)

    with contextlib.ExitStack() as top:
        def tsb(name, shape, dt):
            return top.enter_context(nc.sbuf_tensor(name, shape, dt))
        ident_b = tsb("ident_b", [128, 128], BF16)
        idf = tsb("ident_f", [128, 128], F32)
        sc = tsb("sc", [128, 4], F32)
        eps_t = tsb("eps_t", [128, 1], F32)
        B_id, B_sc, B_idf = Buf("id"), Buf("sc"), Buf("idf")

        wbufs = {}

        def cast_weight(key, dst, src, rows, cols):
            b = Buf(key)
            ds = S.dsem(persist=True)
            wbufs[key] = b
            bw = cols
            while bw > 2048:
                for dv in (2, 3, 4, 5, 6, 7, 8, 11):
                    if cols % dv == 0 and cols // dv <= 2048:
                        bw = cols // dv
                        break
                break
            for r0 in range(0, rows, 128):
                DMA("pool", dst[r0:r0 + 128, :].rearrange("r (a b) -> r a b", b=bw),
                    src[r0:r0 + 128, :].rearrange("r (a b) -> r a b", b=bw), writes=[b], dsem=ds, cowrite=True)

        with contextlib.ExitStack() as es:
            def sb(name, shape, dt):
                return es.enter_context(nc.sbuf_tensor(name, shape, dt))
            cast_weight("in", wb_in, w_in, D, INC)
            cast_weight("kvb", wb_kvb, w_kvb, 512, 2048)
            cast_weight("qb", wb_qb, w_qb, 512, 1536)
            lq = sb("lq", [128, 4, 64], F32)
            prod = sb("prod", [128, 2, 64], F32)
            s12 = sb("s12", [128, 2], F32)
            e12 = sb("e12", [128, 2], F32)
            d0, d0b, d0c = S.dsem(), S.dsem(), S.dsem()
            b_idf, b_lq, b_pr, b_s12 = B_idf, Buf(), Buf(), Buf()
            DMA("sp", idf[:], ident_d[:, :], writes=[b_idf], dsem=d0)
            CP("dve", ident_b[:], idf[:], [b_idf], [B_id])
            S.op("dve", lambda e: e.memset(eps_t[:], EPS), (), [B_sc], cowrite=True)
            for i in range(4):
                DMA("sp", lq[:, i, :], lam_d[i].partition_broadcast(128), writes=[b_lq], dsem=d0b, cowrite=True)
            TT("dve", prod[:, 0, :], lq[:, 0, :], lq[:, 1, :], ALU.mult, [b_lq], [b_pr], cowrite=True)
            TT("dve", prod[:, 1, :], lq[:, 2, :], lq[:, 3, :], ALU.mult, [b_lq], [b_pr], cowrite=True)
            S.op("dve", lambda e: e.tensor_reduce(out=s12[:], in_=prod[:], axis=AX.X, op=ALU.add), [b_pr], [b_s12])
            ACT(e12[:], s12[:], AF.Exp, [b_s12], [b_s12], cowrite=True)
            TT("dve", sc[:, 0:1], e12[:, 0:1], e12[:, 1:2], ALU.subtract, [b_s12], [B_sc], cowrite=True)
            TS("dve", sc[:, 1:2], sc[:, 0:1], LAMBDA_INIT, ALU.add, [B_sc], [B_sc], s2=-1.0, op1=ALU.mult, cowrite=True)
            pos = sb("pos", [128, NT], F32)
            b_pos = Buf()
            DMA("sp", pos[:, 0:NTK], posk[:, :], writes=[b_pos], dsem=d0c, cowrite=True)
            DMA("sp", pos[:, NTK:NT], posq[:, :], writes=[b_pos], dsem=d0c, cowrite=True)
            invf = np.concatenate([
                (np.float32(THETA) ** (-np.arange(0, 16, 2, dtype=np.float32) / np.float32(16))).astype(np.float32),
                (np.float32(THETA) ** (-np.arange(0, 64, 2, dtype=np.float32) / np.float32(64))).astype(np.float32)])
            CH = 24
            PI = math.pi
            C1 = 6.28125
            C2 = 2.0 * math.pi - C1
            PIC = 3.1415925
            tA = sb("tA", [128, CH, 40], F32)
            tB = sb("tB", [128, CH, 40], F32)
            tC = sb("tC", [128, CH, 40], F32)
            tD = sb("tD", [128, CH, 40], F32)
            tI = sb("tI", [128, CH, 40], I32)
            rtt = sb("rtt", [128, CH, 160], F32)
            bA, bB, bC, bD, bI, bR = Buf(), Buf(), Buf(), Buf(), Buf(), Buf()
            drt = S.dsem()
            for c0 in range(0, NT, CH):
                n = min(CH, NT - c0)
                A, Bt, C, Dt, It = tA[:, 0:n, :], tB[:, 0:n, :], tC[:, 0:n, :], tD[:, 0:n, :], tI[:, 0:n, :]
                for j in range(40):
                    TS("dve", tA[:, 0:n, j], pos[:, c0:c0 + n], float(invf[j]), ALU.mult, [b_pos], [bA], cowrite=(j > 0))
                TS("dve", It, A, 1.0 / (2.0 * PI), ALU.mult, [bA], [bI])
                CP("dve", C, It, [bI], [bC])
                STT(Dt, C, -C1, A, ALU.mult, ALU.add, [bC, bA], [bD])
                STT(A, C, -C2, Dt, ALU.mult, ALU.add, [bC, bD], [bA])

                def wrap(src, bs, tmp, bt, dst, bd_):
                    TS("dve", tmp, src, PI, ALU.is_gt, [bs], [bt], s2=-2.0 * PI, op1=ALU.mult)
                    TT("dve", dst, src, tmp, ALU.add, [bs, bt], [bd_])
                    TS("dve", tmp, dst, -PI, ALU.is_lt, [bd_], [bt], s2=2.0 * PI, op1=ALU.mult)
                    TT("dve", src, dst, tmp, ALU.add, [bd_, bt], [bs])
                    TS("dve", src, src, -PIC, ALU.max, [bs], [bs], s2=PIC, op1=ALU.min)
                wrap(A, bA, Bt, bB, Dt, bD)
                ACT(C, A, AF.Sin, [bA], [bC])
                TS("dve", Dt, A, PI / 2.0, ALU.add, [bA], [bD])
                wrap(Dt, bD, Bt, bB, A, bA)
                ACT(A, Dt, AF.Sin, [bD], [bA])
                R = rtt[:, 0:n, :]
                CP("dve", rtt[:, 0:n, 0:8], tA[:, 0:n, 0:8], [bA], [bR])
                CP("dve", rtt[:, 0:n, 8:16], tA[:, 0:n, 0:8], [bA], [bR], cowrite=True)
                TS("dve", rtt[:, 0:n, 16:24], tC[:, 0:n, 0:8], -1.0, ALU.mult, [bC], [bR], cowrite=True)
                CP("dve", rtt[:, 0:n, 24:32], tC[:, 0:n, 0:8], [bC], [bR], cowrite=True)
                CP("dve", rtt[:, 0:n, 32:64], tA[:, 0:n, 8:40], [bA], [bR], cowrite=True)
                CP("dve", rtt[:, 0:n, 64:96], tA[:, 0:n, 8:40], [bA], [bR], cowrite=True)
                TS("dve", rtt[:, 0:n, 96:128], tC[:, 0:n, 8:40], -1.0, ALU.mult, [bC], [bR], cowrite=True)
                CP("dve", rtt[:, 0:n, 128:160], tC[:, 0:n, 8:40], [bC], [bR], cowrite=True)
                DMA("sp", rt[:, c0:c0 + n, :], R, [bR], [B_rt], dsem=drt, cowrite=True)
            S.run_block(final=(stop == 1))
            for d_ in (d0, d0b, d0c, drt):
                S.dsem_free(d_)
        if stop == 1:
            return nc

        def rope(src3, C, W, rot, tab, tab_b, src_b, t1, t2, tb, dst3, dst_b):
            o = 0 if rot == 16 else 32
            hh = rot // 2
            cosf = tab[:, o:o + rot]
            s_a = tab[:, o + rot:o + rot + hh]
            s_b = tab[:, o + rot + hh:o + 2 * rot]
            for c in range(C):
                TT("dve", t1[:, c, 0:rot], src3[:, c, 0:rot], cosf, ALU.mult, [src_b, tab_b], [tb], cowrite=True)
                TT("dve", t2[:, c, 0:hh], src3[:, c, hh:rot], s_a, ALU.mult, [src_b, tab_b], [tb], cowrite=True)
                TT("dve", t2[:, c, hh:rot], src3[:, c, 0:hh], s_b, ALU.mult, [src_b, tab_b], [tb], cowrite=True)
            TT("dve", dst3[:, :, 0:rot], t1[:, 0:C, 0:rot], t2[:, 0:C, 0:rot], ALU.add, [tb], [dst_b], cowrite=True)
            if W > rot:
                CP("dve", dst3[:, :, rot:W], src3[:, :, rot:W], [src_b], [dst_b], cowrite=True)

        def proj_phase(isK):
            nt = NTK if isK else NTQ
            xin = xall if isK else xq
            t_off = 0 if isK else NTK
            NA = 2624 if isK else 1536
            NB = 2048 if isK else 1536
            with contextlib.ExitStack() as es:
                def sb(name, shape, dt):
                    return es.enter_context(nc.sbuf_tensor(name, shape, dt))

                def ps(name, shape, dt):
                    return es.enter_context(nc.psum_tensor(name, shape, dt))
                pf = "k" if isK else "q"
                w = sb(pf + "w", [128, 16, NA], BF16)
                wb2 = sb(pf + "wb2", [128, 4, NB], BF16)
                gat = sb(pf + "gat", [128, D], F32)
                gl = sb(pf + "gl", [128, 512], F32)
                b_w, b_wb2, b_g = Buf(), Buf(), Buf()
                dw, dwb, dwg = S.dsem(), S.dsem(), S.dsem()
                win = wb_in.rearrange("(kc p) c -> p kc c", p=128)
                if isK:
                    DMAS("sp", w[:, :, 0:2048], win[:, :, 1024:3072], 16, [wbufs["in"]], [b_w], dsem=dw)
                    DMAS("sp", w[:, :, 2048:2624], win[:, :, 3584:4160], 4, [wbufs["in"]], [b_w], dsem=dw)
                    DMAS("sp", wb2[:], wb_kvb.rearrange("(kc p) c -> p kc c", p=128), 4, [wbufs["kvb"]], [b_wb2], dsem=dwb)
                    DMA("sp", gl[:], g_kv_d.partition_broadcast(128), (), [b_g], dsem=dwg, cowrite=True)
                else:
                    DMAS("sp", w[:, :, 0:1024], win[:, :, 0:1024], 8, [wbufs["in"]], [b_w], dsem=dw)
                    DMAS("sp", w[:, :, 1024:1536], win[:, :, 3072:3584], 4, [wbufs["in"]], [b_w], dsem=dw)
                    DMAS("sp", wb2[:], wb_qb.rearrange("(kc p) c -> p kc c", p=128), 4, [wbufs["qb"]], [b_wb2], dsem=dwb)
                    DMA("sp", gl[:], g_q_d.partition_broadcast(128), (), [b_g], dsem=dwg, cowrite=True)
                DMA("sp", gat[:], g_attn_d.partition_broadcast(128), (), [b_g], dsem=dwg, cowrite=True)
                if isK:
                    cast_weight("bd", wb_bd, w_bd, 1024, D)
                    cast_weight("bm", wb_bm, w_bm, 1024, D)
                    cast_weight("out", wb_out, w_out, D, D)
                    cast_weight("gate", wb_gate, w_gate, D, FF)
                    cast_weight("up", wb_up, w_up, D, FF)
                    cast_weight("down", wb_down, w_down, FF, D)

                xR = Ring(S, [sb(f"{pf}x{i}", [128, D], F32) for i in range(2)])
                rtR = Ring(S, [sb(f"{pf}rt{i}", [128, 4, 160], F32) for i in range(2)])
                junk = sb(pf + "junk", [128, D], BF16)
                b_junk = Buf()
                hR = Ring(S, [sb(f"{pf}h{i}", [128, D], BF16) for i in range(2)], dma=False)
                hTR = Ring(S, [sb(f"{pf}hT{i}", [128, 16, 128], BF16) for i in range(2)], dma=False)
                stR = Ring(S, [sb(f"{pf}st{i}", [128, 8], F32) for i in range(3)], dma=False)
                t1 = sb(pf + "t1", [128, 8, 64], F32)
                t2 = sb(pf + "t2", [128, 8, 64], F32)
                b_t = Buf()
                tmA = sb(pf + "tmA", [128, 1024], BF16)
                lat = sb(pf + "lat", [128, 512], BF16)
                latT = sb(pf + "latT", [128, 4, 128], BF16)
                tmN = sb(pf + "tmN", [128, 8, 128], BF16)
                tmR = sb(pf + "tmR", [128, 8, 64], BF16)
                b_tmA, b_lat, b_latT, b_tmN, b_tmR = Buf(), Buf(), Buf(), Buf(), Buf()
                stA = sb(pf + "stA", [128, 8, 512], BF16)
                stN = sb(pf + "stN", [128, 8, 512], BF16)
                stRr = sb(pf + "stR", [128, 4, 512], BF16)
                b_stA, b_stN, b_stR = Buf(), Buf(), Buf()
                d_stA, d_stN, d_stR = S.dsem(), S.dsem(), S.dsem()
                if isK:
                    stV = sb("kstV", [128, 8, 4, 132], BF16)
                    stVm = sb("kstVm", [128, 8, 4, 132], BF16)
                    b_stV, b_stVm = Buf(), Buf()
                    S.op("dve", lambda e: e.memset(stV[:, :, :, 128:132], 1.0), (), [b_stV], cowrite=True)
                    S.op("dve", lambda e: e.memset(stVm[:, :, :, 128:132], 1.0), (), [b_stVm], cowrite=True)
                    d_stV, d_stVm = S.dsem(), S.dsem()
                tp = ps(pf + "tp", [128, 16, 128], BF16)
                pj = ps(pf + "pj", [128, 2, 512], F32)
                tk = ps(pf + "tk", [128, 2, 8, 128], BF16)
                tcp = ps(pf + "tc", [128, 4, 128], BF16)
                b_tp, b_tc = Buf(), Buf()
                pjR = Ring(S, [pj[:, i, :] for i in range(2)], dma=False)
                tkR = Ring(S, [tk[:, i, :, :] for i in range(2)], dma=False)

                ngrp = nt // 4
                for g in range(ngrp):
                    rtt_, b_rtt, d_rtt = rtR.next()
                    DMA("sp", rtt_[:], rt[:, t_off + g * 4:t_off + g * 4 + 4, :], [B_rt], [b_rtt], dsem=d_rtt)
                    for j in range(4):
                        ti = g * 4 + j
                        x_, b_x, d_x = xR.next()
                        DMA("sp", x_[:], xin[ti * 128:(ti + 1) * 128, :], (), [b_x], dsem=d_x)
                        st_, b_st, _ = stR.next()
                        ACT(junk[:], x_[:], AF.Square, [b_x], [b_junk, b_st], accum_out=st_[:, 0:1])
                        rstd_from_ss(st_[:, 0:1], 1, D, b_st, st_[:, 1:2], st_[:, 2:3], b_st)
                        h_, b_h, _ = hR.next()
                        STT(h_[:], x_[:], st_[:, 2:3], gat[:], ALU.mult, ALU.mult, [b_x, b_st, b_g], [b_h])
                        for kc in range(16):
                            TR(tp[:, kc, :], h_[:, kc * 128:(kc + 1) * 128], [b_h], [b_tp], cowrite=(kc > 0))
                        hT_, b_hT, _ = hTR.next()
                        CP("dve", hT_[:, 0:8, :], tp[:, 0:8, :], [b_tp], [b_hT])
                        CP("act", hT_[:, 8:16, :], tp[:, 8:16, :], [b_tp], [b_hT], cowrite=True)
                        tab = rtt_[:, j, :]
                        if DBG == 1:
                            continue

                        def proj_block(c0, ncol):
                            p_, b_p, _ = pjR.next()
                            for kc in range(16):
                                MM(p_[:, 0:ncol], hT_[:, kc, :], w[:, kc, c0:c0 + ncol], kc == 0, kc == 15,
                                   [b_hT, b_w], [b_p], cowrite=(kc > 0))
                            return p_, b_p

                        for blk in range(2):
                            p_, b_p = proj_block(blk * 512, 512)
                            if DBG == 5:
                                continue
                            rope(p_.rearrange("p (c w) -> p c w", w=64), 8, 64, 16, tab, b_rtt, b_p, t1, t2, b_t,
                                 tmA[:, blk * 512:(blk + 1) * 512].rearrange("p (c w) -> p c w", w=64), b_tmA)
                        if DBG == 5 or DBG == 6:
                            continue
                        tk_, b_tk, _ = tkR.next()
                        if DBG != 8:
                            for hh in range(8):
                                TR(tk_[:, hh, :], tmA[:, hh * 128:(hh + 1) * 128], [b_tmA], [b_tk], cowrite=(hh > 0))
                        if DBG != 7:
                            CP("dve", stA[:, :, j * 128:(j + 1) * 128], tk_, [b_tk], [b_stA], cowrite=True)
                        if DBG in (2, 7, 8):
                            continue
                        if isK:
                            for blk in range(2):
                                p_, b_p = proj_block(1024 + blk * 512, 512)
                                CP("act", stV[:, blk * 4:(blk + 1) * 4, j, 0:128], p_.rearrange("p (h d) -> p h d", d=128),
                                   [b_p], [b_stV], cowrite=True)
                        if DBG == 3:
                            continue
                        lat0 = 2048 if isK else 1024
                        p_, b_p = proj_block(lat0, 512)
                        st2, b_st2, _ = stR.next()
                        ACT(junk[:, 0:512], p_, AF.Square, [b_p], [b_junk, b_st2], accum_out=st2[:, 0:1])
                        rstd_from_ss(st2[:, 0:1], 1, 512, b_st2, st2[:, 1:2], st2[:, 2:3], b_st2)
                        STT(lat[:], p_, st2[:, 2:3], gl[:], ALU.mult, ALU.mult, [b_p, b_st2, b_g], [b_lat])
                        if DBG == 9:
                            continue
                        for kc in range(4):
                            TR(tcp[:, kc, :], lat[:, kc * 128:(kc + 1) * 128], [b_lat], [b_tc], cowrite=(kc > 0))
                        CP("act", latT[:], tcp[:], [b_tc], [b_latT])
                        if DBG == 10:
                            continue
                        bw = 512 if isK else 384
                        for cb in range(4):
                            p_, b_p, _ = pjR.next()
                            for kc in range(4):
                                MM(p_[:, 0:bw], latT[:, kc, :], wb2[:, kc, cb * bw:(cb + 1) * bw], kc == 0, kc == 3,
                                   [b_latT, b_wb2], [b_p], cowrite=(kc > 0))
                            if DBG == 11:
                                continue
                            if isK:
                                pv = p_.rearrange("p (h t d) -> p h t d", t=2, d=128)
                                ceng = "dve" if cb % 2 == 0 else "act"
                                for h2 in range(2):
                                    CP(ceng, tmN[:, 2 * cb + h2, :], p_[:, h2 * 256:h2 * 256 + 128], [b_p], [b_tmN], cowrite=True)
                                for h2 in range(2):
                                    CP(ceng, stVm[:, 2 * cb + h2, j, 0:128], p_[:, h2 * 256 + 128:h2 * 256 + 256], [b_p], [b_stVm], cowrite=True)
                            else:
                                pv = p_[:, 0:384].rearrange("p (h d) -> p h d", d=192)
                                for h2 in range(2):
                                    CP("dve", tmN[:, 2 * cb + h2, :], p_[:, h2 * 192:h2 * 192 + 128], [b_p], [b_tmN], cowrite=True)
                                rope(pv[:, :, 128:192], 2, 64, 64, tab, b_rtt, b_p, t1, t2, b_t,
                                     tmR[:, 2 * cb:2 * cb + 2, :], b_tmR)
                        if DBG in (11, 12):
                            continue
                        tk_, b_tk, _ = tkR.next()
                        for hh in range(8):
                            TR(tk_[:, hh, :], tmN[:, hh, :], [b_tmN], [b_tk], cowrite=(hh > 0))
                        CP("dve", stN[:, :, j * 128:(j + 1) * 128], tk_, [b_tk], [b_stN], cowrite=True)
                        if DBG == 4:
                            continue
                        if isK:
                            p_, b_p = proj_block(2560, 64)
                            p3 = p_[:, 0:64].unsqueeze(1)
                            rope(p3, 1, 64, 64, tab, b_rtt, b_p, t1, t2, b_t, tmR[:, 0:1, :], b_tmR)
                            CP("dve", tmR[:, 1:2, :], tmR[:, 0:1, :], [b_tmR], [b_tmR], cowrite=True)
                            tk_, b_tk, _ = tkR.next()
                            TR(tk_[:, 0, :], tmR[:, 0:2, :].rearrange("p a d -> p (a d)"), [b_tmR], [b_tk], cowrite=False)
                            CP("act", stRr[:, 0, j * 128:(j + 1) * 128], tk_[:, 0, :], [b_tk], [b_stR], cowrite=True)
                        else:
                            tk_, b_tk, _ = tkR.next()
                            for pr in range(4):
                                TR(tk_[:, pr, :], tmR[:, 2 * pr:2 * pr + 2, :].rearrange("p a d -> p (a d)"), [b_tmR], [b_tk],
                                   cowrite=(pr > 0))
                            CP("act", stRr[:, :, j * 128:(j + 1) * 128], tk_[:, 0:4, :], [b_tk], [b_stR], cowrite=True)
                    c0 = g * 512
                    if isK:
                        DMAS("sp", KTda[:, :, c0:c0 + 512].rearrange("h p t -> p h t"), stA[:], 2, [b_stA], [B_kv], dsem=d_stA)
                        DMAS("sp", KTn[:, :, c0:c0 + 512].rearrange("h p t -> p h t"), stN[:], 2, [b_stN], [B_kv], dsem=d_stN)
                        DMA("sp", KTr[:, c0:c0 + 512], stRr[:, 0, :], [b_stR], [B_kv], dsem=d_stR, cowrite=True)
                        DMAS("sp", Vda[:, :, g * 4:g * 4 + 4, :].rearrange("h p t d -> p h t d"), stV[:], 2, [b_stV], [B_kv], dsem=d_stV)
                        DMAS("sp", Vm[:, :, g * 4:g * 4 + 4, :].rearrange("h p t d -> p h t d"), stVm[:], 2, [b_stVm], [B_kv], dsem=d_stVm)
                    else:
                        DMAS("sp", QTda[:, :, c0:c0 + 512].rearrange("h p t -> p h t"), stA[:], 2, [b_stA], [B_q], dsem=d_stA)
                        DMAS("sp", QTn[:, :, c0:c0 + 512].rearrange("h p t -> p h t"), stN[:], 2, [b_stN], [B_q], dsem=d_stN)
                        DMA("sp", QTr[:, :, c0:c0 + 512].rearrange("h p t -> p h t"), stRr[:], [b_stR], [B_q], dsem=d_stR, cowrite=True)
                S.run_block(final=(stop == (2 if isK else 3)))
                for r in (xR, rtR):
                    r.free()
                for d_ in [dw, dwb, dwg, d_stA, d_stN, d_stR] + ([d_stV, d_stVm] if isK else []):
                    S.dsem_free(d_)

        proj_phase(True)
        if stop == 2:
            return nc
        proj_phase(False)
        if stop == 3:
            return nc

        with contextlib.ExitStack() as es:
            def sb(name, shape, dt):
                return es.enter_context(nc.sbuf_tensor(name, shape, dt))

            def ps(name, shape, dt):
                return es.enter_context(nc.psum_tensor(name, shape, dt))
            NKT = KC // 128
            ktR = Ring(S, [sb(f"a_kt{i}", [128, KC], BF16) for i in range(3)])
            krR = Ring(S, [sb(f"a_kr{i}", [128, KC], BF16) for i in range(2)])
            v_tiles = [sb(f"a_v{i}", [128, NKT, 132], BF16) for i in range(3)]
            vR = Ring(S, v_tiles)
            qtR = Ring(S, [sb(f"a_qt{i}", [128, 512], BF16) for i in range(2)])
            qrR = Ring(S, [sb(f"a_qr{i}", [128, 512], BF16) for i in range(2)])
            pTR = Ring(S, [sb(f"a_pT{i}", [128, 512], BF16) for i in range(6)], dma=False)
            fa = [sb(f"fa{i}", [128, 512], F32) for i in range(2)]
            fl = [sb(f"fl{i}", [128, 512], F32) for i in range(2)]
            fo = sb("fo", [128, 512], F32)
            frs = sb("frs", [128, 512], F32)
            fq = sb("fq", [128, 512], BF16)
            b_fa, b_fl = [Buf(), Buf()], [Buf(), Buf()]
            b_fo, b_frs, b_fq = Buf(), Buf(), Buf()
            gsub = sb("gsub", [128, 1], F32)
            ones_b = sb("ones_b", [128, 128], BF16)
            b_gsub, b_ones = Buf(), Buf()
            S.op("dve", lambda e: e.memset(ones_b[:], 1.0), (), [b_ones])
            oT = sb("oT", [128, 16, 512], BF16)
            b_oT = Buf()
            x1 = sb("x1", [128, 4, D], F32)
            b_x1 = [Buf() for _ in range(4)]
            d_x1 = [S.dsem() for _ in range(4)]
            d_y = [S.dsem() for _ in range(4)]
            hT = sb("hT", [128, 16, 512], BF16)
            b_hT = Buf()
            hb = sb("hb", [128, D], F32)
            b_hb = Buf()
            big = sb("big", [128, 22, 512], BF16)
            b_big = [Buf() for _ in range(22)]
            gbuf = sb("gbuf", [128, D], F32)
            b_gbuf = Buf()
            d_g = S.dsem()
            junk = hb
            b_junk = b_hb
            pst = sb("pst", [128, 16], F32)
            b_pst = Buf()
            sg1 = sb("sg1", [128, 512], F32)
            sg2 = sb("sg2", [128, 512], F32)
            b_sg1, b_sg2 = Buf(), Buf()
            wR = Ring(S, [sb(f"wbuf{i}", [128, 8192], BF16) for i in range(3)])
            dg0 = S.dsem()
            gsb = sb("gsb", [128, 128], F32)
            b_gsb = Buf()
            DMA("sp", gsb[:], g_sub_d.partition_broadcast(128), (), [b_gsb], dsem=dg0)
            TT("dve", gsb[:], gsb[:], idf[:], ALU.mult, [b_gsb, B_idf], [b_gsb], cowrite=True)
            S.op("dve", lambda e: e.tensor_reduce(out=gsub[:], in_=gsb[:], axis=AX.X, op=ALU.add), [b_gsb], [b_gsub])
            TS("dve", gsub[:], gsub[:], 1.0 - LAMBDA_INIT, ALU.mult, [b_gsub], [b_gsub], cowrite=True)

            stp = ps("stp", [128, 3, 512], F32)
            acc = ps("acc", [128, 4, 512], F32)
            aux = ps("aux", [128, 512], F32)
            tpo = aux[:].rearrange("p (a b) -> p a b", b=128)
            stpR = Ring(S, [stp[:, i, :] for i in range(3)], dma=False)
            b_acc, b_tpo = Buf(), Buf()
            accB = [Buf() for _ in range(4)]

            SC_DA = 64 ** -0.5
            SC_MLA = 192 ** -0.5

            def load_w(view_fn):
                t_, b_, d_ = wR.next()
                view_fn(t_, b_, d_)
                return t_, b_

            for g in range(NG):
                prompt = g < NPo // 512
                k0 = 0 if prompt else SP
                klen = SP if prompt else SS
                q0 = g * 512
                nch = klen // KC
                steps = []
                for u in range(16):
                    for c in range(nch):
                        for t in range(NKT):
                            for m in range(2 if u < 8 else 1):
                                steps.append((u, m, c, t))
                cur = {"u": None, "chunk": None}
                rec = {}

                def emit_qk(i):
                    u, m, c, t = steps[i]
                    da = u < 8
                    h = u % 8
                    if cur["u"] != u:
                        cur["u"] = u
                        qt_, b_qt, d_qt = qtR.next()
                        if da:
                            DMA("sp", qt_[:], QTda[h, :, q0:q0 + 512], [B_q], [b_qt], dsem=d_qt)
                            cur["q"] = (qt_, b_qt, None, None)
                        else:
                            DMA("sp", qt_[:], QTn[h, :, q0:q0 + 512], [B_q], [b_qt], dsem=d_qt)
                            qr_, b_qr, d_qr = qrR.next()
                            DMA("sp", qr_[:], QTr[h // 2, :, q0:q0 + 512], [B_q], [b_qr], dsem=d_qr)
                            oh = slice(64, 128) if h % 2 == 0 else slice(0, 64)
                            S.op("pool", lambda e, qr_=qr_, oh=oh: e.memset(qr_[oh, :], 0.0), [b_qr], [b_qr], cowrite=True)
                            cur["q"] = (qt_, b_qt, qr_, b_qr)
                    qt_, b_qt, qr_, b_qr = cur["q"]
                    if cur["chunk"] != (u, c):
                        cur["chunk"] = (u, c)
                        ks = k0 + c * KC
                        kt_, b_kt, d_kt = ktR.next()
                        v_, b_v, d_v = vR.next()
                        if da:
                            DMA("sp", kt_[:], KTda[h, :, ks:ks + KC], [B_kv], [b_kt], dsem=d_kt)
                            DMA("sp", v_[:], Vda[h, :, ks // 128:ks // 128 + NKT, :], [B_kv], [b_v], dsem=d_v)
                            cur["kv"] = (kt_, b_kt, None, None, v_, b_v)
                        else:
                            DMA("sp", kt_[:], KTn[h, :, ks:ks + KC], [B_kv], [b_kt], dsem=d_kt)
                            kr_, b_kr, d_kr = krR.next()
                            DMA("sp", kr_[:], KTr[:, ks:ks + KC], [B_kv], [b_kr], dsem=d_kr)
                            DMA("sp", v_[:], Vm[h, :, ks // 128:ks // 128 + NKT, :], [B_kv], [b_v], dsem=d_v)
                            cur["kv"] = (kt_, b_kt, kr_, b_kr, v_, b_v)
                    kt_, b_kt, kr_, b_kr, v_, b_v = cur["kv"]
                    s_, b_s, _ = stpR.next()
                    ksl = slice(t * 128, (t + 1) * 128)
                    if da:
                        pr = slice(m * 64, (m + 1) * 64)
                        MM(s_, kt_[pr, ksl], qt_[pr, :], True, True, [b_kt, b_qt], [b_s], cowrite=False)
                    else:
                        MM(s_, kt_[:, ksl], qt_[:], True, False, [b_kt, b_qt], [b_s], cowrite=False)
                        MM(s_, kr_[:, ksl], qr_[:], False, True, [b_kr, b_qr], [b_s], cowrite=True)
                    rec[i] = (s_, b_s, v_, b_v)

                def emit_exp_pv(i):
                    u, m, c, t = steps[i]
                    da = u < 8
                    s_, b_s, v_, b_v = rec.pop(i)
                    first = (c == 0 and t == 0)
                    last = (c == nch - 1 and t == NKT - 1)
                    p_, b_p, _ = pTR.next()
                    ACT(p_[:], s_, AF.Exp, [b_s], [b_p], scale=(SC_DA if da else SC_MLA))
                    MM(acc[:, m, :], v_[:, t, 0:128], p_[:], first, last, [b_p, b_v], [accB[m]], cowrite=not first)
                    MM(acc[:, 2 + m, :], ones_b[:], p_[:], first, last, [b_p, b_ones], [accB[2 + m]], cowrite=not first)
                    return last and (m == (1 if da else 0))

                def finalize(u):
                    da = u < 8
                    nm = 2 if da else 1
                    for m in range(nm):
                        CP("dve", fl[m][:], acc[:, 2 + m, :], [accB[2 + m]], [b_fl[m]])
                        CP("dve", fa[m][:], acc[:, m, :], [accB[m]], [b_fa[m]])
                    for m in range(nm):
                        S.op("dve", lambda e, m=m: e.reciprocal(out=fl[m][:], in_=fl[m][:]), [b_fl[m]], [b_fl[m]], cowrite=True)
                    if not da:
                        TT("dve", oT[:, u, :], fa[0][:], fl[0][:], ALU.mult, [b_fa[0], b_fl[0]], [b_oT], cowrite=True)
                        return
                    for m in range(2):
                        TT("dve", fa[m][:], fa[m][:], fl[m][:], ALU.mult, [b_fa[m], b_fl[m]], [b_fa[m]], cowrite=True)
                    STT(fo[:], fa[1][:], sc[:, 1:2], fa[0][:], ALU.mult, ALU.add, [b_fa[0], b_fa[1], B_sc], [b_fo])
                    TT("dve", fq[:], fo[:], fo[:], ALU.mult, [b_fo], [b_fq])
                    MM(aux[:], ones_b[:], fq[:], True, True, [b_fq, b_ones], [b_tpo], cowrite=False)
                    ACT(frs[:], aux[:], AF.Ln, [b_tpo], [b_frs], scale=1.0 / 128.0, bias=eps_t[:, 0:1])
                    ACT(frs[:], frs[:], AF.Exp, [b_frs], [b_frs], scale=-0.5, cowrite=True)
                    STT(oT[:, u, :], fo[:], gsub[:, 0:1], frs[:], ALU.mult, ALU.mult, [b_fo, b_frs, b_gsub], [b_oT], cowrite=True)

                LOOK = 2
                nst = len(steps)
                for i in range(min(LOOK, nst)):
                    emit_qk(i)
                for i in range(nst):
                    if i + LOOK < nst:
                        emit_qk(i + LOOK)
                    if emit_exp_pv(i):
                        finalize(steps[i][0])

                if DBG == 20:
                    continue
                DMA("sp", gbuf[:], g_attn_d.partition_broadcast(128), (), [b_gbuf], dsem=d_g)
                for j in range(4):
                    r0 = q0 + j * 128
                    DMA("sp", x1[:, j, :], xq[r0:r0 + 128, :], (), [b_x1[j]], dsem=d_x1[j])

                def norm_T(j, dstT, b_dstT, first):
                    ACT(junk[:], x1[:, j, :], AF.Square, [b_x1[j]], [b_junk, b_pst], accum_out=pst[:, 0:1])
                    rstd_from_ss(pst[:, 0:1], 1, D, b_pst, pst[:, 1:2], pst[:, 2:3], b_pst)
                    STT(hb[:], x1[:, j, :], pst[:, 2:3], gbuf[:], ALU.mult, ALU.mult, [b_x1[j], b_pst, b_gbuf], [b_hb])
                    for q4 in range(4):
                        for kk in range(4):
                            kc = q4 * 4 + kk
                            S.op("pe", lambda e, kk=kk, kc=kc: e.transpose(tpo[:, kk, :], hb[:, kc * 128:(kc + 1) * 128], idf[:]),
                                 [b_hb, B_idf], [b_tpo], cowrite=(kk > 0))
                        CP("dve" if q4 % 2 == 0 else "act", dstT[:, q4 * 4:q4 * 4 + 4, j * 128:(j + 1) * 128], tpo, [b_tpo], [b_dstT],
                           cowrite=not (first and q4 == 0))
                for j in range(4):
                    norm_T(j, hT, b_hT, j == 0)
                win = wb_in.rearrange("(kc p) c -> p kc c", p=128)
                wbd_v = wb_bd.rearrange("(kc p) c -> p kc c", p=128)
                wbm_v = wb_bm.rearrange("(kc p) c -> p kc c", p=128)
                for fc in range(16):
                    wt, b_wt, d_wt = wR.next()
                    wv = wt[:, 0:48 * 128].rearrange("p (k c) -> p k c", c=128)
                    DMA("sp", wv[:, 0:4, :], win[:, 0:4, 4160 + fc * 128:4160 + (fc + 1) * 128], [wbufs["in"]], [b_wt], dsem=d_wt)
                    DMAS("sp", wv[:, 4:16, :], win[:, 4:16, 4160 + fc * 128:4160 + (fc + 1) * 128], 3, [wbufs["in"]], [b_wt], dsem=d_wt)
                    DMAS("sp", wv[:, 16:32, :], win[:, :, 4160 + D + fc * 128:4160 + D + (fc + 1) * 128], 4, [wbufs["in"]], [b_wt], dsem=d_wt)
                    DMAS("sp", wv[:, 32:40, :], wbd_v[:, :, fc * 128:(fc + 1) * 128], 2, [wbufs["bd"]], [b_wt], dsem=d_wt)
                    DMAS("sp", wv[:, 40:48, :], wbm_v[:, :, fc * 128:(fc + 1) * 128], 2, [wbufs["bm"]], [b_wt], dsem=d_wt)
                    for kc in range(16):
                        MM(acc[:, 0, :], wv[:, kc, :], hT[:, kc, :], kc == 0, kc == 15, [b_wt, b_hT], [accB[0]], cowrite=(kc > 0))
                    for kc in range(16):
                        MM(acc[:, 1, :], wv[:, 16 + kc, :], hT[:, kc, :], kc == 0, kc == 15, [b_wt, b_hT], [accB[1]], cowrite=(kc > 0))
                    for kc in range(8):
                        MM(acc[:, 2, :], wv[:, 32 + kc, :], oT[:, kc, :], kc == 0, kc == 7, [b_wt, b_oT], [accB[2]], cowrite=(kc > 0))
                    for kc in range(8):
                        MM(acc[:, 3, :], wv[:, 40 + kc, :], oT[:, 8 + kc, :], kc == 0, kc == 7, [b_wt, b_oT], [accB[3]], cowrite=(kc > 0))
                    ACT(sg1[:], acc[:, 0, :], AF.Sigmoid, [accB[0]], [b_sg1])
                    ACT(sg2[:], acc[:, 1, :], AF.Sigmoid, [accB[1]], [b_sg2])
                    TT("dve", sg1[:], sg1[:], acc[:, 2, :], ALU.mult, [b_sg1, accB[2]], [b_sg1])
                    TT("dve", sg2[:], sg2[:], acc[:, 3, :], ALU.mult, [b_sg2, accB[3]], [b_sg2])
                    TT("dve", big[:, fc, :], sg1[:], sg2[:], ALU.add, [b_sg1, b_sg2], [b_big[fc]])
                wout_v = wb_out.rearrange("(kc p) c -> p kc c", p=128)
                for cb in range(4):
                    wt, b_wt, d_wt = wR.next()
                    wv = wt[:].rearrange("p (k c) -> p k c", c=512)
                    DMA("sp", wv[:, 0:4, :], wout_v[:, 0:4, cb * 512:(cb + 1) * 512], [wbufs["out"]], [b_wt], dsem=d_wt)
                    DMAS("sp", wv[:, 4:16, :], wout_v[:, 4:16, cb * 512:(cb + 1) * 512], 3, [wbufs["out"]], [b_wt], dsem=d_wt)
                    for j in range(4):
                        for kc in range(16):
                            MM(acc[:, j, :], big[:, kc, j * 128:(j + 1) * 128], wv[:, kc, :], kc == 0, kc == 15, [b_wt, b_big[kc]], [accB[j]],
                               cowrite=(kc > 0))
                        xs = x1[:, j, cb * 512:(cb + 1) * 512]
                        TT("dve", xs, xs, acc[:, j, :], ALU.add, [b_x1[j], accB[j]], [b_x1[j]], cowrite=True)
                DMA("sp", gbuf[:], g_ffn_d.partition_broadcast(128), (), [b_gbuf], dsem=d_g)
                for j in range(4):
                    norm_T(j, hT, b_hT, j == 0)
                wg_v = wb_gate.rearrange("(kc p) c -> p kc c", p=128)
                wu_v = wb_up.rearrange("(kc p) c -> p kc c", p=128)
                wd_v = wb_down.rearrange("(hc p) c -> p hc c", p=128)
                for half in range(2):
                    for hp2 in range(11):
                        hc0 = half * 22 + hp2 * 2
                        wt, b_wt, d_wt = wR.next()
                        wv = wt[:].rearrange("p (s k c) -> p s k c", s=2, c=256)
                        DMA("sp", wv[:, 0, 0:4, :], wg_v[:, 0:4, hc0 * 128:(hc0 + 2) * 128], [wbufs["gate"]], [b_wt], dsem=d_wt)
                        DMAS("sp", wv[:, 0, 4:16, :], wg_v[:, 4:16, hc0 * 128:(hc0 + 2) * 128], 3, [wbufs["gate"]], [b_wt], dsem=d_wt)
                        DMAS("sp", wv[:, 1, :, :], wu_v[:, :, hc0 * 128:(hc0 + 2) * 128], 4, [wbufs["up"]], [b_wt], dsem=d_wt)
                        for i2 in range(2):
                            ga, ua = (0, 1) if i2 == 0 else (2, 3)
                            for kc in range(16):
                                MM(acc[:, ga, :], wv[:, 0, kc, i2 * 128:(i2 + 1) * 128], hT[:, kc, :], kc == 0, kc == 15, [b_wt, b_hT], [accB[ga]],
                                   cowrite=(kc > 0))
                            for kc in range(16):
                                MM(acc[:, ua, :], wv[:, 1, kc, i2 * 128:(i2 + 1) * 128], hT[:, kc, :], kc == 0, kc == 15, [b_wt, b_hT], [accB[ua]],
                                   cowrite=(kc > 0))
                            sg, b_sg = (sg1, b_sg1) if i2 == 0 else (sg2, b_sg2)
                            ACT(sg[:], acc[:, ga, :], AF.Silu, [accB[ga]], [b_sg])
                            ci = hp2 * 2 + i2
                            TT("dve", big[:, ci, :], sg[:], acc[:, ua, :], ALU.mult, [b_sg, accB[ua]], [b_big[ci]])
                    for cb in range(4):
                        wts = []
                        for part in range(2):
                            wt, b_wt, d_wt = wR.next()
                            wv = wt[:, 0:11 * 512].rearrange("p (k c) -> p k c", c=512)
                            hc0 = half * 22 + part * 11
                            DMA("sp", wv[:, 0:4, :], wd_v[:, hc0:hc0 + 4, cb * 512:(cb + 1) * 512], [wbufs["down"]], [b_wt], dsem=d_wt)
                            DMAS("sp", wv[:, 4:11, :], wd_v[:, hc0 + 4:hc0 + 11, cb * 512:(cb + 1) * 512], 2, [wbufs["down"]], [b_wt], dsem=d_wt)
                            wts.append((wv, b_wt))
                        for j in range(4):
                            for ci in range(22):
                                wv, b_wt = wts[ci // 11]
                                MM(acc[:, j, :], big[:, ci, j * 128:(j + 1) * 128], wv[:, ci % 11, :], ci == 0, ci == 21, [b_wt, b_big[ci]], [accB[j]],
                                   cowrite=(ci > 0))
                            xs = x1[:, j, cb * 512:(cb + 1) * 512]
                            TT("dve", xs, xs, acc[:, j, :], ALU.add, [b_x1[j], accB[j]], [b_x1[j]], cowrite=True)
                DMA("sp", gbuf[:], g_fin_d.partition_broadcast(128), (), [b_gbuf], dsem=d_g)
                for j in range(4):
                    ACT(junk[:], x1[:, j, :], AF.Square, [b_x1[j]], [b_junk, b_pst], accum_out=pst[:, 0:1])
                    rstd_from_ss(pst[:, 0:1], 1, D, b_pst, pst[:, 1:2], pst[:, 2:3], b_pst)
                    STT(x1[:, j, :], x1[:, j, :], pst[:, 2:3], gbuf[:], ALU.mult, ALU.mult, [b_x1[j], b_pst, b_gbuf], [b_x1[j]], cowrite=True)
                    r0 = q0 + j * 128
                    DMA("sp", y[r0:r0 + 128, :], x1[:, j, :], [b_x1[j]], (), dsem=d_y[j])
            S.run_block(final=True)
    return nc


_NC_CACHE = {}
PARAM_NAMES = ["attn_norm_g", "w_in", "da_lambda_q1", "da_lambda_k1", "da_lambda_q2", "da_lambda_k2", "da_subln_g",
               "mla_q_norm_g", "mla_w_q_b", "mla_kv_norm_g", "mla_w_kv_b", "w_branch_da", "w_branch_mla", "w_out",
               "ffn_norm_g", "w_gate", "w_up", "w_down"]


STOP = 9
NCORES = 8
DBG = 0


def kernel(x_prompt, x_sample, final_norm_g, **params):
    xp = np.asarray(x_prompt, dtype=np.float32)[0]
    xs = np.asarray(x_sample, dtype=np.float32)[0]
    SP, SS = xp.shape[0], xs.shape[0]
    NPo, NSo = SP // 8, SS // 8
    key = (NPo, NSo)
    if key not in _NC_CACHE:
        _NC_CACHE[key] = build_nc(NPo, NSo, STOP)
    nc = _NC_CACHE[key]
    xall = np.ascontiguousarray(np.concatenate([xp, xs], axis=0))
    shared = {"xall": xall, "ident": np.eye(128, dtype=np.float32)}
    for n in PARAM_NAMES:
        a = np.asarray(params[n], dtype=np.float32)
        a = a[0]
        if a.ndim == 1:
            a = a[None, :]
        shared[n] = np.ascontiguousarray(a)
    shared["final_norm_g"] = np.ascontiguousarray(np.asarray(final_norm_g, dtype=np.float32)[None, :])
    pk = np.concatenate([np.arange(SP), np.arange(SS)]).astype(np.float32)
    shared["posk"] = np.ascontiguousarray(pk.reshape(-1, 128).T)
    in_maps = []
    for c in range(8):
        m = dict(shared)
        m["xq"] = np.ascontiguousarray(np.concatenate([xp[c * NPo:(c + 1) * NPo], xs[c * NSo:(c + 1) * NSo]], axis=0))
        pq = np.concatenate([np.arange(c * NPo, (c + 1) * NPo), np.arange(c * NSo, (c + 1) * NSo)]).astype(np.float32)
        m["posq"] = np.ascontiguousarray(pq.reshape(-1, 128).T)
        in_maps.append(m)
    res = run_bass_kernel_spmd(nc, in_maps[:NCORES], core_ids=list(range(NCORES)))
    rr = [res.results[c]["y"] if c < NCORES else np.zeros((NPo + NSo, D), np.float32) for c in range(8)]
    yp = np.concatenate([rr[c][:NPo] for c in range(8)], axis=0)[None]
    ys = np.concatenate([rr[c][NPo:] for c in range(8)], axis=0)[None]
    return (yp.astype(np.float32), ys.astype(np.float32))
```

```python
import contextlib
import math
import numpy as np
import concourse.bass as bass
import concourse.mybir as mybir
from concourse.bass_utils import run_bass_kernel_spmd

F32 = mybir.dt.float32
BF16 = mybir.dt.bfloat16
I32 = mybir.dt.int32
AF = mybir.ActivationFunctionType
ALU = mybir.AluOpType
AX = mybir.AxisListType

D = 2048
NH = 8
FF = 5632
INC = 8256
EPS = 1e-6
THETA = 500000.0
LAMBDA_INIT = 0.8 - 0.6 * math.exp(0.0)


class LSem:
    def __init__(self, S, name, step):
        self.S, self.name, self.step = S, name, step
        self.epoch = 28000 // step
        self.n = 0
        self.phys = []
        self.persist = False
        self.base = 0
        self.rank = {}

    def target_v(self, v):
        e = (v - 1) // self.epoch
        return self._phys(e), ((v - 1) % self.epoch + 1) * self.step

    def _phys(self, e):
        while len(self.phys) <= e:
            self.phys.append(self.S.nc.alloc_semaphore(name=f"{self.name}_{len(self.phys)}"))
        return self.phys[e]

    def next(self):
        self.n += 1
        return self.n

    def inc_target(self, n):
        return self._phys((n - 1) // self.epoch), self.step

    def wait_target(self, n):
        e = (n - 1) // self.epoch
        return self._phys(e), ((n - 1) % self.epoch + 1) * self.step


class Buf:
    def __init__(self, name=""):
        self.name = name
        self.writers = {}
        self.readers = {}


ENGS = ("pe", "act", "dve", "pool", "sp")


class Sched:
    def __init__(self, nc):
        self.nc = nc
        self.es = {e: LSem(self, "e_" + e, 1) for e in ("pe", "act", "dve", "pool")}
        self.prog = {e: [] for e in ENGS}
        self.seen = {e: {} for e in ENGS}
        self.dma_pool = []
        self.dma_all = []
        self.nops = 0

    def dsem(self, persist=False):
        if self.dma_pool and not persist:
            return self.dma_pool.pop()
        s = LSem(self, f"d{len(self.dma_all)}", 16)
        s.persist = persist
        self.dma_all.append(s)
        return s

    def dsem_free(self, s):
        self.dma_pool.append(s)

    def _wait(self, eng, lsem, n):
        seen = self.seen[eng]
        if seen.get(lsem, 0) >= n:
            return
        seen[lsem] = n
        self.prog[eng].append(("w", lsem, n))

    def op(self, eng, fn, reads=(), writes=(), dsem=None, cowrite=False):
        own = self.es.get(eng) if dsem is None else None
        deps = []
        for b in reads:
            for ls, n in b.writers.items():
                if ls is own and eng == "pe":
                    continue
                deps.append((ls, n))
        for b in writes:
            if not cowrite:
                for ls, n in b.writers.items():
                    if ls is own:
                        continue
                    deps.append((ls, n))
            for ls, n in b.readers.items():
                if ls is own:
                    continue
                deps.append((ls, n))
        for ls, n in deps:
            self._wait(eng, ls, n)
        ls = dsem if dsem is not None else own
        n = ls.next()
        self.prog[eng].append(("o", fn, ls, n))
        self.nops += 1
        for b in reads:
            if b.readers.get(ls, 0) < n:
                b.readers[ls] = n
        for b in writes:
            if cowrite:
                b.writers[ls] = n
            else:
                b.writers = {ls: n}
                b.readers = {}
        return (ls, n)

    def run_block(self, final=False):
        for s in self.dma_all:
            if s.n and (final or not s.persist):
                self._wait("sp", s, s.n)
        for e, s in self.es.items():
            if s.n:
                self._wait("sp", s, s.n)
        nc = self.nc
        prog = self.prog
        comp = set(self.es.values())
        lazy = {self.es["pe"]}
        waited = {ls: set() for ls in comp}
        for e in ENGS:
            for it in prog[e]:
                if it[0] == "w" and it[1] in comp:
                    waited[it[1]].add(it[2])
        for ls in comp:
            if ls in lazy:
                ws = sorted(waited[ls])
            else:
                ws = sorted(it[-1] for e in ENGS for it in prog[e] if it[0] == "o" and it[-2] is ls)
            ls.rank = {n: ls.base + i + 1 for i, n in enumerate(ws)}

        def replay(lst, e):
            for it in lst:
                ls, n = it[-2], it[-1]
                if it[0] == "w":
                    if ls in comp:
                        ph, val = ls.target_v(ls.rank[n])
                    else:
                        ph, val = ls.wait_target(n)
                    e.wait_ge(ph, val)
                else:
                    ins = it[1](e)
                    if ls in comp:
                        if n in ls.rank:
                            ph, _ = ls.target_v(ls.rank[n])
                            ins.then_inc(ph, 1)
                    else:
                        ph, amt = ls.inc_target(n)
                        ins.then_inc(ph, amt)

        with nc.Block() as block:
            @block.sync
            def _(e):
                replay(prog["sp"], e)

            @block.tensor
            def _(e):
                replay(prog["pe"], e)

            @block.scalar
            def _(e):
                replay(prog["act"], e)

            @block.vector
            def _(e):
                replay(prog["dve"], e)

            @block.gpsimd
            def _(e):
                replay(prog["pool"], e)
        self.prog = {e: [] for e in ENGS}
        for ls in comp:
            ls.base += len(ls.rank)
            ls.rank = {}
        for e in ENGS:
            for s in self.dma_all:
                if not s.persist:
                    self.seen[e][s] = s.n
            for s in self.es.values():
                self.seen[e][s] = s.n


class Ring:
    def __init__(self, S, tiles, dma=True):
        self.S = S
        self.slots = [(t, Buf(), S.dsem() if dma else None) for t in tiles]
        self.i = -1

    def next(self):
        self.i = (self.i + 1) % len(self.slots)
        return self.slots[self.i]

    def free(self):
        for t, b, d in self.slots:
            if d is not None:
                self.S.dsem_free(d)


def build_nc(NPo, NSo, stop=9):
    NQ = NPo + NSo
    SP, SS = 8 * NPo, 8 * NSo
    STOT = SP + SS
    NTK = STOT // 128
    NTQ = NQ // 128
    NT = NTK + NTQ
    NG = NQ // 512
    KC = 1024
    nc = bass.Bass("TRN2", target_bir_lowering=False)

    def din(name, shape):
        return nc.dram_tensor(name, shape, F32, kind="ExternalInput").ap()

    def dscr(name, shape, dt=BF16):
        return nc.dram_tensor(name, shape, dt, kind="Internal").ap()

    xall = din("xall", [STOT, D])
    xq = din("xq", [NQ, D])
    posk = din("posk", [128, NTK])
    posq = din("posq", [128, NTQ])
    ident_d = din("ident", [128, 128])
    g_attn_d = din("attn_norm_g", [1, D])
    w_in = din("w_in", [D, INC])
    lam_d = [din(n, [1, 64]) for n in ("da_lambda_q1", "da_lambda_k1", "da_lambda_q2", "da_lambda_k2")]
    g_sub_d = din("da_subln_g", [1, 128])
    g_q_d = din("mla_q_norm_g", [1, 512])
    w_qb = din("mla_w_q_b", [512, 1536])
    g_kv_d = din("mla_kv_norm_g", [1, 512])
    w_kvb = din("mla_w_kv_b", [512, 2048])
    w_bd = din("w_branch_da", [1024, D])
    w_bm = din("w_branch_mla", [1024, D])
    w_out = din("w_out", [D, D])
    g_ffn_d = din("ffn_norm_g", [1, D])
    w_gate = din("w_gate", [D, FF])
    w_up = din("w_up", [D, FF])
    w_down = din("w_down", [FF, D])
    g_fin_d = din("final_norm_g", [1, D])
    y = nc.dram_tensor("y", [NQ, D], F32, kind="ExternalOutput").ap()

    wb_in = dscr("wb_in", [D, INC])
    wb_qb = dscr("wb_qb", [512, 1536])
    wb_kvb = dscr("wb_kvb", [512, 2048])
    wb_bd = dscr("wb_bd", [1024, D])
    wb_bm = dscr("wb_bm", [1024, D])
    wb_out = dscr("wb_out", [D, D])
    wb_gate = dscr("wb_gate", [D, FF])
    wb_up = dscr("wb_up", [D, FF])
    wb_down = dscr("wb_down", [FF, D])
    rt = dscr("rt", [128, NT, 160], F32)
    KTda = dscr("KTda", [NH, 128, STOT])
    KTn = dscr("KTn", [NH, 128, STOT])
    KTr = dscr("KTr", [128, STOT])
    Vda = dscr("Vda", [NH, 128, NTK, 132])
    Vm = dscr("Vm", [NH, 128, NTK, 132])
    QTda = dscr("QTda", [NH, 128, NQ])
    QTn = dscr("QTn", [NH, 128, NQ])
    QTr = dscr("QTr", [4, 128, NQ])

    S = Sched(nc)
    B_rt, B_kv, B_q = Buf("rt"), Buf("kv"), Buf("q")

    def DMA(eng, out, in_, reads=(), writes=(), dsem=None, cowrite=False):
        return S.op(eng, lambda e: e.dma_start(out=out, in_=in_), reads, writes, dsem=dsem, cowrite=cowrite)

    def DMAS(eng, out, in_, n, reads=(), writes=(), dsem=None):
        sz = out.shape[1]
        step = (sz + n - 1) // n
        for a in range(0, sz, step):
            b_ = min(sz, a + step)
            DMA(eng, out[:, a:b_], in_[:, a:b_], reads, writes, dsem=dsem, cowrite=True)

    def MM(out, lhsT, rhs, start, stop, reads, writes, cowrite=True):
        return S.op("pe", lambda e: e.matmul(out, lhsT=lhsT, rhs=rhs, start=start, stop=stop), reads, writes, cowrite=cowrite)

    def TR(out, in_, reads, writes, cowrite=True):
        return S.op("pe", lambda e: e.transpose(out, in_, ident_b[:]), list(reads) + [B_id], writes, cowrite=cowrite)

    def ACT(out, in_, func, reads, writes, scale=1.0, bias=0.0, accum_out=None, cowrite=False):
        return S.op("act", lambda e: e.activation(out=out, in_=in_, func=func, bias=bias, scale=scale, accum_out=accum_out),
                    reads, writes, cowrite=cowrite)

    def TT(eng, out, in0, in1, op, reads, writes, cowrite=False):
        return S.op(eng, lambda e: e.tensor_tensor(out=out, in0=in0, in1=in1, op=op), reads, writes, cowrite=cowrite)

    def TS(eng, out, in0, s1, op0, reads, writes, s2=None, op1=ALU.bypass, cowrite=False):
        return S.op(eng, lambda e: e.tensor_scalar(out=out, in0=in0, scalar1=s1, scalar2=s2, op0=op0, op1=op1), reads, writes, cowrite=cowrite)

    def STT(out, in0, scalar, in1, op0, op1, reads, writes, cowrite=False):
        return S.op("dve", lambda e: e.scalar_tensor_tensor(out=out, in0=in0, scalar=scalar, in1=in1, op0=op0, op1=op1),
                    reads, writes, cowrite=cowrite)

    def CP(eng, out, in_, reads, writes, cowrite=False):
        if eng == "act":
            return S.op("act", lambda e: e.copy(out=out, in_=in_), reads, writes, cowrite=cowrite)
        return S.op(eng, lambda e: e.tensor_copy(out=out, in_=in_), reads, writes, cowrite=cowrite)

    def rstd_from_ss(ss, n, dim, reads_b, tmp, out, out_b):
        ACT(tmp, ss, AF.Ln, [reads_b], [out_b], scale=1.0 / dim, bias=eps_t[:, 0:1], cowrite=True)
        ACT(out, tmp, AF.Exp, [out_b], [out_b], scale=-0.5, cowrite=True)

    with contextlib.ExitStack() as top:
        def tsb(name, shape, dt):
            return top.enter_context(nc.sbuf_tensor(name, shape, dt))
        ident_b = tsb("ident_b", [128, 128], BF16)
        sc = tsb("sc", [128, 4], F32)
        eps_t = tsb("eps_t", [128, 1], F32)
        B_id, B_sc = Buf("id"), Buf("sc")

        wbufs = {}

        def cast_weight(key, dst, src, rows, cols):
            b = Buf(key)
            ds = S.dsem(persist=True)
            wbufs[key] = b
            bw = cols
            while bw > 2048:
                for dv in (2, 3, 4, 5, 6, 7, 8, 11):
                    if cols % dv == 0 and cols // dv <= 2048:
                        bw = cols // dv
                        break
                break
            for r0 in range(0, rows, 128):
                DMA("pool", dst[r0:r0 + 128, :].rearrange("r (a b) -> r a b", b=bw),
                    src[r0:r0 + 128, :].rearrange("r (a b) -> r a b", b=bw), writes=[b], dsem=ds, cowrite=True)

        with contextlib.ExitStack() as es:
            def sb(name, shape, dt):
                return es.enter_context(nc.sbuf_tensor(name, shape, dt))
            cast_weight("in", wb_in, w_in, D, INC)
            cast_weight("kvb", wb_kvb, w_kvb, 512, 2048)
            cast_weight("qb", wb_qb, w_qb, 512, 1536)
            idf = sb("idf", [128, 128], F32)
            lq = sb("lq", [128, 4, 64], F32)
            prod = sb("prod", [128, 2, 64], F32)
            s12 = sb("s12", [128, 2], F32)
            e12 = sb("e12", [128, 2], F32)
            d0, d0b, d0c = S.dsem(), S.dsem(), S.dsem()
            b_idf, b_lq, b_pr, b_s12 = Buf(), Buf(), Buf(), Buf()
            DMA("sp", idf[:], ident_d[:, :], writes=[b_idf], dsem=d0)
            CP("dve", ident_b[:], idf[:], [b_idf], [B_id])
            S.op("dve", lambda e: e.memset(eps_t[:], EPS), (), [B_sc], cowrite=True)
            for i in range(4):
                DMA("sp", lq[:, i, :], lam_d[i].partition_broadcast(128), writes=[b_lq], dsem=d0b, cowrite=True)
            TT("dve", prod[:, 0, :], lq[:, 0, :], lq[:, 1, :], ALU.mult, [b_lq], [b_pr], cowrite=True)
            TT("dve", prod[:, 1, :], lq[:, 2, :], lq[:, 3, :], ALU.mult, [b_lq], [b_pr], cowrite=True)
            S.op("dve", lambda e: e.tensor_reduce(out=s12[:], in_=prod[:], axis=AX.X, op=ALU.add), [b_pr], [b_s12])
            ACT(e12[:], s12[:], AF.Exp, [b_s12], [b_s12], cowrite=True)
            TT("dve", sc[:, 0:1], e12[:, 0:1], e12[:, 1:2], ALU.subtract, [b_s12], [B_sc], cowrite=True)
            TS("dve", sc[:, 1:2], sc[:, 0:1], LAMBDA_INIT, ALU.add, [B_sc], [B_sc], s2=-1.0, op1=ALU.mult, cowrite=True)
            pos = sb("pos", [128, NT], F32)
            b_pos = Buf()
            DMA("sp", pos[:, 0:NTK], posk[:, :], writes=[b_pos], dsem=d0c, cowrite=True)
            DMA("sp", pos[:, NTK:NT], posq[:, :], writes=[b_pos], dsem=d0c, cowrite=True)
            invf = np.concatenate([
                (np.float32(THETA) ** (-np.arange(0, 16, 2, dtype=np.float32) / np.float32(16))).astype(np.float32),
                (np.float32(THETA) ** (-np.arange(0, 64, 2, dtype=np.float32) / np.float32(64))).astype(np.float32)])
            CH = 24
            PI = math.pi
            C1 = 6.28125
            C2 = 2.0 * math.pi - C1
            PIC = 3.1415925
            tA = sb("tA", [128, CH, 40], F32)
            tB = sb("tB", [128, CH, 40], F32)
            tC = sb("tC", [128, CH, 40], F32)
            tD = sb("tD", [128, CH, 40], F32)
            tI = sb("tI", [128, CH, 40], I32)
            rtt = sb("rtt", [128, CH, 160], F32)
            bA, bB, bC, bD, bI, bR = Buf(), Buf(), Buf(), Buf(), Buf(), Buf()
            drt = S.dsem()
            for c0 in range(0, NT, CH):
                n = min(CH, NT - c0)
                A, Bt, C, Dt, It = tA[:, 0:n, :], tB[:, 0:n, :], tC[:, 0:n, :], tD[:, 0:n, :], tI[:, 0:n, :]
                for j in range(40):
                    TS("dve", tA[:, 0:n, j], pos[:, c0:c0 + n], float(invf[j]), ALU.mult, [b_pos], [bA], cowrite=(j > 0))
                TS("dve", It, A, 1.0 / (2.0 * PI), ALU.mult, [bA], [bI])
                CP("dve", C, It, [bI], [bC])
                STT(Dt, C, -C1, A, ALU.mult, ALU.add, [bC, bA], [bD])
                STT(A, C, -C2, Dt, ALU.mult, ALU.add, [bC, bD], [bA])

                def wrap(src, bs, tmp, bt, dst, bd_):
                    TS("dve", tmp, src, PI, ALU.is_gt, [bs], [bt], s2=-2.0 * PI, op1=ALU.mult)
                    TT("dve", dst, src, tmp, ALU.add, [bs, bt], [bd_])
                    TS("dve", tmp, dst, -PI, ALU.is_lt, [bd_], [bt], s2=2.0 * PI, op1=ALU.mult)
                    TT("dve", src, dst, tmp, ALU.add, [bd_, bt], [bs])
                    TS("dve", src, src, -PIC, ALU.max, [bs], [bs], s2=PIC, op1=ALU.min)
                wrap(A, bA, Bt, bB, Dt, bD)
                ACT(C, A, AF.Sin, [bA], [bC])
                TS("dve", Dt, A, PI / 2.0, ALU.add, [bA], [bD])
                wrap(Dt, bD, Bt, bB, A, bA)
                ACT(A, Dt, AF.Sin, [bD], [bA])
                R = rtt[:, 0:n, :]
                CP("dve", rtt[:, 0:n, 0:8], tA[:, 0:n, 0:8], [bA], [bR])
                CP("dve", rtt[:, 0:n, 8:16], tA[:, 0:n, 0:8], [bA], [bR], cowrite=True)
                TS("dve", rtt[:, 0:n, 16:24], tC[:, 0:n, 0:8], -1.0, ALU.mult, [bC], [bR], cowrite=True)
                CP("dve", rtt[:, 0:n, 24:32], tC[:, 0:n, 0:8], [bC], [bR], cowrite=True)
                CP("dve", rtt[:, 0:n, 32:64], tA[:, 0:n, 8:40], [bA], [bR], cowrite=True)
                CP("dve", rtt[:, 0:n, 64:96], tA[:, 0:n, 8:40], [bA], [bR], cowrite=True)
                TS("dve", rtt[:, 0:n, 96:128], tC[:, 0:n, 8:40], -1.0, ALU.mult, [bC], [bR], cowrite=True)
                CP("dve", rtt[:, 0:n, 128:160], tC[:, 0:n, 8:40], [bC], [bR], cowrite=True)
                DMA("sp", rt[:, c0:c0 + n, :], R, [bR], [B_rt], dsem=drt, cowrite=True)
            S.run_block(final=(stop == 1))
            for d_ in (d0, d0b, d0c, drt):
                S.dsem_free(d_)
        if stop == 1:
            return nc

        def rope(src3, C, W, rot, tab, tab_b, src_b, t1, t2, tb, dst3, dst_b):
            o = 0 if rot == 16 else 32
            hh = rot // 2
            cosf = tab[:, o:o + rot]
            s_a = tab[:, o + rot:o + rot + hh]
            s_b = tab[:, o + rot + hh:o + 2 * rot]
            for c in range(C):
                TT("dve", t1[:, c, 0:rot], src3[:, c, 0:rot], cosf, ALU.mult, [src_b, tab_b], [tb], cowrite=True)
                TT("dve", t2[:, c, 0:hh], src3[:, c, hh:rot], s_a, ALU.mult, [src_b, tab_b], [tb], cowrite=True)
                TT("dve", t2[:, c, hh:rot], src3[:, c, 0:hh], s_b, ALU.mult, [src_b, tab_b], [tb], cowrite=True)
            TT("dve", dst3[:, :, 0:rot], t1[:, 0:C, 0:rot], t2[:, 0:C, 0:rot], ALU.add, [tb], [dst_b], cowrite=True)
            if W > rot:
                CP("dve", dst3[:, :, rot:W], src3[:, :, rot:W], [src_b], [dst_b], cowrite=True)

        def proj_phase(isK):
            nt = NTK if isK else NTQ
            xin = xall if isK else xq
            t_off = 0 if isK else NTK
            NA = 2624 if isK else 1536
            NB = 2048 if isK else 1536
            with contextlib.ExitStack() as es:
                def sb(name, shape, dt):
                    return es.enter_context(nc.sbuf_tensor(name, shape, dt))

                def ps(name, shape, dt):
                    return es.enter_context(nc.psum_tensor(name, shape, dt))
                pf = "k" if isK else "q"
                w = sb(pf + "w", [128, 16, NA], BF16)
                wb2 = sb(pf + "wb2", [128, 4, NB], BF16)
                gat = sb(pf + "gat", [128, D], F32)
                gl = sb(pf + "gl", [128, 512], F32)
                b_w, b_wb2, b_g = Buf(), Buf(), Buf()
                dw, dwb, dwg = S.dsem(), S.dsem(), S.dsem()
                win = wb_in.rearrange("(kc p) c -> p kc c", p=128)
                if isK:
                    DMAS("sp", w[:, :, 0:2048], win[:, :, 1024:3072], 16, [wbufs["in"]], [b_w], dsem=dw)
                    DMAS("sp", w[:, :, 2048:2624], win[:, :, 3584:4160], 4, [wbufs["in"]], [b_w], dsem=dw)
                    DMAS("sp", wb2[:], wb_kvb.rearrange("(kc p) c -> p kc c", p=128), 4, [wbufs["kvb"]], [b_wb2], dsem=dwb)
                    DMA("sp", gl[:], g_kv_d.partition_broadcast(128), (), [b_g], dsem=dwg, cowrite=True)
                else:
                    DMAS("sp", w[:, :, 0:1024], win[:, :, 0:1024], 8, [wbufs["in"]], [b_w], dsem=dw)
                    DMAS("sp", w[:, :, 1024:1536], win[:, :, 3072:3584], 4, [wbufs["in"]], [b_w], dsem=dw)
                    DMAS("sp", wb2[:], wb_qb.rearrange("(kc p) c -> p kc c", p=128), 4, [wbufs["qb"]], [b_wb2], dsem=dwb)
                    DMA("sp", gl[:], g_q_d.partition_broadcast(128), (), [b_g], dsem=dwg, cowrite=True)
                DMA("sp", gat[:], g_attn_d.partition_broadcast(128), (), [b_g], dsem=dwg, cowrite=True)
                if isK:
                    cast_weight("bd", wb_bd, w_bd, 1024, D)
                    cast_weight("bm", wb_bm, w_bm, 1024, D)
                    cast_weight("out", wb_out, w_out, D, D)
                    cast_weight("gate", wb_gate, w_gate, D, FF)
                    cast_weight("up", wb_up, w_up, D, FF)
                    cast_weight("down", wb_down, w_down, FF, D)

                xR = Ring(S, [sb(f"{pf}x{i}", [128, D], F32) for i in range(2)])
                rtR = Ring(S, [sb(f"{pf}rt{i}", [128, 4, 160], F32) for i in range(2)])
                junk = sb(pf + "junk", [128, D], BF16)
                b_junk = Buf()
                hR = Ring(S, [sb(f"{pf}h{i}", [128, D], BF16) for i in range(2)], dma=False)
                hTR = Ring(S, [sb(f"{pf}hT{i}", [128, 16, 128], BF16) for i in range(2)], dma=False)
                stR = Ring(S, [sb(f"{pf}st{i}", [128, 8], F32) for i in range(3)], dma=False)
                t1 = sb(pf + "t1", [128, 8, 64], F32)
                t2 = sb(pf + "t2", [128, 8, 64], F32)
                b_t = Buf()
                tmA = sb(pf + "tmA", [128, 1024], BF16)
                lat = sb(pf + "lat", [128, 512], BF16)
                latT = sb(pf + "latT", [128, 4, 128], BF16)
                tmN = sb(pf + "tmN", [128, 8, 128], BF16)
                tmR = sb(pf + "tmR", [128, 8, 64], BF16)
                b_tmA, b_lat, b_latT, b_tmN, b_tmR = Buf(), Buf(), Buf(), Buf(), Buf()
                stA = sb(pf + "stA", [128, 8, 512], BF16)
                stN = sb(pf + "stN", [128, 8, 512], BF16)
                stRr = sb(pf + "stR", [128, 4, 512], BF16)
                b_stA, b_stN, b_stR = Buf(), Buf(), Buf()
                d_stA, d_stN, d_stR = S.dsem(), S.dsem(), S.dsem()
                if isK:
                    stV = sb("kstV", [128, 8, 4, 132], BF16)
                    stVm = sb("kstVm", [128, 8, 4, 132], BF16)
                    b_stV, b_stVm = Buf(), Buf()
                    S.op("dve", lambda e: e.memset(stV[:, :, :, 128:132], 1.0), (), [b_stV], cowrite=True)
                    S.op("dve", lambda e: e.memset(stVm[:, :, :, 128:132], 1.0), (), [b_stVm], cowrite=True)
                    d_stV, d_stVm = S.dsem(), S.dsem()
                tp = ps(pf + "tp", [128, 16, 128], BF16)
                pj = ps(pf + "pj", [128, 2, 512], F32)
                tk = ps(pf + "tk", [128, 2, 8, 128], BF16)
                tcp = ps(pf + "tc", [128, 4, 128], BF16)
                b_tp, b_tc = Buf(), Buf()
                pjR = Ring(S, [pj[:, i, :] for i in range(2)], dma=False)
                tkR = Ring(S, [tk[:, i, :, :] for i in range(2)], dma=False)

                ngrp = nt // 4
                for g in range(ngrp):
                    rtt_, b_rtt, d_rtt = rtR.next()
                    DMA("sp", rtt_[:], rt[:, t_off + g * 4:t_off + g * 4 + 4, :], [B_rt], [b_rtt], dsem=d_rtt)
                    for j in range(4):
                        ti = g * 4 + j
                        x_, b_x, d_x = xR.next()
                        DMA("sp", x_[:], xin[ti * 128:(ti + 1) * 128, :], (), [b_x], dsem=d_x)
                        st_, b_st, _ = stR.next()
                        ACT(junk[:], x_[:], AF.Square, [b_x], [b_junk, b_st], accum_out=st_[:, 0:1])
                        rstd_from_ss(st_[:, 0:1], 1, D, b_st, st_[:, 1:2], st_[:, 2:3], b_st)
                        h_, b_h, _ = hR.next()
                        STT(h_[:], x_[:], st_[:, 2:3], gat[:], ALU.mult, ALU.mult, [b_x, b_st, b_g], [b_h])
                        for kc in range(16):
                            TR(tp[:, kc, :], h_[:, kc * 128:(kc + 1) * 128], [b_h], [b_tp], cowrite=(kc > 0))
                        hT_, b_hT, _ = hTR.next()
                        CP("dve", hT_[:, 0:8, :], tp[:, 0:8, :], [b_tp], [b_hT])
                        CP("act", hT_[:, 8:16, :], tp[:, 8:16, :], [b_tp], [b_hT], cowrite=True)
                        tab = rtt_[:, j, :]
                        if DBG == 1:
                            continue

                        def proj_block(c0, ncol):
                            p_, b_p, _ = pjR.next()
                            for kc in range(16):
                                MM(p_[:, 0:ncol], hT_[:, kc, :], w[:, kc, c0:c0 + ncol], kc == 0, kc == 15,
                                   [b_hT, b_w], [b_p], cowrite=(kc > 0))
                            return p_, b_p

                        for blk in range(2):
                            p_, b_p = proj_block(blk * 512, 512)
                            if DBG == 5:
                                continue
                            rope(p_.rearrange("p (c w) -> p c w", w=64), 8, 64, 16, tab, b_rtt, b_p, t1, t2, b_t,
                                 tmA[:, blk * 512:(blk + 1) * 512].rearrange("p (c w) -> p c w", w=64), b_tmA)
                        if DBG == 5 or DBG == 6:
                            continue
                        tk_, b_tk, _ = tkR.next()
                        if DBG != 8:
                            for hh in range(8):
                                TR(tk_[:, hh, :], tmA[:, hh * 128:(hh + 1) * 128], [b_tmA], [b_tk], cowrite=(hh > 0))
                        if DBG != 7:
                            CP("dve", stA[:, :, j * 128:(j + 1) * 128], tk_, [b_tk], [b_stA], cowrite=True)
                        if DBG in (2, 7, 8):
                            continue
                        if isK:
                            for blk in range(2):
                                p_, b_p = proj_block(1024 + blk * 512, 512)
                                CP("act", stV[:, blk * 4:(blk + 1) * 4, j, 0:128], p_.rearrange("p (h d) -> p h d", d=128),
                                   [b_p], [b_stV], cowrite=True)
                        if DBG == 3:
                            continue
                        lat0 = 2048 if isK else 1024
                        p_, b_p = proj_block(lat0, 512)
                        st2, b_st2, _ = stR.next()
                        ACT(junk[:, 0:512], p_, AF.Square, [b_p], [b_junk, b_st2], accum_out=st2[:, 0:1])
                        rstd_from_ss(st2[:, 0:1], 1, 512, b_st2, st2[:, 1:2], st2[:, 2:3], b_st2)
                        STT(lat[:], p_, st2[:, 2:3], gl[:], ALU.mult, ALU.mult, [b_p, b_st2, b_g], [b_lat])
                        if DBG == 9:
                            continue
                        for kc in range(4):
                            TR(tcp[:, kc, :], lat[:, kc * 128:(kc + 1) * 128], [b_lat], [b_tc], cowrite=(kc > 0))
                        CP("act", latT[:], tcp[:], [b_tc], [b_latT])
                        if DBG == 10:
                            continue
                        bw = 512 if isK else 384
                        for cb in range(4):
                            p_, b_p, _ = pjR.next()
                            for kc in range(4):
                                MM(p_[:, 0:bw], latT[:, kc, :], wb2[:, kc, cb * bw:(cb + 1) * bw], kc == 0, kc == 3,
                                   [b_latT, b_wb2], [b_p], cowrite=(kc > 0))
                            if DBG == 11:
                                continue
                            if isK:
                                pv = p_.rearrange("p (h t d) -> p h t d", t=2, d=128)
                                ceng = "dve" if cb % 2 == 0 else "act"
                                for h2 in range(2):
                                    CP(ceng, tmN[:, 2 * cb + h2, :], p_[:, h2 * 256:h2 * 256 + 128], [b_p], [b_tmN], cowrite=True)
                                for h2 in range(2):
                                    CP(ceng, stVm[:, 2 * cb + h2, j, 0:128], p_[:, h2 * 256 + 128:h2 * 256 + 256], [b_p], [b_stVm], cowrite=True)
                            else:
                                pv = p_[:, 0:384].rearrange("p (h d) -> p h d", d=192)
                                for h2 in range(2):
                                    CP("dve", tmN[:, 2 * cb + h2, :], p_[:, h2 * 192:h2 * 192 + 128], [b_p], [b_tmN], cowrite=True)
                                rope(pv[:, :, 128:192], 2, 64, 64, tab, b_rtt, b_p, t1, t2, b_t,
                                     tmR[:, 2 * cb:2 * cb + 2, :], b_tmR)
                        if DBG in (11, 12):
                            continue
                        tk_, b_tk, _ = tkR.next()
                        for hh in range(8):
                            TR(tk_[:, hh, :], tmN[:, hh, :], [b_tmN], [b_tk], cowrite=(hh > 0))
                        CP("dve", stN[:, :, j * 128:(j + 1) * 128], tk_, [b_tk], [b_stN], cowrite=True)
                        if DBG == 4:
                            continue
                        if isK:
                            p_, b_p = proj_block(2560, 64)
                            p3 = p_[:, 0:64].unsqueeze(1)
                            rope(p3, 1, 64, 64, tab, b_rtt, b_p, t1, t2, b_t, tmR[:, 0:1, :], b_tmR)
                            CP("dve", tmR[:, 1:2, :], tmR[:, 0:1, :], [b_tmR], [b_tmR], cowrite=True)
                            tk_, b_tk, _ = tkR.next()
                            TR(tk_[:, 0, :], tmR[:, 0:2, :].rearrange("p a d -> p (a d)"), [b_tmR], [b_tk], cowrite=False)
                            CP("act", stRr[:, 0, j * 128:(j + 1) * 128], tk_[:, 0, :], [b_tk], [b_stR], cowrite=True)
                        else:
                            tk_, b_tk, _ = tkR.next()
                            for pr in range(4):
                                TR(tk_[:, pr, :], tmR[:, 2 * pr:2 * pr + 2, :].rearrange("p a d -> p (a d)"), [b_tmR], [b_tk],
                                   cowrite=(pr > 0))
                            CP("act", stRr[:, :, j * 128:(j + 1) * 128], tk_[:, 0:4, :], [b_tk], [b_stR], cowrite=True)
                    c0 = g * 512
                    if isK:
                        DMAS("sp", KTda[:, :, c0:c0 + 512].rearrange("h p t -> p h t"), stA[:], 2, [b_stA], [B_kv], dsem=d_stA)
                        DMAS("sp", KTn[:, :, c0:c0 + 512].rearrange("h p t -> p h t"), stN[:], 2, [b_stN], [B_kv], dsem=d_stN)
                        DMA("sp", KTr[:, c0:c0 + 512], stRr[:, 0, :], [b_stR], [B_kv], dsem=d_stR, cowrite=True)
                        DMAS("sp", Vda[:, :, g * 4:g * 4 + 4, :].rearrange("h p t d -> p h t d"), stV[:], 2, [b_stV], [B_kv], dsem=d_stV)
                        DMAS("sp", Vm[:, :, g * 4:g * 4 + 4, :].rearrange("h p t d -> p h t d"), stVm[:], 2, [b_stVm], [B_kv], dsem=d_stVm)
                    else:
                        DMAS("sp", QTda[:, :, c0:c0 + 512].rearrange("h p t -> p h t"), stA[:], 2, [b_stA], [B_q], dsem=d_stA)
                        DMAS("sp", QTn[:, :, c0:c0 + 512].rearrange("h p t -> p h t"), stN[:], 2, [b_stN], [B_q], dsem=d_stN)
                        DMA("sp", QTr[:, :, c0:c0 + 512].rearrange("h p t -> p h t"), stRr[:], [b_stR], [B_q], dsem=d_stR, cowrite=True)
                S.run_block(final=(stop == (2 if isK else 3)))
                for r in (xR, rtR):
                    r.free()
                for d_ in [dw, dwb, dwg, d_stA, d_stN, d_stR] + ([d_stV, d_stVm] if isK else []):
                    S.dsem_free(d_)

        proj_phase(True)
        if stop == 2:
            return nc
        proj_phase(False)
        if stop == 3:
            return nc

        with contextlib.ExitStack() as es:
            def sb(name, shape, dt):
                return es.enter_context(nc.sbuf_tensor(name, shape, dt))

            def ps(name, shape, dt):
                return es.enter_context(nc.psum_tensor(name, shape, dt))
            NKT = KC // 128
            ktR = Ring(S, [sb(f"a_kt{i}", [128, KC], BF16) for i in range(3)])
            krR = Ring(S, [sb(f"a_kr{i}", [128, KC], BF16) for i in range(2)])
            v_tiles = [sb(f"a_v{i}", [128, NKT, 132], BF16) for i in range(3)]
            vR = Ring(S, v_tiles)
            qtR = Ring(S, [sb(f"a_qt{i}", [128, 512], BF16) for i in range(2)])
            qrR = Ring(S, [sb(f"a_qr{i}", [128, 512], BF16) for i in range(2)])
            pTR = Ring(S, [sb(f"a_pT{i}", [128, 512], BF16) for i in range(4)], dma=False)
            o1 = sb("o1", [128, 4, 128], F32)
            o2 = sb("o2", [128, 4, 128], F32)
            o3 = sb("o3", [128, 4, 128], F32)
            osq = sb("osq", [128, 4, 128], F32)
            onb = sb("onb", [128, 4, 128], BF16)
            fst = sb("fst", [128, 16], F32)
            gsub = sb("gsub", [128, 128], F32)
            b_o1, b_o2, b_o3, b_osq, b_onb, b_fst, b_gsub = Buf(), Buf(), Buf(), Buf(), Buf(), Buf(), Buf()
            oT = sb("oT", [128, 16, 512], BF16)
            b_oT = Buf()
            x1 = sb("x1", [128, 4, D], F32)
            b_x1 = [Buf() for _ in range(4)]
            d_x1 = [S.dsem() for _ in range(4)]
            d_y = [S.dsem() for _ in range(4)]
            hT = sb("hT", [128, 16, 512], BF16)
            b_hT = Buf()
            hb = sb("hb", [128, D], BF16)
            b_hb = Buf()
            big = sb("big", [128, 22, 512], BF16)
            b_big = [Buf() for _ in range(22)]
            gbuf = sb("gbuf", [128, D], F32)
            b_gbuf = Buf()
            d_g = S.dsem()
            junk = sb("ajunk", [128, D], BF16)
            b_junk = Buf()
            pst = sb("pst", [128, 16], F32)
            b_pst = Buf()
            sg1 = sb("sg1", [128, 512], F32)
            sg2 = sb("sg2", [128, 512], F32)
            b_sg1, b_sg2 = Buf(), Buf()
            wR = Ring(S, [sb(f"wbuf{i}", [128, 8192], BF16) for i in range(3)])
            dg0 = S.dsem()
            DMA("sp", gsub[:], g_sub_d.partition_broadcast(128), (), [b_gsub], dsem=dg0)
            TS("dve", gsub[:], gsub[:], 1.0 - LAMBDA_INIT, ALU.mult, [b_gsub], [b_gsub], cowrite=True)

            stp = ps("stp", [128, 3, 512], F32)
            acc = ps("acc", [128, 4, 512], F32)
            tpo = ps("tpo", [128, 4, 128], BF16)
            stpR = Ring(S, [stp[:, i, :] for i in range(3)], dma=False)
            b_acc, b_tpo = Buf(), Buf()
            accB = [Buf() for _ in range(4)]

            SC_DA = 64 ** -0.5
            SC_MLA = 192 ** -0.5

            def load_w(view_fn):
                t_, b_, d_ = wR.next()
                view_fn(t_, b_, d_)
                return t_, b_

            for g in range(NG):
                prompt = g < NPo // 512
                k0 = 0 if prompt else SP
                klen = SP if prompt else SS
                q0 = g * 512
                nch = klen // KC
                steps = []
                for u in range(16):
                    for m in range(2 if u < 8 else 1):
                        for c in range(nch):
                            for t in range(NKT):
                                steps.append((u, m, c, t))
                cur = {"u": None, "chunk": None}
                rec = {}

                def emit_qk(i):
                    u, m, c, t = steps[i]
                    da = u < 8
                    h = u % 8
                    if cur["u"] != u:
                        cur["u"] = u
                        qt_, b_qt, d_qt = qtR.next()
                        if da:
                            DMA("sp", qt_[:], QTda[h, :, q0:q0 + 512], [B_q], [b_qt], dsem=d_qt)
                            cur["q"] = (qt_, b_qt, None, None)
                        else:
                            DMA("sp", qt_[:], QTn[h, :, q0:q0 + 512], [B_q], [b_qt], dsem=d_qt)
                            qr_, b_qr, d_qr = qrR.next()
                            DMA("sp", qr_[:], QTr[h // 2, :, q0:q0 + 512], [B_q], [b_qr], dsem=d_qr)
                            oh = slice(64, 128) if h % 2 == 0 else slice(0, 64)
                            S.op("pool", lambda e, qr_=qr_, oh=oh: e.memset(qr_[oh, :], 0.0), [b_qr], [b_qr], cowrite=True)
                            cur["q"] = (qt_, b_qt, qr_, b_qr)
                    qt_, b_qt, qr_, b_qr = cur["q"]
                    if cur["chunk"] != (u, m, c):
                        cur["chunk"] = (u, m, c)
                        ks = k0 + c * KC
                        kt_, b_kt, d_kt = ktR.next()
                        v_, b_v, d_v = vR.next()
                        if da:
                            pr = slice(m * 64, (m + 1) * 64)
                            DMA("sp", kt_[pr, :], KTda[h, pr, ks:ks + KC], [B_kv], [b_kt], dsem=d_kt)
                            DMA("sp", v_[:], Vda[h, :, ks // 128:ks // 128 + NKT, :], [B_kv], [b_v], dsem=d_v)
                            cur["kv"] = (kt_, b_kt, None, None, v_, b_v)
                        else:
                            DMA("sp", kt_[:], KTn[h, :, ks:ks + KC], [B_kv], [b_kt], dsem=d_kt)
                            kr_, b_kr, d_kr = krR.next()
                            DMA("sp", kr_[:], KTr[:, ks:ks + KC], [B_kv], [b_kr], dsem=d_kr)
                            DMA("sp", v_[:], Vm[h, :, ks // 128:ks // 128 + NKT, :], [B_kv], [b_v], dsem=d_v)
                            cur["kv"] = (kt_, b_kt, kr_, b_kr, v_, b_v)
                    kt_, b_kt, kr_, b_kr, v_, b_v = cur["kv"]
                    s_, b_s, _ = stpR.next()
                    ksl = slice(t * 128, (t + 1) * 128)
                    if da:
                        pr = slice(m * 64, (m + 1) * 64)
                        MM(s_, kt_[pr, ksl], qt_[pr, :], True, True, [b_kt, b_qt], [b_s], cowrite=False)
                    else:
                        MM(s_, kt_[:, ksl], qt_[:], True, False, [b_kt, b_qt], [b_s], cowrite=False)
                        MM(s_, kr_[:, ksl], qr_[:], False, True, [b_kr, b_qr], [b_s], cowrite=True)
                    rec[i] = (s_, b_s, v_, b_v)

                def emit_exp_pv(i):
                    u, m, c, t = steps[i]
                    da = u < 8
                    s_, b_s, v_, b_v = rec.pop(i)
                    first = (c == 0 and t == 0)
                    last = (c == nch - 1 and t == NKT - 1)
                    p_, b_p, _ = pTR.next()
                    ACT(p_[:], s_, AF.Exp, [b_s], [b_p], scale=(SC_DA if da else SC_MLA))
                    for s4 in range(4):
                        MM(acc[:, s4, 0:129], p_[:, s4 * 128:(s4 + 1) * 128], v_[:, t, 0:129], first, last,
                           [b_p, b_v], [accB[s4]], cowrite=not first)
                    return last

                def finalize(u, m):
                    da = u < 8
                    rl = fst[:, 0:4]
                    S.op("dve", lambda e, rl=rl: e.reciprocal(out=rl.unsqueeze(2), in_=acc[:, :, 128:129]), accB, [b_fst])
                    if da and m == 0:
                        for s4 in range(4):
                            TS("dve", o1[:, s4, :], acc[:, s4, 0:128], rl[:, s4:s4 + 1], ALU.mult, accB + [b_fst], [b_o1], cowrite=True)
                        return
                    if da:
                        for s4 in range(4):
                            TS("dve", o2[:, s4, :], acc[:, s4, 0:128], rl[:, s4:s4 + 1], ALU.mult, accB + [b_fst], [b_o2], cowrite=True)
                        STT(o3[:], o2[:], sc[:, 1:2], o1[:], ALU.mult, ALU.add, [b_o2, b_o1, B_sc], [b_o3])
                        TT("dve", osq[:], o3[:], o3[:], ALU.mult, [b_o3], [b_osq])
                        S.op("dve", lambda e: e.tensor_reduce(out=fst[:, 4:8], in_=osq[:], axis=AX.X, op=ALU.add), [b_osq], [b_fst], cowrite=True)
                        rstd_from_ss(fst[:, 4:8], 4, 128, b_fst, fst[:, 8:12], fst[:, 12:16], b_fst)
                        for s4 in range(4):
                            STT(onb[:, s4, :], o3[:, s4, :], fst[:, 12 + s4:13 + s4], gsub[:], ALU.mult, ALU.mult,
                                [b_o3, b_fst, b_gsub], [b_onb], cowrite=True)
                    else:
                        for s4 in range(4):
                            TS("dve", onb[:, s4, :], acc[:, s4, 0:128], rl[:, s4:s4 + 1], ALU.mult, accB + [b_fst], [b_onb], cowrite=True)
                    for s4 in range(4):
                        TR(tpo[:, s4, :], onb[:, s4, :], [b_onb], [b_tpo], cowrite=(s4 > 0))
                    CP("dve", oT[:, u, :], tpo[:].rearrange("p a b -> p (a b)"), [b_tpo], [b_oT], cowrite=True)

                LOOK = 2
                nst = len(steps)
                for i in range(min(LOOK, nst)):
                    emit_qk(i)
                for i in range(nst):
                    if i + LOOK < nst:
                        emit_qk(i + LOOK)
                    if emit_exp_pv(i):
                        finalize(steps[i][0], steps[i][1])

                DMA("sp", gbuf[:], g_attn_d.partition_broadcast(128), (), [b_gbuf], dsem=d_g)
                for j in range(4):
                    r0 = q0 + j * 128
                    DMA("sp", x1[:, j, :], xq[r0:r0 + 128, :], (), [b_x1[j]], dsem=d_x1[j])

                def norm_T(j, dstT, b_dstT, first):
                    ACT(junk[:], x1[:, j, :], AF.Square, [b_x1[j]], [b_junk, b_pst], accum_out=pst[:, 0:1])
                    rstd_from_ss(pst[:, 0:1], 1, D, b_pst, pst[:, 1:2], pst[:, 2:3], b_pst)
                    STT(hb[:], x1[:, j, :], pst[:, 2:3], gbuf[:], ALU.mult, ALU.mult, [b_x1[j], b_pst, b_gbuf], [b_hb])
                    for q4 in range(4):
                        for kk in range(4):
                            kc = q4 * 4 + kk
                            TR(tpo[:, kk, :], hb[:, kc * 128:(kc + 1) * 128], [b_hb], [b_tpo], cowrite=(kk > 0))
                        CP("dve" if q4 % 2 == 0 else "act", dstT[:, q4 * 4:q4 * 4 + 4, j * 128:(j + 1) * 128], tpo[:], [b_tpo], [b_dstT],
                           cowrite=not (first and q4 == 0))
                for j in range(4):
                    norm_T(j, hT, b_hT, j == 0)
                win = wb_in.rearrange("(kc p) c -> p kc c", p=128)
                wbd_v = wb_bd.rearrange("(kc p) c -> p kc c", p=128)
                wbm_v = wb_bm.rearrange("(kc p) c -> p kc c", p=128)
                for fc in range(16):
                    wt, b_wt, d_wt = wR.next()
                    wv = wt[:, 0:48 * 128].rearrange("p (k c) -> p k c", c=128)
                    DMA("sp", wv[:, 0:4, :], win[:, 0:4, 4160 + fc * 128:4160 + (fc + 1) * 128], [wbufs["in"]], [b_wt], dsem=d_wt)
                    DMAS("sp", wv[:, 4:16, :], win[:, 4:16, 4160 + fc * 128:4160 + (fc + 1) * 128], 3, [wbufs["in"]], [b_wt], dsem=d_wt)
                    DMAS("sp", wv[:, 16:32, :], win[:, :, 4160 + D + fc * 128:4160 + D + (fc + 1) * 128], 4, [wbufs["in"]], [b_wt], dsem=d_wt)
                    DMAS("sp", wv[:, 32:40, :], wbd_v[:, :, fc * 128:(fc + 1) * 128], 2, [wbufs["bd"]], [b_wt], dsem=d_wt)
                    DMAS("sp", wv[:, 40:48, :], wbm_v[:, :, fc * 128:(fc + 1) * 128], 2, [wbufs["bm"]], [b_wt], dsem=d_wt)
                    for kc in range(16):
                        MM(acc[:, 0, :], wv[:, kc, :], hT[:, kc, :], kc == 0, kc == 15, [b_wt, b_hT], [accB[0]], cowrite=(kc > 0))
                    for kc in range(16):
                        MM(acc[:, 1, :], wv[:, 16 + kc, :], hT[:, kc, :], kc == 0, kc == 15, [b_wt, b_hT], [accB[1]], cowrite=(kc > 0))
                    for kc in range(8):
                        MM(acc[:, 2, :], wv[:, 32 + kc, :], oT[:, kc, :], kc == 0, kc == 7, [b_wt, b_oT], [accB[2]], cowrite=(kc > 0))
                    for kc in range(8):
                        MM(acc[:, 3, :], wv[:, 40 + kc, :], oT[:, 8 + kc, :], kc == 0, kc == 7, [b_wt, b_oT], [accB[3]], cowrite=(kc > 0))
                    ACT(sg1[:], acc[:, 0, :], AF.Sigmoid, [accB[0]], [b_sg1])
                    ACT(sg2[:], acc[:, 1, :], AF.Sigmoid, [accB[1]], [b_sg2])
                    TT("dve", sg1[:], sg1[:], acc[:, 2, :], ALU.mult, [b_sg1, accB[2]], [b_sg1])
                    TT("dve", sg2[:], sg2[:], acc[:, 3, :], ALU.mult, [b_sg2, accB[3]], [b_sg2])
                    TT("dve", big[:, fc, :], sg1[:], sg2[:], ALU.add, [b_sg1, b_sg2], [b_big[fc]])
                wout_v = wb_out.rearrange("(kc p) c -> p kc c", p=128)
                for cb in range(4):
                    wt, b_wt, d_wt = wR.next()
                    wv = wt[:].rearrange("p (k c) -> p k c", c=512)
                    DMA("sp", wv[:, 0:4, :], wout_v[:, 0:4, cb * 512:(cb + 1) * 512], [wbufs["out"]], [b_wt], dsem=d_wt)
                    DMAS("sp", wv[:, 4:16, :], wout_v[:, 4:16, cb * 512:(cb + 1) * 512], 3, [wbufs["out"]], [b_wt], dsem=d_wt)
                    for j in range(4):
                        for kc in range(16):
                            MM(acc[:, j, :], big[:, kc, j * 128:(j + 1) * 128], wv[:, kc, :], kc == 0, kc == 15, [b_wt, b_big[kc]], [accB[j]],
                               cowrite=(kc > 0))
                        xs = x1[:, j, cb * 512:(cb + 1) * 512]
                        TT("dve", xs, xs, acc[:, j, :], ALU.add, [b_x1[j], accB[j]], [b_x1[j]], cowrite=True)
                DMA("sp", gbuf[:], g_ffn_d.partition_broadcast(128), (), [b_gbuf], dsem=d_g)
                for j in range(4):
                    norm_T(j, hT, b_hT, j == 0)
                wg_v = wb_gate.rearrange("(kc p) c -> p kc c", p=128)
                wu_v = wb_up.rearrange("(kc p) c -> p kc c", p=128)
                wd_v = wb_down.rearrange("(hc p) c -> p hc c", p=128)
                for half in range(2):
                    for hp2 in range(11):
                        hc0 = half * 22 + hp2 * 2
                        wt, b_wt, d_wt = wR.next()
                        wv = wt[:].rearrange("p (s k c) -> p s k c", s=2, c=256)
                        DMA("sp", wv[:, 0, 0:4, :], wg_v[:, 0:4, hc0 * 128:(hc0 + 2) * 128], [wbufs["gate"]], [b_wt], dsem=d_wt)
                        DMAS("sp", wv[:, 0, 4:16, :], wg_v[:, 4:16, hc0 * 128:(hc0 + 2) * 128], 3, [wbufs["gate"]], [b_wt], dsem=d_wt)
                        DMAS("sp", wv[:, 1, :, :], wu_v[:, :, hc0 * 128:(hc0 + 2) * 128], 4, [wbufs["up"]], [b_wt], dsem=d_wt)
                        for i2 in range(2):
                            ga, ua = (0, 1) if i2 == 0 else (2, 3)
                            for kc in range(16):
                                MM(acc[:, ga, :], wv[:, 0, kc, i2 * 128:(i2 + 1) * 128], hT[:, kc, :], kc == 0, kc == 15, [b_wt, b_hT], [accB[ga]],
                                   cowrite=(kc > 0))
                            for kc in range(16):
                                MM(acc[:, ua, :], wv[:, 1, kc, i2 * 128:(i2 + 1) * 128], hT[:, kc, :], kc == 0, kc == 15, [b_wt, b_hT], [accB[ua]],
                                   cowrite=(kc > 0))
                            sg, b_sg = (sg1, b_sg1) if i2 == 0 else (sg2, b_sg2)
                            ACT(sg[:], acc[:, ga, :], AF.Silu, [accB[ga]], [b_sg])
                            ci = hp2 * 2 + i2
                            TT("dve", big[:, ci, :], sg[:], acc[:, ua, :], ALU.mult, [b_sg, accB[ua]], [b_big[ci]])
                    for cb in range(4):
                        wts = []
                        for part in range(2):
                            wt, b_wt, d_wt = wR.next()
                            wv = wt[:, 0:11 * 512].rearrange("p (k c) -> p k c", c=512)
                            hc0 = half * 22 + part * 11
                            DMA("sp", wv[:, 0:4, :], wd_v[:, hc0:hc0 + 4, cb * 512:(cb + 1) * 512], [wbufs["down"]], [b_wt], dsem=d_wt)
                            DMAS("sp", wv[:, 4:11, :], wd_v[:, hc0 + 4:hc0 + 11, cb * 512:(cb + 1) * 512], 2, [wbufs["down"]], [b_wt], dsem=d_wt)
                            wts.append((wv, b_wt))
                        for j in range(4):
                            for ci in range(22):
                                wv, b_wt = wts[ci // 11]
                                MM(acc[:, j, :], big[:, ci, j * 128:(j + 1) * 128], wv[:, ci % 11, :], ci == 0, ci == 21, [b_wt, b_big[ci]], [accB[j]],
                                   cowrite=(ci > 0))
                            xs = x1[:, j, cb * 512:(cb + 1) * 512]
                            TT("dve", xs, xs, acc[:, j, :], ALU.add, [b_x1[j], accB[j]], [b_x1[j]], cowrite=True)
                DMA("sp", gbuf[:], g_fin_d.partition_broadcast(128), (), [b_gbuf], dsem=d_g)
                for j in range(4):
                    ACT(junk[:], x1[:, j, :], AF.Square, [b_x1[j]], [b_junk, b_pst], accum_out=pst[:, 0:1])
                    rstd_from_ss(pst[:, 0:1], 1, D, b_pst, pst[:, 1:2], pst[:, 2:3], b_pst)
                    STT(x1[:, j, :], x1[:, j, :], pst[:, 2:3], gbuf[:], ALU.mult, ALU.mult, [b_x1[j], b_pst, b_gbuf], [b_x1[j]], cowrite=True)
                    r0 = q0 + j * 128
                    DMA("sp", y[r0:r0 + 128, :], x1[:, j, :], [b_x1[j]], (), dsem=d_y[j])
            S.run_block(final=True)
    return nc


_NC_CACHE = {}
PARAM_NAMES = ["attn_norm_g", "w_in", "da_lambda_q1", "da_lambda_k1", "da_lambda_q2", "da_lambda_k2", "da_subln_g",
               "mla_q_norm_g", "mla_w_q_b", "mla_kv_norm_g", "mla_w_kv_b", "w_branch_da", "w_branch_mla", "w_out",
               "ffn_norm_g", "w_gate", "w_up", "w_down"]


STOP = 9
NCORES = 8
DBG = 0


def kernel(x_prompt, x_sample, final_norm_g, **params):
    xp = np.asarray(x_prompt, dtype=np.float32)[0]
    xs = np.asarray(x_sample, dtype=np.float32)[0]
    SP, SS = xp.shape[0], xs.shape[0]
    NPo, NSo = SP // 8, SS // 8
    key = (NPo, NSo)
    if key not in _NC_CACHE:
        _NC_CACHE[key] = build_nc(NPo, NSo, STOP)
    nc = _NC_CACHE[key]
    xall = np.ascontiguousarray(np.concatenate([xp, xs], axis=0))
    shared = {"xall": xall, "ident": np.eye(128, dtype=np.float32)}
    for n in PARAM_NAMES:
        a = np.asarray(params[n], dtype=np.float32)
        a = a[0]
        if a.ndim == 1:
            a = a[None, :]
        shared[n] = np.ascontiguousarray(a)
    shared["final_norm_g"] = np.ascontiguousarray(np.asarray(final_norm_g, dtype=np.float32)[None, :])
    pk = np.concatenate([np.arange(SP), np.arange(SS)]).astype(np.float32)
    shared["posk"] = np.ascontiguousarray(pk.reshape(-1, 128).T)
    in_maps = []
    for c in range(8):
        m = dict(shared)
        m["xq"] = np.ascontiguousarray(np.concatenate([xp[c * NPo:(c + 1) * NPo], xs[c * NSo:(c + 1) * NSo]], axis=0))
        pq = np.concatenate([np.arange(c * NPo, (c + 1) * NPo), np.arange(c * NSo, (c + 1) * NSo)]).astype(np.float32)
        m["posq"] = np.ascontiguousarray(pq.reshape(-1, 128).T)
        in_maps.append(m)
    res = run_bass_kernel_spmd(nc, in_maps[:NCORES], core_ids=list(range(NCORES)))
    rr = [res.results[c]["y"] if c < NCORES else np.zeros((NPo + NSo, D), np.float32) for c in range(8)]
    yp = np.concatenate([rr[c][:NPo] for c in range(8)], axis=0)[None]
    ys = np.concatenate([rr[c][NPo:] for c in range(8)], axis=0)[None]
    return (yp.astype(np.float32), ys.astype(np.float32))
```

```python
import contextlib
import math
import numpy as np
import concourse.bass as bass
import concourse.mybir as mybir
from concourse.bass_utils import run_bass_kernel_spmd

F32 = mybir.dt.float32
BF16 = mybir.dt.bfloat16
I32 = mybir.dt.int32
AF = mybir.ActivationFunctionType
ALU = mybir.AluOpType
AX = mybir.AxisListType

D = 2048
NH = 8
FF = 5632
INC = 8256
EPS = 1e-6
THETA = 500000.0
LAMBDA_INIT = 0.8 - 0.6 * math.exp(0.0)


class LSem:
    def __init__(self, S, name, step):
        self.S, self.name, self.step = S, name, step
        self.epoch = 28000 // step
        self.n = 0
        self.phys = []
        self.persist = False
        self.base = 0
        self.rank = {}

    def target_v(self, v):
        e = (v - 1) // self.epoch
        return self._phys(e), ((v - 1) % self.epoch + 1) * self.step

    def _phys(self, e):
        while len(self.phys) <= e:
            self.phys.append(self.S.nc.alloc_semaphore(name=f"{self.name}_{len(self.phys)}"))
        return self.phys[e]

    def next(self):
        self.n += 1
        return self.n

    def inc_target(self, n):
        return self._phys((n - 1) // self.epoch), self.step

    def wait_target(self, n):
        e = (n - 1) // self.epoch
        return self._phys(e), ((n - 1) % self.epoch + 1) * self.step


class Buf:
    def __init__(self, name=""):
        self.name = name
        self.writers = {}
        self.readers = {}


ENGS = ("pe", "act", "dve", "pool", "sp")


class Sched:
    def __init__(self, nc):
        self.nc = nc
        self.es = {e: LSem(self, "e_" + e, 1) for e in ("pe", "act", "dve", "pool")}
        self.prog = {e: [] for e in ENGS}
        self.seen = {e: {} for e in ENGS}
        self.dma_pool = []
        self.dma_all = []
        self.nops = 0

    def dsem(self, persist=False):
        if self.dma_pool and not persist:
            return self.dma_pool.pop()
        s = LSem(self, f"d{len(self.dma_all)}", 16)
        s.persist = persist
        self.dma_all.append(s)
        return s

    def dsem_free(self, s):
        self.dma_pool.append(s)

    def _wait(self, eng, lsem, n):
        seen = self.seen[eng]
        if seen.get(lsem, 0) >= n:
            return
        seen[lsem] = n
        self.prog[eng].append(("w", lsem, n))

    def op(self, eng, fn, reads=(), writes=(), dsem=None, cowrite=False):
        own = self.es.get(eng) if dsem is None else None
        deps = []
        for b in reads:
            for ls, n in b.writers.items():
                if ls is own and eng == "pe":
                    continue
                deps.append((ls, n))
        for b in writes:
            if not cowrite:
                for ls, n in b.writers.items():
                    if ls is own:
                        continue
                    deps.append((ls, n))
            for ls, n in b.readers.items():
                if ls is own:
                    continue
                deps.append((ls, n))
        for ls, n in deps:
            self._wait(eng, ls, n)
        ls = dsem if dsem is not None else own
        n = ls.next()
        self.prog[eng].append(("o", fn, ls, n))
        self.nops += 1
        for b in reads:
            if b.readers.get(ls, 0) < n:
                b.readers[ls] = n
        for b in writes:
            if cowrite:
                b.writers[ls] = n
            else:
                b.writers = {ls: n}
                b.readers = {}
        return (ls, n)

    def run_block(self, final=False):
        for s in self.dma_all:
            if s.n and (final or not s.persist):
                self._wait("sp", s, s.n)
        for e, s in self.es.items():
            if s.n:
                self._wait("sp", s, s.n)
        nc = self.nc
        prog = self.prog
        comp = set(self.es.values())
        lazy = {self.es["pe"]}
        waited = {ls: set() for ls in comp}
        for e in ENGS:
            for it in prog[e]:
                if it[0] == "w" and it[1] in comp:
                    waited[it[1]].add(it[2])
        for ls in comp:
            if ls in lazy:
                ws = sorted(waited[ls])
            else:
                ws = sorted(it[-1] for e in ENGS for it in prog[e] if it[0] == "o" and it[-2] is ls)
            ls.rank = {n: ls.base + i + 1 for i, n in enumerate(ws)}

        def replay(lst, e):
            for it in lst:
                ls, n = it[-2], it[-1]
                if it[0] == "w":
                    if ls in comp:
                        ph, val = ls.target_v(ls.rank[n])
                    else:
                        ph, val = ls.wait_target(n)
                    e.wait_ge(ph, val)
                else:
                    ins = it[1](e)
                    if ls in comp:
                        if n in ls.rank:
                            ph, _ = ls.target_v(ls.rank[n])
                            ins.then_inc(ph, 1)
                    else:
                        ph, amt = ls.inc_target(n)
                        ins.then_inc(ph, amt)

        with nc.Block() as block:
            @block.sync
            def _(e):
                replay(prog["sp"], e)

            @block.tensor
            def _(e):
                replay(prog["pe"], e)

            @block.scalar
            def _(e):
                replay(prog["act"], e)

            @block.vector
            def _(e):
                replay(prog["dve"], e)

            @block.gpsimd
            def _(e):
                replay(prog["pool"], e)
        self.prog = {e: [] for e in ENGS}
        for ls in comp:
            ls.base += len(ls.rank)
            ls.rank = {}
        for e in ENGS:
            for s in self.dma_all:
                if not s.persist:
                    self.seen[e][s] = s.n
            for s in self.es.values():
                self.seen[e][s] = s.n


class Ring:
    def __init__(self, S, tiles, dma=True):
        self.S = S
        self.slots = [(t, Buf(), S.dsem() if dma else None) for t in tiles]
        self.i = -1

    def next(self):
        self.i = (self.i + 1) % len(self.slots)
        return self.slots[self.i]

    def free(self):
        for t, b, d in self.slots:
            if d is not None:
                self.S.dsem_free(d)


def build_nc(NPo, NSo, stop=9):
    NQ = NPo + NSo
    SP, SS = 8 * NPo, 8 * NSo
    STOT = SP + SS
    NTK = STOT // 128
    NTQ = NQ // 128
    NT = NTK + NTQ
    NG = NQ // 512
    KC = 1024
    nc = bass.Bass("TRN2", target_bir_lowering=False)

    def din(name, shape):
        return nc.dram_tensor(name, shape, F32, kind="ExternalInput").ap()

    def dscr(name, shape, dt=BF16):
        return nc.dram_tensor(name, shape, dt, kind="Internal").ap()

    xall = din("xall", [STOT, D])
    xq = din("xq", [NQ, D])
    posk = din("posk", [128, NTK])
    posq = din("posq", [128, NTQ])
    ident_d = din("ident", [128, 128])
    g_attn_d = din("attn_norm_g", [1, D])
    w_in = din("w_in", [D, INC])
    lam_d = [din(n, [1, 64]) for n in ("da_lambda_q1", "da_lambda_k1", "da_lambda_q2", "da_lambda_k2")]
    g_sub_d = din("da_subln_g", [1, 128])
    g_q_d = din("mla_q_norm_g", [1, 512])
    w_qb = din("mla_w_q_b", [512, 1536])
    g_kv_d = din("mla_kv_norm_g", [1, 512])
    w_kvb = din("mla_w_kv_b", [512, 2048])
    w_bd = din("w_branch_da", [1024, D])
    w_bm = din("w_branch_mla", [1024, D])
    w_out = din("w_out", [D, D])
    g_ffn_d = din("ffn_norm_g", [1, D])
    w_gate = din("w_gate", [D, FF])
    w_up = din("w_up", [D, FF])
    w_down = din("w_down", [FF, D])
    g_fin_d = din("final_norm_g", [1, D])
    y = nc.dram_tensor("y", [NQ, D], F32, kind="ExternalOutput").ap()

    wb_in = dscr("wb_in", [D, INC])
    wb_qb = dscr("wb_qb", [512, 1536])
    wb_kvb = dscr("wb_kvb", [512, 2048])
    wb_bd = dscr("wb_bd", [1024, D])
    wb_bm = dscr("wb_bm", [1024, D])
    wb_out = dscr("wb_out", [D, D])
    wb_gate = dscr("wb_gate", [D, FF])
    wb_up = dscr("wb_up", [D, FF])
    wb_down = dscr("wb_down", [FF, D])
    rt = dscr("rt", [128, NT, 160], F32)
    KTda = dscr("KTda", [NH, 128, STOT])
    KTn = dscr("KTn", [NH, 128, STOT])
    KTr = dscr("KTr", [128, STOT])
    Vda = dscr("Vda", [NH, 128, NTK, 132])
    Vm = dscr("Vm", [NH, 128, NTK, 132])
    QTda = dscr("QTda", [NH, 128, NQ])
    QTn = dscr("QTn", [NH, 128, NQ])
    QTr = dscr("QTr", [4, 128, NQ])

    S = Sched(nc)
    B_rt, B_kv, B_q = Buf("rt"), Buf("kv"), Buf("q")

    def DMA(eng, out, in_, reads=(), writes=(), dsem=None, cowrite=False):
        return S.op(eng, lambda e: e.dma_start(out=out, in_=in_), reads, writes, dsem=dsem, cowrite=cowrite)

    def DMAS(eng, out, in_, n, reads=(), writes=(), dsem=None):
        sz = out.shape[1]
        step = (sz + n - 1) // n
        for a in range(0, sz, step):
            b_ = min(sz, a + step)
            DMA(eng, out[:, a:b_], in_[:, a:b_], reads, writes, dsem=dsem, cowrite=True)

    def MM(out, lhsT, rhs, start, stop, reads, writes, cowrite=True):
        return S.op("pe", lambda e: e.matmul(out, lhsT=lhsT, rhs=rhs, start=start, stop=stop), reads, writes, cowrite=cowrite)

    def TR(out, in_, reads, writes, cowrite=True):
        return S.op("pe", lambda e: e.transpose(out, in_, ident_b[:]), list(reads) + [B_id], writes, cowrite=cowrite)

    def ACT(out, in_, func, reads, writes, scale=1.0, bias=0.0, accum_out=None, cowrite=False):
        return S.op("act", lambda e: e.activation(out=out, in_=in_, func=func, bias=bias, scale=scale, accum_out=accum_out),
                    reads, writes, cowrite=cowrite)

    def TT(eng, out, in0, in1, op, reads, writes, cowrite=False):
        return S.op(eng, lambda e: e.tensor_tensor(out=out, in0=in0, in1=in1, op=op), reads, writes, cowrite=cowrite)

    def TS(eng, out, in0, s1, op0, reads, writes, s2=None, op1=ALU.bypass, cowrite=False):
        return S.op(eng, lambda e: e.tensor_scalar(out=out, in0=in0, scalar1=s1, scalar2=s2, op0=op0, op1=op1), reads, writes, cowrite=cowrite)

    def STT(out, in0, scalar, in1, op0, op1, reads, writes, cowrite=False):
        return S.op("dve", lambda e: e.scalar_tensor_tensor(out=out, in0=in0, scalar=scalar, in1=in1, op0=op0, op1=op1),
                    reads, writes, cowrite=cowrite)

    def CP(eng, out, in_, reads, writes, cowrite=False):
        if eng == "act":
            return S.op("act", lambda e: e.copy(out=out, in_=in_), reads, writes, cowrite=cowrite)
        return S.op(eng, lambda e: e.tensor_copy(out=out, in_=in_), reads, writes, cowrite=cowrite)

    def rstd_from_ss(ss, n, dim, reads_b, tmp, out, out_b):
        ACT(tmp, ss, AF.Ln, [reads_b], [out_b], scale=1.0 / dim, bias=eps_t[:, 0:1], cowrite=True)
        ACT(out, tmp, AF.Exp, [out_b], [out_b], scale=-0.5, cowrite=True)

    with contextlib.ExitStack() as top:
        def tsb(name, shape, dt):
            return top.enter_context(nc.sbuf_tensor(name, shape, dt))
        ident_b = tsb("ident_b", [128, 128], BF16)
        sc = tsb("sc", [128, 4], F32)
        eps_t = tsb("eps_t", [128, 1], F32)
        B_id, B_sc = Buf("id"), Buf("sc")

        wbufs = {}

        def cast_weight(key, dst, src, rows, cols):
            b = Buf(key)
            ds = S.dsem(persist=True)
            wbufs[key] = b
            bw = cols
            while bw > 2048:
                for dv in (2, 3, 4, 5, 6, 7, 8, 11):
                    if cols % dv == 0 and cols // dv <= 2048:
                        bw = cols // dv
                        break
                break
            for r0 in range(0, rows, 128):
                DMA("pool", dst[r0:r0 + 128, :].rearrange("r (a b) -> r a b", b=bw),
                    src[r0:r0 + 128, :].rearrange("r (a b) -> r a b", b=bw), writes=[b], dsem=ds, cowrite=True)

        with contextlib.ExitStack() as es:
            def sb(name, shape, dt):
                return es.enter_context(nc.sbuf_tensor(name, shape, dt))
            cast_weight("in", wb_in, w_in, D, INC)
            cast_weight("kvb", wb_kvb, w_kvb, 512, 2048)
            cast_weight("qb", wb_qb, w_qb, 512, 1536)
            idf = sb("idf", [128, 128], F32)
            lq = sb("lq", [128, 4, 64], F32)
            prod = sb("prod", [128, 2, 64], F32)
            s12 = sb("s12", [128, 2], F32)
            e12 = sb("e12", [128, 2], F32)
            d0, d0b, d0c = S.dsem(), S.dsem(), S.dsem()
            b_idf, b_lq, b_pr, b_s12 = Buf(), Buf(), Buf(), Buf()
            DMA("sp", idf[:], ident_d[:, :], writes=[b_idf], dsem=d0)
            CP("dve", ident_b[:], idf[:], [b_idf], [B_id])
            S.op("dve", lambda e: e.memset(eps_t[:], EPS), (), [B_sc], cowrite=True)
            for i in range(4):
                DMA("sp", lq[:, i, :], lam_d[i].partition_broadcast(128), writes=[b_lq], dsem=d0b, cowrite=True)
            TT("dve", prod[:, 0, :], lq[:, 0, :], lq[:, 1, :], ALU.mult, [b_lq], [b_pr], cowrite=True)
            TT("dve", prod[:, 1, :], lq[:, 2, :], lq[:, 3, :], ALU.mult, [b_lq], [b_pr], cowrite=True)
            S.op("dve", lambda e: e.tensor_reduce(out=s12[:], in_=prod[:], axis=AX.X, op=ALU.add), [b_pr], [b_s12])
            ACT(e12[:], s12[:], AF.Exp, [b_s12], [b_s12], cowrite=True)
            TT("dve", sc[:, 0:1], e12[:, 0:1], e12[:, 1:2], ALU.subtract, [b_s12], [B_sc], cowrite=True)
            TS("dve", sc[:, 1:2], sc[:, 0:1], LAMBDA_INIT, ALU.add, [B_sc], [B_sc], s2=-1.0, op1=ALU.mult, cowrite=True)
            pos = sb("pos", [128, NT], F32)
            b_pos = Buf()
            DMA("sp", pos[:, 0:NTK], posk[:, :], writes=[b_pos], dsem=d0c, cowrite=True)
            DMA("sp", pos[:, NTK:NT], posq[:, :], writes=[b_pos], dsem=d0c, cowrite=True)
            invf = np.concatenate([
                (np.float32(THETA) ** (-np.arange(0, 16, 2, dtype=np.float32) / np.float32(16))).astype(np.float32),
                (np.float32(THETA) ** (-np.arange(0, 64, 2, dtype=np.float32) / np.float32(64))).astype(np.float32)])
            CH = 24
            PI = math.pi
            C1 = 6.28125
            C2 = 2.0 * math.pi - C1
            PIC = 3.1415925
            tA = sb("tA", [128, CH, 40], F32)
            tB = sb("tB", [128, CH, 40], F32)
            tC = sb("tC", [128, CH, 40], F32)
            tD = sb("tD", [128, CH, 40], F32)
            tI = sb("tI", [128, CH, 40], I32)
            rtt = sb("rtt", [128, CH, 160], F32)
            bA, bB, bC, bD, bI, bR = Buf(), Buf(), Buf(), Buf(), Buf(), Buf()
            drt = S.dsem()
            for c0 in range(0, NT, CH):
                n = min(CH, NT - c0)
                A, Bt, C, Dt, It = tA[:, 0:n, :], tB[:, 0:n, :], tC[:, 0:n, :], tD[:, 0:n, :], tI[:, 0:n, :]
                for j in range(40):
                    TS("dve", tA[:, 0:n, j], pos[:, c0:c0 + n], float(invf[j]), ALU.mult, [b_pos], [bA], cowrite=(j > 0))
                TS("dve", It, A, 1.0 / (2.0 * PI), ALU.mult, [bA], [bI])
                CP("dve", C, It, [bI], [bC])
                STT(Dt, C, -C1, A, ALU.mult, ALU.add, [bC, bA], [bD])
                STT(A, C, -C2, Dt, ALU.mult, ALU.add, [bC, bD], [bA])

                def wrap(src, bs, tmp, bt, dst, bd_):
                    TS("dve", tmp, src, PI, ALU.is_gt, [bs], [bt], s2=-2.0 * PI, op1=ALU.mult)
                    TT("dve", dst, src, tmp, ALU.add, [bs, bt], [bd_])
                    TS("dve", tmp, dst, -PI, ALU.is_lt, [bd_], [bt], s2=2.0 * PI, op1=ALU.mult)
                    TT("dve", src, dst, tmp, ALU.add, [bd_, bt], [bs])
                    TS("dve", src, src, -PIC, ALU.max, [bs], [bs], s2=PIC, op1=ALU.min)
                wrap(A, bA, Bt, bB, Dt, bD)
                ACT(C, A, AF.Sin, [bA], [bC])
                TS("dve", Dt, A, PI / 2.0, ALU.add, [bA], [bD])
                wrap(Dt, bD, Bt, bB, A, bA)
                ACT(A, Dt, AF.Sin, [bD], [bA])
                R = rtt[:, 0:n, :]
                CP("dve", rtt[:, 0:n, 0:8], tA[:, 0:n, 0:8], [bA], [bR])
                CP("dve", rtt[:, 0:n, 8:16], tA[:, 0:n, 0:8], [bA], [bR], cowrite=True)
                TS("dve", rtt[:, 0:n, 16:24], tC[:, 0:n, 0:8], -1.0, ALU.mult, [bC], [bR], cowrite=True)
                CP("dve", rtt[:, 0:n, 24:32], tC[:, 0:n, 0:8], [bC], [bR], cowrite=True)
                CP("dve", rtt[:, 0:n, 32:64], tA[:, 0:n, 8:40], [bA], [bR], cowrite=True)
                CP("dve", rtt[:, 0:n, 64:96], tA[:, 0:n, 8:40], [bA], [bR], cowrite=True)
                TS("dve", rtt[:, 0:n, 96:128], tC[:, 0:n, 8:40], -1.0, ALU.mult, [bC], [bR], cowrite=True)
                CP("dve", rtt[:, 0:n, 128:160], tC[:, 0:n, 8:40], [bC], [bR], cowrite=True)
                DMA("sp", rt[:, c0:c0 + n, :], R, [bR], [B_rt], dsem=drt, cowrite=True)
            S.run_block(final=(stop == 1))
            for d_ in (d0, d0b, d0c, drt):
                S.dsem_free(d_)
        if stop == 1:
            return nc

        def rope(src3, C, W, rot, tab, tab_b, src_b, t1, t2, tb, dst3, dst_b):
            o = 0 if rot == 16 else 32
            hh = rot // 2
            cosf = tab[:, o:o + rot]
            s_a = tab[:, o + rot:o + rot + hh]
            s_b = tab[:, o + rot + hh:o + 2 * rot]
            for c in range(C):
                TT("dve", t1[:, c, 0:rot], src3[:, c, 0:rot], cosf, ALU.mult, [src_b, tab_b], [tb], cowrite=True)
                TT("dve", t2[:, c, 0:hh], src3[:, c, hh:rot], s_a, ALU.mult, [src_b, tab_b], [tb], cowrite=True)
                TT("dve", t2[:, c, hh:rot], src3[:, c, 0:hh], s_b, ALU.mult, [src_b, tab_b], [tb], cowrite=True)
            TT("dve", dst3[:, :, 0:rot], t1[:, 0:C, 0:rot], t2[:, 0:C, 0:rot], ALU.add, [tb], [dst_b], cowrite=True)
            if W > rot:
                CP("dve", dst3[:, :, rot:W], src3[:, :, rot:W], [src_b], [dst_b], cowrite=True)

        def proj_phase(isK):
            nt = NTK if isK else NTQ
            xin = xall if isK else xq
            t_off = 0 if isK else NTK
            NA = 2624 if isK else 1536
            NB = 2048 if isK else 1536
            with contextlib.ExitStack() as es:
                def sb(name, shape, dt):
                    return es.enter_context(nc.sbuf_tensor(name, shape, dt))

                def ps(name, shape, dt):
                    return es.enter_context(nc.psum_tensor(name, shape, dt))
                pf = "k" if isK else "q"
                w = sb(pf + "w", [128, 16, NA], BF16)
                wb2 = sb(pf + "wb2", [128, 4, NB], BF16)
                gat = sb(pf + "gat", [128, D], F32)
                gl = sb(pf + "gl", [128, 512], F32)
                b_w, b_wb2, b_g = Buf(), Buf(), Buf()
                dw, dwb, dwg = S.dsem(), S.dsem(), S.dsem()
                win = wb_in.rearrange("(kc p) c -> p kc c", p=128)
                if isK:
                    DMAS("sp", w[:, :, 0:2048], win[:, :, 1024:3072], 16, [wbufs["in"]], [b_w], dsem=dw)
                    DMAS("sp", w[:, :, 2048:2624], win[:, :, 3584:4160], 4, [wbufs["in"]], [b_w], dsem=dw)
                    DMAS("sp", wb2[:], wb_kvb.rearrange("(kc p) c -> p kc c", p=128), 4, [wbufs["kvb"]], [b_wb2], dsem=dwb)
                    DMA("sp", gl[:], g_kv_d.partition_broadcast(128), (), [b_g], dsem=dwg, cowrite=True)
                else:
                    DMAS("sp", w[:, :, 0:1024], win[:, :, 0:1024], 8, [wbufs["in"]], [b_w], dsem=dw)
                    DMAS("sp", w[:, :, 1024:1536], win[:, :, 3072:3584], 4, [wbufs["in"]], [b_w], dsem=dw)
                    DMAS("sp", wb2[:], wb_qb.rearrange("(kc p) c -> p kc c", p=128), 4, [wbufs["qb"]], [b_wb2], dsem=dwb)
                    DMA("sp", gl[:], g_q_d.partition_broadcast(128), (), [b_g], dsem=dwg, cowrite=True)
                DMA("sp", gat[:], g_attn_d.partition_broadcast(128), (), [b_g], dsem=dwg, cowrite=True)
                if isK:
                    cast_weight("bd", wb_bd, w_bd, 1024, D)
                    cast_weight("bm", wb_bm, w_bm, 1024, D)
                    cast_weight("out", wb_out, w_out, D, D)
                    cast_weight("gate", wb_gate, w_gate, D, FF)
                    cast_weight("up", wb_up, w_up, D, FF)
                    cast_weight("down", wb_down, w_down, FF, D)

                xR = Ring(S, [sb(f"{pf}x{i}", [128, D], F32) for i in range(2)])
                rtR = Ring(S, [sb(f"{pf}rt{i}", [128, 4, 160], F32) for i in range(2)])
                junk = sb(pf + "junk", [128, D], BF16)
                b_junk = Buf()
                hR = Ring(S, [sb(f"{pf}h{i}", [128, D], BF16) for i in range(2)], dma=False)
                hTR = Ring(S, [sb(f"{pf}hT{i}", [128, 16, 128], BF16) for i in range(2)], dma=False)
                stR = Ring(S, [sb(f"{pf}st{i}", [128, 8], F32) for i in range(3)], dma=False)
                t1 = sb(pf + "t1", [128, 8, 64], F32)
                t2 = sb(pf + "t2", [128, 8, 64], F32)
                b_t = Buf()
                tmA = sb(pf + "tmA", [128, 1024], BF16)
                lat = sb(pf + "lat", [128, 512], BF16)
                latT = sb(pf + "latT", [128, 4, 128], BF16)
                tmN = sb(pf + "tmN", [128, 8, 128], BF16)
                tmR = sb(pf + "tmR", [128, 8, 64], BF16)
                b_tmA, b_lat, b_latT, b_tmN, b_tmR = Buf(), Buf(), Buf(), Buf(), Buf()
                stA = sb(pf + "stA", [128, 8, 512], BF16)
                stN = sb(pf + "stN", [128, 8, 512], BF16)
                stRr = sb(pf + "stR", [128, 4, 512], BF16)
                b_stA, b_stN, b_stR = Buf(), Buf(), Buf()
                d_stA, d_stN, d_stR = S.dsem(), S.dsem(), S.dsem()
                if isK:
                    stV = sb("kstV", [128, 8, 4, 132], BF16)
                    stVm = sb("kstVm", [128, 8, 4, 132], BF16)
                    b_stV, b_stVm = Buf(), Buf()
                    S.op("dve", lambda e: e.memset(stV[:, :, :, 128:132], 1.0), (), [b_stV], cowrite=True)
                    S.op("dve", lambda e: e.memset(stVm[:, :, :, 128:132], 1.0), (), [b_stVm], cowrite=True)
                    d_stV, d_stVm = S.dsem(), S.dsem()
                tp = ps(pf + "tp", [128, 16, 128], BF16)
                pj = ps(pf + "pj", [128, 2, 512], F32)
                tk = ps(pf + "tk", [128, 2, 8, 128], BF16)
                tcp = ps(pf + "tc", [128, 4, 128], BF16)
                b_tp, b_tc = Buf(), Buf()
                pjR = Ring(S, [pj[:, i, :] for i in range(2)], dma=False)
                tkR = Ring(S, [tk[:, i, :, :] for i in range(2)], dma=False)

                ngrp = nt // 4
                for g in range(ngrp):
                    rtt_, b_rtt, d_rtt = rtR.next()
                    DMA("sp", rtt_[:], rt[:, t_off + g * 4:t_off + g * 4 + 4, :], [B_rt], [b_rtt], dsem=d_rtt)
                    for j in range(4):
                        ti = g * 4 + j
                        x_, b_x, d_x = xR.next()
                        DMA("sp", x_[:], xin[ti * 128:(ti + 1) * 128, :], (), [b_x], dsem=d_x)
                        st_, b_st, _ = stR.next()
                        ACT(junk[:], x_[:], AF.Square, [b_x], [b_junk, b_st], accum_out=st_[:, 0:1])
                        rstd_from_ss(st_[:, 0:1], 1, D, b_st, st_[:, 1:2], st_[:, 2:3], b_st)
                        h_, b_h, _ = hR.next()
                        STT(h_[:], x_[:], st_[:, 2:3], gat[:], ALU.mult, ALU.mult, [b_x, b_st, b_g], [b_h])
                        for kc in range(16):
                            TR(tp[:, kc, :], h_[:, kc * 128:(kc + 1) * 128], [b_h], [b_tp], cowrite=(kc > 0))
                        hT_, b_hT, _ = hTR.next()
                        CP("dve", hT_[:, 0:8, :], tp[:, 0:8, :], [b_tp], [b_hT])
                        CP("act", hT_[:, 8:16, :], tp[:, 8:16, :], [b_tp], [b_hT], cowrite=True)
                        tab = rtt_[:, j, :]
                        if DBG == 1:
                            continue

                        def proj_block(c0, ncol):
                            p_, b_p, _ = pjR.next()
                            for kc in range(16):
                                MM(p_[:, 0:ncol], hT_[:, kc, :], w[:, kc, c0:c0 + ncol], kc == 0, kc == 15,
                                   [b_hT, b_w], [b_p], cowrite=(kc > 0))
                            return p_, b_p

                        for blk in range(2):
                            p_, b_p = proj_block(blk * 512, 512)
                            if DBG == 5:
                                continue
                            rope(p_.rearrange("p (c w) -> p c w", w=64), 8, 64, 16, tab, b_rtt, b_p, t1, t2, b_t,
                                 tmA[:, blk * 512:(blk + 1) * 512].rearrange("p (c w) -> p c w", w=64), b_tmA)
                        if DBG == 5 or DBG == 6:
                            continue
                        tk_, b_tk, _ = tkR.next()
                        if DBG != 8:
                            for hh in range(8):
                                TR(tk_[:, hh, :], tmA[:, hh * 128:(hh + 1) * 128], [b_tmA], [b_tk], cowrite=(hh > 0))
                        if DBG != 7:
                            CP("dve", stA[:, :, j * 128:(j + 1) * 128], tk_, [b_tk], [b_stA], cowrite=True)
                        if DBG in (2, 7, 8):
                            continue
                        if isK:
                            for blk in range(2):
                                p_, b_p = proj_block(1024 + blk * 512, 512)
                                CP("act", stV[:, blk * 4:(blk + 1) * 4, j, 0:128], p_.rearrange("p (h d) -> p h d", d=128),
                                   [b_p], [b_stV], cowrite=True)
                        if DBG == 3:
                            continue
                        lat0 = 2048 if isK else 1024
                        p_, b_p = proj_block(lat0, 512)
                        st2, b_st2, _ = stR.next()
                        ACT(junk[:, 0:512], p_, AF.Square, [b_p], [b_junk, b_st2], accum_out=st2[:, 0:1])
                        rstd_from_ss(st2[:, 0:1], 1, 512, b_st2, st2[:, 1:2], st2[:, 2:3], b_st2)
                        STT(lat[:], p_, st2[:, 2:3], gl[:], ALU.mult, ALU.mult, [b_p, b_st2, b_g], [b_lat])
                        if DBG == 9:
                            continue
                        for kc in range(4):
                            TR(tcp[:, kc, :], lat[:, kc * 128:(kc + 1) * 128], [b_lat], [b_tc], cowrite=(kc > 0))
                        CP("act", latT[:], tcp[:], [b_tc], [b_latT])
                        if DBG == 10:
                            continue
                        bw = 512 if isK else 384
                        for cb in range(4):
                            p_, b_p, _ = pjR.next()
                            for kc in range(4):
                                MM(p_[:, 0:bw], latT[:, kc, :], wb2[:, kc, cb * bw:(cb + 1) * bw], kc == 0, kc == 3,
                                   [b_latT, b_wb2], [b_p], cowrite=(kc > 0))
                            if DBG == 11:
                                continue
                            if isK:
                                pv = p_.rearrange("p (h t d) -> p h t d", t=2, d=128)
                                ceng = "dve" if cb % 2 == 0 else "act"
                                for h2 in range(2):
                                    CP(ceng, tmN[:, 2 * cb + h2, :], p_[:, h2 * 256:h2 * 256 + 128], [b_p], [b_tmN], cowrite=True)
                                for h2 in range(2):
                                    CP(ceng, stVm[:, 2 * cb + h2, j, 0:128], p_[:, h2 * 256 + 128:h2 * 256 + 256], [b_p], [b_stVm], cowrite=True)
                            else:
                                pv = p_[:, 0:384].rearrange("p (h d) -> p h d", d=192)
                                for h2 in range(2):
                                    CP("dve", tmN[:, 2 * cb + h2, :], p_[:, h2 * 192:h2 * 192 + 128], [b_p], [b_tmN], cowrite=True)
                                rope(pv[:, :, 128:192], 2, 64, 64, tab, b_rtt, b_p, t1, t2, b_t,
                                     tmR[:, 2 * cb:2 * cb + 2, :], b_tmR)
                        if DBG in (11, 12):
                            continue
                        tk_, b_tk, _ = tkR.next()
                        for hh in range(8):
                            TR(tk_[:, hh, :], tmN[:, hh, :], [b_tmN], [b_tk], cowrite=(hh > 0))
                        CP("dve", stN[:, :, j * 128:(j + 1) * 128], tk_, [b_tk], [b_stN], cowrite=True)
                        if DBG == 4:
                            continue
                        if isK:
                            p_, b_p = proj_block(2560, 64)
                            p3 = p_[:, 0:64].unsqueeze(1)
                            rope(p3, 1, 64, 64, tab, b_rtt, b_p, t1, t2, b_t, tmR[:, 0:1, :], b_tmR)
                            CP("dve", tmR[:, 1:2, :], tmR[:, 0:1, :], [b_tmR], [b_tmR], cowrite=True)
                            tk_, b_tk, _ = tkR.next()
                            TR(tk_[:, 0, :], tmR[:, 0:2, :].rearrange("p a d -> p (a d)"), [b_tmR], [b_tk], cowrite=False)
                            CP("act", stRr[:, 0, j * 128:(j + 1) * 128], tk_[:, 0, :], [b_tk], [b_stR], cowrite=True)
                        else:
                            tk_, b_tk, _ = tkR.next()
                            for pr in range(4):
                                TR(tk_[:, pr, :], tmR[:, 2 * pr:2 * pr + 2, :].rearrange("p a d -> p (a d)"), [b_tmR], [b_tk],
                                   cowrite=(pr > 0))
                            CP("act", stRr[:, :, j * 128:(j + 1) * 128], tk_[:, 0:4, :], [b_tk], [b_stR], cowrite=True)
                    c0 = g * 512
                    if isK:
                        DMAS("sp", KTda[:, :, c0:c0 + 512].rearrange("h p t -> p h t"), stA[:], 2, [b_stA], [B_kv], dsem=d_stA)
                        DMAS("sp", KTn[:, :, c0:c0 + 512].rearrange("h p t -> p h t"), stN[:], 2, [b_stN], [B_kv], dsem=d_stN)
                        DMA("sp", KTr[:, c0:c0 + 512], stRr[:, 0, :], [b_stR], [B_kv], dsem=d_stR, cowrite=True)
                        DMAS("sp", Vda[:, :, g * 4:g * 4 + 4, :].rearrange("h p t d -> p h t d"), stV[:], 2, [b_stV], [B_kv], dsem=d_stV)
                        DMAS("sp", Vm[:, :, g * 4:g * 4 + 4, :].rearrange("h p t d -> p h t d"), stVm[:], 2, [b_stVm], [B_kv], dsem=d_stVm)
                    else:
                        DMAS("sp", QTda[:, :, c0:c0 + 512].rearrange("h p t -> p h t"), stA[:], 2, [b_stA], [B_q], dsem=d_stA)
                        DMAS("sp", QTn[:, :, c0:c0 + 512].rearrange("h p t -> p h t"), stN[:], 2, [b_stN], [B_q], dsem=d_stN)
                        DMA("sp", QTr[:, :, c0:c0 + 512].rearrange("h p t -> p h t"), stRr[:], [b_stR], [B_q], dsem=d_stR, cowrite=True)
                S.run_block(final=(stop == (2 if isK else 3)))
                for r in (xR, rtR):
                    r.free()
                for d_ in [dw, dwb, dwg, d_stA, d_stN, d_stR] + ([d_stV, d_stVm] if isK else []):
                    S.dsem_free(d_)

        proj_phase(True)
        if stop == 2:
            return nc
        proj_phase(False)
        if stop == 3:
            return nc

        with contextlib.ExitStack() as es:
            def sb(name, shape, dt):
                return es.enter_context(nc.sbuf_tensor(name, shape, dt))

            def ps(name, shape, dt):
                return es.enter_context(nc.psum_tensor(name, shape, dt))
            NKT = KC // 128
            ktR = Ring(S, [sb(f"a_kt{i}", [128, KC], BF16) for i in range(3)])
            krR = Ring(S, [sb(f"a_kr{i}", [128, KC], BF16) for i in range(2)])
            v_tiles = [sb(f"a_v{i}", [128, NKT, 132], BF16) for i in range(3)]
            vR = Ring(S, v_tiles)
            qtR = Ring(S, [sb(f"a_qt{i}", [128, 512], BF16) for i in range(5)])
            qrR = Ring(S, [sb(f"a_qr{i}", [128, 512], BF16) for i in range(2)])
            pTR = Ring(S, [sb(f"a_pT{i}", [128, 512], BF16) for i in range(4)], dma=False)
            o1 = sb("o1", [128, 4, 128], F32)
            o2 = sb("o2", [128, 4, 128], F32)
            o3 = sb("o3", [128, 4, 128], F32)
            osq = sb("osq", [128, 4, 128], F32)
            onb = sb("onb", [128, 4, 128], BF16)
            fst = sb("fst", [128, 16], F32)
            gsub = sb("gsub", [128, 128], F32)
            b_o1, b_o2, b_o3, b_osq, b_onb, b_fst, b_gsub = Buf(), Buf(), Buf(), Buf(), Buf(), Buf(), Buf()
            oT = sb("oT", [128, 16, 512], BF16)
            b_oT = Buf()
            x1 = sb("x1", [128, 4, D], F32)
            b_x1 = [Buf() for _ in range(4)]
            d_x1 = [S.dsem() for _ in range(4)]
            d_y = [S.dsem() for _ in range(4)]
            hT = sb("hT", [128, 16, 512], BF16)
            b_hT = Buf()
            hb = sb("hb", [128, D], BF16)
            b_hb = Buf()
            big = sb("big", [128, 22, 512], BF16)
            b_big = [Buf() for _ in range(22)]
            gbuf = sb("gbuf", [128, D], F32)
            b_gbuf = Buf()
            d_g = S.dsem()
            junk = sb("ajunk", [128, D], BF16)
            b_junk = Buf()
            pst = sb("pst", [128, 16], F32)
            b_pst = Buf()
            sg1 = sb("sg1", [128, 512], F32)
            sg2 = sb("sg2", [128, 512], F32)
            b_sg1, b_sg2 = Buf(), Buf()
            wR = Ring(S, [sb(f"wbuf{i}", [128, 8192], BF16) for i in range(3)])
            dg0 = S.dsem()
            DMA("sp", gsub[:], g_sub_d.partition_broadcast(128), (), [b_gsub], dsem=dg0)
            TS("dve", gsub[:], gsub[:], 1.0 - LAMBDA_INIT, ALU.mult, [b_gsub], [b_gsub], cowrite=True)

            stp = ps("stp", [128, 3, 512], F32)
            acc = ps("acc", [128, 4, 512], F32)
            tpo = ps("tpo", [128, 4, 128], BF16)
            stpR = Ring(S, [stp[:, i, :] for i in range(3)], dma=False)
            b_acc, b_tpo = Buf(), Buf()
            accB = [Buf() for _ in range(4)]

            SC_DA = 64 ** -0.5
            SC_MLA = 192 ** -0.5

            def load_w(view_fn):
                t_, b_, d_ = wR.next()
                view_fn(t_, b_, d_)
                return t_, b_

            for g in range(NG):
                prompt = g < NPo // 512
                k0 = 0 if prompt else SP
                klen = SP if prompt else SS
                q0 = g * 512
                nch = klen // KC
                steps = []
                for u in range(16):
                    for m in range(2 if u < 8 else 1):
                        for c in range(nch):
                            for t in range(NKT):
                                steps.append((u, m, c, t))
                cur = {"u": None, "chunk": None}
                rec = {}

                def emit_qk(i):
                    u, m, c, t = steps[i]
                    da = u < 8
                    h = u % 8
                    if cur["u"] != u:
                        cur["u"] = u
                        if da:
                            qs = []
                            for mm_ in range(2):
                                qt_, b_qt, d_qt = qtR.next()
                                DMA("sp", qt_[:], QTda[h, :, q0:q0 + 512], [B_q], [b_qt], dsem=d_qt)
                                oh = slice(64, 128) if mm_ == 0 else slice(0, 64)
                                S.op("pool", lambda e, qt_=qt_, oh=oh: e.memset(qt_[oh, :], 0.0), [b_qt], [b_qt], cowrite=True)
                                qs.append((qt_, b_qt))
                            cur["q"] = (qs, None, None, None)
                        else:
                            qt_, b_qt, d_qt = qtR.next()
                            DMA("sp", qt_[:], QTn[h, :, q0:q0 + 512], [B_q], [b_qt], dsem=d_qt)
                            qr_, b_qr, d_qr = qrR.next()
                            DMA("sp", qr_[:], QTr[h // 2, :, q0:q0 + 512], [B_q], [b_qr], dsem=d_qr)
                            oh = slice(64, 128) if h % 2 == 0 else slice(0, 64)
                            S.op("pool", lambda e, qr_=qr_, oh=oh: e.memset(qr_[oh, :], 0.0), [b_qr], [b_qr], cowrite=True)
                            cur["q"] = (qt_, b_qt, qr_, b_qr)
                    qt_, b_qt, qr_, b_qr = cur["q"]
                    if cur["chunk"] != (u, m, c):
                        cur["chunk"] = (u, m, c)
                        ks = k0 + c * KC
                        kt_, b_kt, d_kt = ktR.next()
                        v_, b_v, d_v = vR.next()
                        if da:
                            DMA("sp", kt_[:], KTda[h, :, ks:ks + KC], [B_kv], [b_kt], dsem=d_kt)
                            DMA("sp", v_[:], Vda[h, :, ks // 128:ks // 128 + NKT, :], [B_kv], [b_v], dsem=d_v)
                            cur["kv"] = (kt_, b_kt, None, None, v_, b_v)
                        else:
                            DMA("sp", kt_[:], KTn[h, :, ks:ks + KC], [B_kv], [b_kt], dsem=d_kt)
                            kr_, b_kr, d_kr = krR.next()
                            DMA("sp", kr_[:], KTr[:, ks:ks + KC], [B_kv], [b_kr], dsem=d_kr)
                            DMA("sp", v_[:], Vm[h, :, ks // 128:ks // 128 + NKT, :], [B_kv], [b_v], dsem=d_v)
                            cur["kv"] = (kt_, b_kt, kr_, b_kr, v_, b_v)
                    kt_, b_kt, kr_, b_kr, v_, b_v = cur["kv"]
                    s_, b_s, _ = stpR.next()
                    ksl = slice(t * 128, (t + 1) * 128)
                    if da:
                        qt_, b_qt = qt_[m]
                        MM(s_, kt_[:, ksl], qt_[:], True, True, [b_kt, b_qt], [b_s], cowrite=False)
                    else:
                        MM(s_, kt_[:, ksl], qt_[:], True, False, [b_kt, b_qt], [b_s], cowrite=False)
                        MM(s_, kr_[:, ksl], qr_[:], False, True, [b_kr, b_qr], [b_s], cowrite=True)
                    rec[i] = (s_, b_s, v_, b_v)

                def emit_exp_pv(i):
                    u, m, c, t = steps[i]
                    da = u < 8
                    s_, b_s, v_, b_v = rec.pop(i)
                    first = (c == 0 and t == 0)
                    last = (c == nch - 1 and t == NKT - 1)
                    p_, b_p, _ = pTR.next()
                    ACT(p_[:], s_, AF.Exp, [b_s], [b_p], scale=(SC_DA if da else SC_MLA))
                    for s4 in range(4):
                        MM(acc[:, s4, 0:129], p_[:, s4 * 128:(s4 + 1) * 128], v_[:, t, 0:129], first, last,
                           [b_p, b_v], [accB[s4]], cowrite=not first)
                    return last

                def finalize(u, m):
                    da = u < 8
                    rl = fst[:, 0:4]
                    S.op("dve", lambda e, rl=rl: e.reciprocal(out=rl.unsqueeze(2), in_=acc[:, :, 128:129]), accB, [b_fst])
                    if da and m == 0:
                        for s4 in range(4):
                            TS("dve", o1[:, s4, :], acc[:, s4, 0:128], rl[:, s4:s4 + 1], ALU.mult, accB + [b_fst], [b_o1], cowrite=True)
                        return
                    if da:
                        for s4 in range(4):
                            TS("dve", o2[:, s4, :], acc[:, s4, 0:128], rl[:, s4:s4 + 1], ALU.mult, accB + [b_fst], [b_o2], cowrite=True)
                        STT(o3[:], o2[:], sc[:, 1:2], o1[:], ALU.mult, ALU.add, [b_o2, b_o1, B_sc], [b_o3])
                        TT("dve", osq[:], o3[:], o3[:], ALU.mult, [b_o3], [b_osq])
                        S.op("dve", lambda e: e.tensor_reduce(out=fst[:, 4:8], in_=osq[:], axis=AX.X, op=ALU.add), [b_osq], [b_fst], cowrite=True)
                        rstd_from_ss(fst[:, 4:8], 4, 128, b_fst, fst[:, 8:12], fst[:, 12:16], b_fst)
                        for s4 in range(4):
                            STT(onb[:, s4, :], o3[:, s4, :], fst[:, 12 + s4:13 + s4], gsub[:], ALU.mult, ALU.mult,
                                [b_o3, b_fst, b_gsub], [b_onb], cowrite=True)
                    else:
                        for s4 in range(4):
                            TS("dve", onb[:, s4, :], acc[:, s4, 0:128], rl[:, s4:s4 + 1], ALU.mult, accB + [b_fst], [b_onb], cowrite=True)
                    for s4 in range(4):
                        TR(tpo[:, s4, :], onb[:, s4, :], [b_onb], [b_tpo], cowrite=(s4 > 0))
                    CP("dve", oT[:, u, :], tpo[:].rearrange("p a b -> p (a b)"), [b_tpo], [b_oT], cowrite=True)

                LOOK = 2
                nst = len(steps)
                for i in range(min(LOOK, nst)):
                    emit_qk(i)
                for i in range(nst):
                    if i + LOOK < nst:
                        emit_qk(i + LOOK)
                    if emit_exp_pv(i):
                        finalize(steps[i][0], steps[i][1])

                DMA("sp", gbuf[:], g_attn_d.partition_broadcast(128), (), [b_gbuf], dsem=d_g)
                for j in range(4):
                    r0 = q0 + j * 128
                    DMA("sp", x1[:, j, :], xq[r0:r0 + 128, :], (), [b_x1[j]], dsem=d_x1[j])

                def norm_T(j, dstT, b_dstT, first):
                    ACT(junk[:], x1[:, j, :], AF.Square, [b_x1[j]], [b_junk, b_pst], accum_out=pst[:, 0:1])
                    rstd_from_ss(pst[:, 0:1], 1, D, b_pst, pst[:, 1:2], pst[:, 2:3], b_pst)
                    STT(hb[:], x1[:, j, :], pst[:, 2:3], gbuf[:], ALU.mult, ALU.mult, [b_x1[j], b_pst, b_gbuf], [b_hb])
                    for q4 in range(4):
                        for kk in range(4):
                            kc = q4 * 4 + kk
                            TR(tpo[:, kk, :], hb[:, kc * 128:(kc + 1) * 128], [b_hb], [b_tpo], cowrite=(kk > 0))
                        CP("dve" if q4 % 2 == 0 else "act", dstT[:, q4 * 4:q4 * 4 + 4, j * 128:(j + 1) * 128], tpo[:], [b_tpo], [b_dstT],
                           cowrite=not (first and q4 == 0))
                for j in range(4):
                    norm_T(j, hT, b_hT, j == 0)
                win = wb_in.rearrange("(kc p) c -> p kc c", p=128)
                wbd_v = wb_bd.rearrange("(kc p) c -> p kc c", p=128)
                wbm_v = wb_bm.rearrange("(kc p) c -> p kc c", p=128)
                for fc in range(16):
                    wt, b_wt, d_wt = wR.next()
                    wv = wt[:, 0:48 * 128].rearrange("p (k c) -> p k c", c=128)
                    DMA("sp", wv[:, 0:4, :], win[:, 0:4, 4160 + fc * 128:4160 + (fc + 1) * 128], [wbufs["in"]], [b_wt], dsem=d_wt)
                    DMAS("sp", wv[:, 4:16, :], win[:, 4:16, 4160 + fc * 128:4160 + (fc + 1) * 128], 3, [wbufs["in"]], [b_wt], dsem=d_wt)
                    DMAS("sp", wv[:, 16:32, :], win[:, :, 4160 + D + fc * 128:4160 + D + (fc + 1) * 128], 4, [wbufs["in"]], [b_wt], dsem=d_wt)
                    DMAS("sp", wv[:, 32:40, :], wbd_v[:, :, fc * 128:(fc + 1) * 128], 2, [wbufs["bd"]], [b_wt], dsem=d_wt)
                    DMAS("sp", wv[:, 40:48, :], wbm_v[:, :, fc * 128:(fc + 1) * 128], 2, [wbufs["bm"]], [b_wt], dsem=d_wt)
                    for kc in range(16):
                        MM(acc[:, 0, :], wv[:, kc, :], hT[:, kc, :], kc == 0, kc == 15, [b_wt, b_hT], [accB[0]], cowrite=(kc > 0))
                    for kc in range(16):
                        MM(acc[:, 1, :], wv[:, 16 + kc, :], hT[:, kc, :], kc == 0, kc == 15, [b_wt, b_hT], [accB[1]], cowrite=(kc > 0))
                    for kc in range(8):
                        MM(acc[:, 2, :], wv[:, 32 + kc, :], oT[:, kc, :], kc == 0, kc == 7, [b_wt, b_oT], [accB[2]], cowrite=(kc > 0))
                    for kc in range(8):
                        MM(acc[:, 3, :], wv[:, 40 + kc, :], oT[:, 8 + kc, :], kc == 0, kc == 7, [b_wt, b_oT], [accB[3]], cowrite=(kc > 0))
                    ACT(sg1[:], acc[:, 0, :], AF.Sigmoid, [accB[0]], [b_sg1])
                    ACT(sg2[:], acc[:, 1, :], AF.Sigmoid, [accB[1]], [b_sg2])
                    TT("dve", sg1[:], sg1[:], acc[:, 2, :], ALU.mult, [b_sg1, accB[2]], [b_sg1])
                    TT("dve", sg2[:], sg2[:], acc[:, 3, :], ALU.mult, [b_sg2, accB[3]], [b_sg2])
                    TT("dve", big[:, fc, :], sg1[:], sg2[:], ALU.add, [b_sg1, b_sg2], [b_big[fc]])
                wout_v = wb_out.rearrange("(kc p) c -> p kc c", p=128)
                for cb in range(4):
                    wt, b_wt, d_wt = wR.next()
                    wv = wt[:].rearrange("p (k c) -> p k c", c=512)
                    DMA("sp", wv[:, 0:4, :], wout_v[:, 0:4, cb * 512:(cb + 1) * 512], [wbufs["out"]], [b_wt], dsem=d_wt)
                    DMAS("sp", wv[:, 4:16, :], wout_v[:, 4:16, cb * 512:(cb + 1) * 512], 3, [wbufs["out"]], [b_wt], dsem=d_wt)
                    for j in range(4):
                        for kc in range(16):
                            MM(acc[:, j, :], big[:, kc, j * 128:(j + 1) * 128], wv[:, kc, :], kc == 0, kc == 15, [b_wt, b_big[kc]], [accB[j]],
                               cowrite=(kc > 0))
                        xs = x1[:, j, cb * 512:(cb + 1) * 512]
                        TT("dve", xs, xs, acc[:, j, :], ALU.add, [b_x1[j], accB[j]], [b_x1[j]], cowrite=True)
                DMA("sp", gbuf[:], g_ffn_d.partition_broadcast(128), (), [b_gbuf], dsem=d_g)
                for j in range(4):
                    norm_T(j, hT, b_hT, j == 0)
                wg_v = wb_gate.rearrange("(kc p) c -> p kc c", p=128)
                wu_v = wb_up.rearrange("(kc p) c -> p kc c", p=128)
                wd_v = wb_down.rearrange("(hc p) c -> p hc c", p=128)
                for half in range(2):
                    for hp2 in range(11):
                        hc0 = half * 22 + hp2 * 2
                        wt, b_wt, d_wt = wR.next()
                        wv = wt[:].rearrange("p (s k c) -> p s k c", s=2, c=256)
                        DMA("sp", wv[:, 0, 0:4, :], wg_v[:, 0:4, hc0 * 128:(hc0 + 2) * 128], [wbufs["gate"]], [b_wt], dsem=d_wt)
                        DMAS("sp", wv[:, 0, 4:16, :], wg_v[:, 4:16, hc0 * 128:(hc0 + 2) * 128], 3, [wbufs["gate"]], [b_wt], dsem=d_wt)
                        DMAS("sp", wv[:, 1, :, :], wu_v[:, :, hc0 * 128:(hc0 + 2) * 128], 4, [wbufs["up"]], [b_wt], dsem=d_wt)
                        for i2 in range(2):
                            ga, ua = (0, 1) if i2 == 0 else (2, 3)
                            for kc in range(16):
                                MM(acc[:, ga, :], wv[:, 0, kc, i2 * 128:(i2 + 1) * 128], hT[:, kc, :], kc == 0, kc == 15, [b_wt, b_hT], [accB[ga]],
                                   cowrite=(kc > 0))
                            for kc in range(16):
                                MM(acc[:, ua, :], wv[:, 1, kc, i2 * 128:(i2 + 1) * 128], hT[:, kc, :], kc == 0, kc == 15, [b_wt, b_hT], [accB[ua]],
                                   cowrite=(kc > 0))
                            sg, b_sg = (sg1, b_sg1) if i2 == 0 else (sg2, b_sg2)
                            ACT(sg[:], acc[:, ga, :], AF.Silu, [accB[ga]], [b_sg])
                            ci = hp2 * 2 + i2
                            TT("dve", big[:, ci, :], sg[:], acc[:, ua, :], ALU.mult, [b_sg, accB[ua]], [b_big[ci]])
                    for cb in range(4):
                        wts = []
                        for part in range(2):
                            wt, b_wt, d_wt = wR.next()
                            wv = wt[:, 0:11 * 512].rearrange("p (k c) -> p k c", c=512)
                            hc0 = half * 22 + part * 11
                            DMA("sp", wv[:, 0:4, :], wd_v[:, hc0:hc0 + 4, cb * 512:(cb + 1) * 512], [wbufs["down"]], [b_wt], dsem=d_wt)
                            DMAS("sp", wv[:, 4:11, :], wd_v[:, hc0 + 4:hc0 + 11, cb * 512:(cb + 1) * 512], 2, [wbufs["down"]], [b_wt], dsem=d_wt)
                            wts.append((wv, b_wt))
                        for j in range(4):
                            for ci in range(22):
                                wv, b_wt = wts[ci // 11]
                                MM(acc[:, j, :], big[:, ci, j * 128:(j + 1) * 128], wv[:, ci % 11, :], ci == 0, ci == 21, [b_wt, b_big[ci]], [accB[j]],
                                   cowrite=(ci > 0))
                            xs = x1[:, j, cb * 512:(cb + 1) * 512]
                            TT("dve", xs, xs, acc[:, j, :], ALU.add, [b_x1[j], accB[j]], [b_x1[j]], cowrite=True)
                DMA("sp", gbuf[:], g_fin_d.partition_broadcast(128), (), [b_gbuf], dsem=d_g)
                for j in range(4):
                    ACT(junk[:], x1[:, j, :], AF.Square, [b_x1[j]], [b_junk, b_pst], accum_out=pst[:, 0:1])
                    rstd_from_ss(pst[:, 0:1], 1, D, b_pst, pst[:, 1:2], pst[:, 2:3], b_pst)
                    STT(x1[:, j, :], x1[:, j, :], pst[:, 2:3], gbuf[:], ALU.mult, ALU.mult, [b_x1[j], b_pst, b_gbuf], [b_x1[j]], cowrite=True)
                    r0 = q0 + j * 128
                    DMA("sp", y[r0:r0 + 128, :], x1[:, j, :], [b_x1[j]], (), dsem=d_y[j])
            S.run_block(final=True)
    return nc


_NC_CACHE = {}
PARAM_NAMES = ["attn_norm_g", "w_in", "da_lambda_q1", "da_lambda_k1", "da_lambda_q2", "da_lambda_k2", "da_subln_g",
               "mla_q_norm_g", "mla_w_q_b", "mla_kv_norm_g", "mla_w_kv_b", "w_branch_da", "w_branch_mla", "w_out",
               "ffn_norm_g", "w_gate", "w_up", "w_down"]


STOP = 9
NCORES = 8
DBG = 0


def kernel(x_prompt, x_sample, final_norm_g, **params):
    xp = np.asarray(x_prompt, dtype=np.float32)[0]
    xs = np.asarray(x_sample, dtype=np.float32)[0]
    SP, SS = xp.shape[0], xs.shape[0]
    NPo, NSo = SP // 8, SS // 8
    key = (NPo, NSo)
    if key not in _NC_CACHE:
        _NC_CACHE[key] = build_nc(NPo, NSo, STOP)
    nc = _NC_CACHE[key]
    xall = np.ascontiguousarray(np.concatenate([xp, xs], axis=0))
    shared = {"xall": xall, "ident": np.eye(128, dtype=np.float32)}
    for n in PARAM_NAMES:
        a = np.asarray(params[n], dtype=np.float32)
        a = a[0]
        if a.ndim == 1:
            a = a[None, :]
        shared[n] = np.ascontiguousarray(a)
    shared["final_norm_g"] = np.ascontiguousarray(np.asarray(final_norm_g, dtype=np.float32)[None, :])
    pk = np.concatenate([np.arange(SP), np.arange(SS)]).astype(np.float32)
    shared["posk"] = np.ascontiguousarray(pk.reshape(-1, 128).T)
    in_maps = []
    for c in range(8):
        m = dict(shared)
        m["xq"] = np.ascontiguousarray(np.concatenate([xp[c * NPo:(c + 1) * NPo], xs[c * NSo:(c + 1) * NSo]], axis=0))
        pq = np.concatenate([np.arange(c * NPo, (c + 1) * NPo), np.arange(c * NSo, (c + 1) * NSo)]).astype(np.float32)
        m["posq"] = np.ascontiguousarray(pq.reshape(-1, 128).T)
        in_maps.append(m)
    res = run_bass_kernel_spmd(nc, in_maps[:NCORES], core_ids=list(range(NCORES)))
    rr = [res.results[c]["y"] if c < NCORES else np.zeros((NPo + NSo, D), np.float32) for c in range(8)]
    yp = np.concatenate([rr[c][:NPo] for c in range(8)], axis=0)[None]
    ys = np.concatenate([rr[c][NPo:] for c in range(8)], axis=0)[None]
    return (yp.astype(np.float32), ys.astype(np.float32))
```

```python
import contextlib
import math
import numpy as np
import concourse.bass as bass
import concourse.mybir as mybir
from concourse.bass_utils import run_bass_kernel_spmd

F32 = mybir.dt.float32
BF16 = mybir.dt.bfloat16
I32 = mybir.dt.int32
AF = mybir.ActivationFunctionType
ALU = mybir.AluOpType
AX = mybir.AxisListType

D = 2048
NH = 8
FF = 5632
INC = 8256
EPS = 1e-6
THETA = 500000.0
LAMBDA_INIT = 0.8 - 0.6 * math.exp(0.0)


class LSem:
    def __init__(self, S, name, step):
        self.S, self.name, self.step = S, name, step
        self.epoch = 28000 // step
        self.n = 0
        self.phys = []
        self.persist = False
        self.base = 0
        self.rank = {}

    def target_v(self, v):
        e = (v - 1) // self.epoch
        return self._phys(e), ((v - 1) % self.epoch + 1) * self.step

    def _phys(self, e):
        while len(self.phys) <= e:
            self.phys.append(self.S.nc.alloc_semaphore(name=f"{self.name}_{len(self.phys)}"))
        return self.phys[e]

    def next(self):
        self.n += 1
        return self.n

    def inc_target(self, n):
        return self._phys((n - 1) // self.epoch), self.step

    def wait_target(self, n):
        e = (n - 1) // self.epoch
        return self._phys(e), ((n - 1) % self.epoch + 1) * self.step


class Buf:
    def __init__(self, name=""):
        self.name = name
        self.writers = {}
        self.readers = {}


ENGS = ("pe", "act", "dve", "pool", "sp")


class Sched:
    def __init__(self, nc):
        self.nc = nc
        self.es = {e: LSem(self, "e_" + e, 1) for e in ("pe", "act", "dve", "pool")}
        self.prog = {e: [] for e in ENGS}
        self.seen = {e: {} for e in ENGS}
        self.dma_pool = []
        self.dma_all = []
        self.nops = 0

    def dsem(self, persist=False):
        if self.dma_pool and not persist:
            return self.dma_pool.pop()
        s = LSem(self, f"d{len(self.dma_all)}", 16)
        s.persist = persist
        self.dma_all.append(s)
        return s

    def dsem_free(self, s):
        self.dma_pool.append(s)

    def _wait(self, eng, lsem, n):
        seen = self.seen[eng]
        if seen.get(lsem, 0) >= n:
            return
        seen[lsem] = n
        self.prog[eng].append(("w", lsem, n))

    def op(self, eng, fn, reads=(), writes=(), dsem=None, cowrite=False):
        own = self.es.get(eng) if dsem is None else None
        deps = []
        for b in reads:
            for ls, n in b.writers.items():
                if ls is own and eng == "pe":
                    continue
                deps.append((ls, n))
        for b in writes:
            if not cowrite:
                for ls, n in b.writers.items():
                    if ls is own:
                        continue
                    deps.append((ls, n))
            for ls, n in b.readers.items():
                if ls is own:
                    continue
                deps.append((ls, n))
        for ls, n in deps:
            self._wait(eng, ls, n)
        ls = dsem if dsem is not None else own
        n = ls.next()
        self.prog[eng].append(("o", fn, ls, n))
        self.nops += 1
        for b in reads:
            if b.readers.get(ls, 0) < n:
                b.readers[ls] = n
        for b in writes:
            if cowrite:
                b.writers[ls] = n
            else:
                b.writers = {ls: n}
                b.readers = {}
        return (ls, n)

    def run_block(self, final=False):
        for s in self.dma_all:
            if s.n and (final or not s.persist):
                self._wait("sp", s, s.n)
        for e, s in self.es.items():
            if s.n:
                self._wait("sp", s, s.n)
        nc = self.nc
        prog = self.prog
        comp = set(self.es.values())
        lazy = {self.es["pe"]}
        waited = {ls: set() for ls in comp}
        for e in ENGS:
            for it in prog[e]:
                if it[0] == "w" and it[1] in comp:
                    waited[it[1]].add(it[2])
        for ls in comp:
            if ls in lazy:
                ws = sorted(waited[ls])
            else:
                ws = sorted(it[-1] for e in ENGS for it in prog[e] if it[0] == "o" and it[-2] is ls)
            ls.rank = {n: ls.base + i + 1 for i, n in enumerate(ws)}

        def replay(lst, e):
            for it in lst:
                ls, n = it[-2], it[-1]
                if it[0] == "w":
                    if ls in comp:
                        ph, val = ls.target_v(ls.rank[n])
                    else:
                        ph, val = ls.wait_target(n)
                    e.wait_ge(ph, val)
                else:
                    ins = it[1](e)
                    if ls in comp:
                        if n in ls.rank:
                            ph, _ = ls.target_v(ls.rank[n])
                            ins.then_inc(ph, 1)
                    else:
                        ph, amt = ls.inc_target(n)
                        ins.then_inc(ph, amt)

        with nc.Block() as block:
            @block.sync
            def _(e):
                replay(prog["sp"], e)

            @block.tensor
            def _(e):
                replay(prog["pe"], e)

            @block.scalar
            def _(e):
                replay(prog["act"], e)

            @block.vector
            def _(e):
                replay(prog["dve"], e)

            @block.gpsimd
            def _(e):
                replay(prog["pool"], e)
        self.prog = {e: [] for e in ENGS}
        for ls in comp:
            ls.base += len(ls.rank)
            ls.rank = {}
        for e in ENGS:
            for s in self.dma_all:
                if not s.persist:
                    self.seen[e][s] = s.n
            for s in self.es.values():
                self.seen[e][s] = s.n


class Ring:
    def __init__(self, S, tiles, dma=True):
        self.S = S
        self.slots = [(t, Buf(), S.dsem() if dma else None) for t in tiles]
        self.i = -1

    def next(self):
        self.i = (self.i + 1) % len(self.slots)
        return self.slots[self.i]

    def free(self):
        for t, b, d in self.slots:
            if d is not None:
                self.S.dsem_free(d)


def build_nc(NPo, NSo, stop=9):
    NQ = NPo + NSo
    SP, SS = 8 * NPo, 8 * NSo
    STOT = SP + SS
    NTK = STOT // 128
    NTQ = NQ // 128
    NT = NTK + NTQ
    NG = NQ // 512
    KC = 1024
    nc = bass.Bass("TRN2", target_bir_lowering=False)

    def din(name, shape):
        return nc.dram_tensor(name, shape, F32, kind="ExternalInput").ap()

    def dscr(name, shape, dt=BF16):
        return nc.dram_tensor(name, shape, dt, kind="Internal").ap()

    xall = din("xall", [STOT, D])
    xq = din("xq", [NQ, D])
    posk = din("posk", [128, NTK])
    posq = din("posq", [128, NTQ])
    ident_d = din("ident", [128, 128])
    g_attn_d = din("attn_norm_g", [1, D])
    w_in = din("w_in", [D, INC])
    lam_d = [din(n, [1, 64]) for n in ("da_lambda_q1", "da_lambda_k1", "da_lambda_q2", "da_lambda_k2")]
    g_sub_d = din("da_subln_g", [1, 128])
    g_q_d = din("mla_q_norm_g", [1, 512])
    w_qb = din("mla_w_q_b", [512, 1536])
    g_kv_d = din("mla_kv_norm_g", [1, 512])
    w_kvb = din("mla_w_kv_b", [512, 2048])
    w_bd = din("w_branch_da", [1024, D])
    w_bm = din("w_branch_mla", [1024, D])
    w_out = din("w_out", [D, D])
    g_ffn_d = din("ffn_norm_g", [1, D])
    w_gate = din("w_gate", [D, FF])
    w_up = din("w_up", [D, FF])
    w_down = din("w_down", [FF, D])
    g_fin_d = din("final_norm_g", [1, D])
    y = nc.dram_tensor("y", [NQ, D], F32, kind="ExternalOutput").ap()

    wb_in = dscr("wb_in", [D, INC])
    wb_qb = dscr("wb_qb", [512, 1536])
    wb_kvb = dscr("wb_kvb", [512, 2048])
    wb_bd = dscr("wb_bd", [1024, D])
    wb_bm = dscr("wb_bm", [1024, D])
    wb_out = dscr("wb_out", [D, D])
    wb_gate = dscr("wb_gate", [D, FF])
    wb_up = dscr("wb_up", [D, FF])
    wb_down = dscr("wb_down", [FF, D])
    rt = dscr("rt", [128, NT, 160], F32)
    KTda = dscr("KTda", [NH, 128, STOT])
    KTn = dscr("KTn", [NH, 128, STOT])
    KTr = dscr("KTr", [128, STOT])
    Vda = dscr("Vda", [NH, 128, NTK, 132])
    Vm = dscr("Vm", [NH, 128, NTK, 132])
    QTda = dscr("QTda", [NH, 128, NQ])
    QTn = dscr("QTn", [NH, 128, NQ])
    QTr = dscr("QTr", [4, 128, NQ])

    S = Sched(nc)
    B_rt, B_kv, B_q = Buf("rt"), Buf("kv"), Buf("q")

    def DMA(eng, out, in_, reads=(), writes=(), dsem=None, cowrite=False):
        return S.op(eng, lambda e: e.dma_start(out=out, in_=in_), reads, writes, dsem=dsem, cowrite=cowrite)

    def DMAS(eng, out, in_, n, reads=(), writes=(), dsem=None):
        sz = out.shape[1]
        step = (sz + n - 1) // n
        for a in range(0, sz, step):
            b_ = min(sz, a + step)
            DMA(eng, out[:, a:b_], in_[:, a:b_], reads, writes, dsem=dsem, cowrite=True)

    def MM(out, lhsT, rhs, start, stop, reads, writes, cowrite=True):
        return S.op("pe", lambda e: e.matmul(out, lhsT=lhsT, rhs=rhs, start=start, stop=stop), reads, writes, cowrite=cowrite)

    def TR(out, in_, reads, writes, cowrite=True):
        return S.op("pe", lambda e: e.transpose(out, in_, ident_b[:]), list(reads) + [B_id], writes, cowrite=cowrite)

    def ACT(out, in_, func, reads, writes, scale=1.0, bias=0.0, accum_out=None, cowrite=False):
        return S.op("act", lambda e: e.activation(out=out, in_=in_, func=func, bias=bias, scale=scale, accum_out=accum_out),
                    reads, writes, cowrite=cowrite)

    def TT(eng, out, in0, in1, op, reads, writes, cowrite=False):
        return S.op(eng, lambda e: e.tensor_tensor(out=out, in0=in0, in1=in1, op=op), reads, writes, cowrite=cowrite)

    def TS(eng, out, in0, s1, op0, reads, writes, s2=None, op1=ALU.bypass, cowrite=False):
        return S.op(eng, lambda e: e.tensor_scalar(out=out, in0=in0, scalar1=s1, scalar2=s2, op0=op0, op1=op1), reads, writes, cowrite=cowrite)

    def STT(out, in0, scalar, in1, op0, op1, reads, writes, cowrite=False):
        return S.op("dve", lambda e: e.scalar_tensor_tensor(out=out, in0=in0, scalar=scalar, in1=in1, op0=op0, op1=op1),
                    reads, writes, cowrite=cowrite)

    def CP(eng, out, in_, reads, writes, cowrite=False):
        if eng == "act":
            return S.op("act", lambda e: e.copy(out=out, in_=in_), reads, writes, cowrite=cowrite)
        return S.op(eng, lambda e: e.tensor_copy(out=out, in_=in_), reads, writes, cowrite=cowrite)

    def rstd_from_ss(ss, n, dim, reads_b, tmp, out, out_b):
        ACT(tmp, ss, AF.Ln, [reads_b], [out_b], scale=1.0 / dim, bias=eps_t[:, 0:1], cowrite=True)
        ACT(out, tmp, AF.Exp, [out_b], [out_b], scale=-0.5, cowrite=True)

    with contextlib.ExitStack() as top:
        def tsb(name, shape, dt):
            return top.enter_context(nc.sbuf_tensor(name, shape, dt))
        ident_b = tsb("ident_b", [128, 128], BF16)
        sc = tsb("sc", [128, 4], F32)
        eps_t = tsb("eps_t", [128, 1], F32)
        B_id, B_sc = Buf("id"), Buf("sc")

        wbufs = {}

        def cast_weight(key, dst, src, rows, cols):
            b = Buf(key)
            ds = S.dsem(persist=True)
            wbufs[key] = b
            bw = cols
            while bw > 2048:
                for dv in (2, 3, 4, 5, 6, 7, 8, 11):
                    if cols % dv == 0 and cols // dv <= 2048:
                        bw = cols // dv
                        break
                break
            for r0 in range(0, rows, 128):
                DMA("pool", dst[r0:r0 + 128, :].rearrange("r (a b) -> r a b", b=bw),
                    src[r0:r0 + 128, :].rearrange("r (a b) -> r a b", b=bw), writes=[b], dsem=ds, cowrite=True)

        with contextlib.ExitStack() as es:
            def sb(name, shape, dt):
                return es.enter_context(nc.sbuf_tensor(name, shape, dt))
            cast_weight("in", wb_in, w_in, D, INC)
            cast_weight("kvb", wb_kvb, w_kvb, 512, 2048)
            cast_weight("qb", wb_qb, w_qb, 512, 1536)
            idf = sb("idf", [128, 128], F32)
            lq = sb("lq", [128, 4, 64], F32)
            prod = sb("prod", [128, 2, 64], F32)
            s12 = sb("s12", [128, 2], F32)
            e12 = sb("e12", [128, 2], F32)
            d0, d0b, d0c = S.dsem(), S.dsem(), S.dsem()
            b_idf, b_lq, b_pr, b_s12 = Buf(), Buf(), Buf(), Buf()
            DMA("sp", idf[:], ident_d[:, :], writes=[b_idf], dsem=d0)
            CP("dve", ident_b[:], idf[:], [b_idf], [B_id])
            S.op("dve", lambda e: e.memset(eps_t[:], EPS), (), [B_sc], cowrite=True)
            for i in range(4):
                DMA("sp", lq[:, i, :], lam_d[i].partition_broadcast(128), writes=[b_lq], dsem=d0b, cowrite=True)
            TT("dve", prod[:, 0, :], lq[:, 0, :], lq[:, 1, :], ALU.mult, [b_lq], [b_pr], cowrite=True)
            TT("dve", prod[:, 1, :], lq[:, 2, :], lq[:, 3, :], ALU.mult, [b_lq], [b_pr], cowrite=True)
            S.op("dve", lambda e: e.tensor_reduce(out=s12[:], in_=prod[:], axis=AX.X, op=ALU.add), [b_pr], [b_s12])
            ACT(e12[:], s12[:], AF.Exp, [b_s12], [b_s12], cowrite=True)
            TT("dve", sc[:, 0:1], e12[:, 0:1], e12[:, 1:2], ALU.subtract, [b_s12], [B_sc], cowrite=True)
            TS("dve", sc[:, 1:2], sc[:, 0:1], LAMBDA_INIT, ALU.add, [B_sc], [B_sc], s2=-1.0, op1=ALU.mult, cowrite=True)
            pos = sb("pos", [128, NT], F32)
            b_pos = Buf()
            DMA("sp", pos[:, 0:NTK], posk[:, :], writes=[b_pos], dsem=d0c, cowrite=True)
            DMA("sp", pos[:, NTK:NT], posq[:, :], writes=[b_pos], dsem=d0c, cowrite=True)
            invf = np.concatenate([
                (np.float32(THETA) ** (-np.arange(0, 16, 2, dtype=np.float32) / np.float32(16))).astype(np.float32),
                (np.float32(THETA) ** (-np.arange(0, 64, 2, dtype=np.float32) / np.float32(64))).astype(np.float32)])
            CH = 24
            PI = math.pi
            C1 = 6.28125
            C2 = 2.0 * math.pi - C1
            PIC = 3.1415925
            tA = sb("tA", [128, CH, 40], F32)
            tB = sb("tB", [128, CH, 40], F32)
            tC = sb("tC", [128, CH, 40], F32)
            tD = sb("tD", [128, CH, 40], F32)
            tI = sb("tI", [128, CH, 40], I32)
            rtt = sb("rtt", [128, CH, 160], F32)
            bA, bB, bC, bD, bI, bR = Buf(), Buf(), Buf(), Buf(), Buf(), Buf()
            drt = S.dsem()
            for c0 in range(0, NT, CH):
                n = min(CH, NT - c0)
                A, Bt, C, Dt, It = tA[:, 0:n, :], tB[:, 0:n, :], tC[:, 0:n, :], tD[:, 0:n, :], tI[:, 0:n, :]
                for j in range(40):
                    TS("dve", tA[:, 0:n, j], pos[:, c0:c0 + n], float(invf[j]), ALU.mult, [b_pos], [bA], cowrite=(j > 0))
                TS("dve", It, A, 1.0 / (2.0 * PI), ALU.mult, [bA], [bI])
                CP("dve", C, It, [bI], [bC])
                STT(Dt, C, -C1, A, ALU.mult, ALU.add, [bC, bA], [bD])
                STT(A, C, -C2, Dt, ALU.mult, ALU.add, [bC, bD], [bA])

                def wrap(src, bs, tmp, bt, dst, bd_):
                    TS("dve", tmp, src, PI, ALU.is_gt, [bs], [bt], s2=-2.0 * PI, op1=ALU.mult)
                    TT("dve", dst, src, tmp, ALU.add, [bs, bt], [bd_])
                    TS("dve", tmp, dst, -PI, ALU.is_lt, [bd_], [bt], s2=2.0 * PI, op1=ALU.mult)
                    TT("dve", src, dst, tmp, ALU.add, [bd_, bt], [bs])
                    TS("dve", src, src, -PIC, ALU.max, [bs], [bs], s2=PIC, op1=ALU.min)
                wrap(A, bA, Bt, bB, Dt, bD)
                ACT(C, A, AF.Sin, [bA], [bC])
                TS("dve", Dt, A, PI / 2.0, ALU.add, [bA], [bD])
                wrap(Dt, bD, Bt, bB, A, bA)
                ACT(A, Dt, AF.Sin, [bD], [bA])
                R = rtt[:, 0:n, :]
                CP("dve", rtt[:, 0:n, 0:8], tA[:, 0:n, 0:8], [bA], [bR])
                CP("dve", rtt[:, 0:n, 8:16], tA[:, 0:n, 0:8], [bA], [bR], cowrite=True)
                TS("dve", rtt[:, 0:n, 16:24], tC[:, 0:n, 0:8], -1.0, ALU.mult, [bC], [bR], cowrite=True)
                CP("dve", rtt[:, 0:n, 24:32], tC[:, 0:n, 0:8], [bC], [bR], cowrite=True)
                CP("dve", rtt[:, 0:n, 32:64], tA[:, 0:n, 8:40], [bA], [bR], cowrite=True)
                CP("dve", rtt[:, 0:n, 64:96], tA[:, 0:n, 8:40], [bA], [bR], cowrite=True)
                TS("dve", rtt[:, 0:n, 96:128], tC[:, 0:n, 8:40], -1.0, ALU.mult, [bC], [bR], cowrite=True)
                CP("dve", rtt[:, 0:n, 128:160], tC[:, 0:n, 8:40], [bC], [bR], cowrite=True)
                DMA("sp", rt[:, c0:c0 + n, :], R, [bR], [B_rt], dsem=drt, cowrite=True)
            S.run_block(final=(stop == 1))
            for d_ in (d0, d0b, d0c, drt):
                S.dsem_free(d_)
        if stop == 1:
            return nc

        def rope(src3, C, W, rot, tab, tab_b, src_b, t1, t2, tb, dst3, dst_b):
            o = 0 if rot == 16 else 32
            hh = rot // 2
            cosf = tab[:, o:o + rot]
            s_a = tab[:, o + rot:o + rot + hh]
            s_b = tab[:, o + rot + hh:o + 2 * rot]
            for c in range(C):
                TT("dve", t1[:, c, 0:rot], src3[:, c, 0:rot], cosf, ALU.mult, [src_b, tab_b], [tb], cowrite=True)
                TT("dve", t2[:, c, 0:hh], src3[:, c, hh:rot], s_a, ALU.mult, [src_b, tab_b], [tb], cowrite=True)
                TT("dve", t2[:, c, hh:rot], src3[:, c, 0:hh], s_b, ALU.mult, [src_b, tab_b], [tb], cowrite=True)
            TT("dve", dst3[:, :, 0:rot], t1[:, 0:C, 0:rot], t2[:, 0:C, 0:rot], ALU.add, [tb], [dst_b], cowrite=True)
            if W > rot:
                CP("dve", dst3[:, :, rot:W], src3[:, :, rot:W], [src_b], [dst_b], cowrite=True)

        def proj_phase(isK):
            nt = NTK if isK else NTQ
            xin = xall if isK else xq
            t_off = 0 if isK else NTK
            NA = 2624 if isK else 1536
            NB = 2048 if isK else 1536
            with contextlib.ExitStack() as es:
                def sb(name, shape, dt):
                    return es.enter_context(nc.sbuf_tensor(name, shape, dt))

                def ps(name, shape, dt):
                    return es.enter_context(nc.psum_tensor(name, shape, dt))
                pf = "k" if isK else "q"
                w = sb(pf + "w", [128, 16, NA], BF16)
                wb2 = sb(pf + "wb2", [128, 4, NB], BF16)
                gat = sb(pf + "gat", [128, D], F32)
                gl = sb(pf + "gl", [128, 512], F32)
                b_w, b_wb2, b_g = Buf(), Buf(), Buf()
                dw, dwb, dwg = S.dsem(), S.dsem(), S.dsem()
                win = wb_in.rearrange("(kc p) c -> p kc c", p=128)
                if isK:
                    DMAS("sp", w[:, :, 0:2048], win[:, :, 1024:3072], 16, [wbufs["in"]], [b_w], dsem=dw)
                    DMAS("sp", w[:, :, 2048:2624], win[:, :, 3584:4160], 4, [wbufs["in"]], [b_w], dsem=dw)
                    DMAS("sp", wb2[:], wb_kvb.rearrange("(kc p) c -> p kc c", p=128), 4, [wbufs["kvb"]], [b_wb2], dsem=dwb)
                    DMA("sp", gl[:], g_kv_d.partition_broadcast(128), (), [b_g], dsem=dwg, cowrite=True)
                else:
                    DMAS("sp", w[:, :, 0:1024], win[:, :, 0:1024], 8, [wbufs["in"]], [b_w], dsem=dw)
                    DMAS("sp", w[:, :, 1024:1536], win[:, :, 3072:3584], 4, [wbufs["in"]], [b_w], dsem=dw)
                    DMAS("sp", wb2[:], wb_qb.rearrange("(kc p) c -> p kc c", p=128), 4, [wbufs["qb"]], [b_wb2], dsem=dwb)
                    DMA("sp", gl[:], g_q_d.partition_broadcast(128), (), [b_g], dsem=dwg, cowrite=True)
                DMA("sp", gat[:], g_attn_d.partition_broadcast(128), (), [b_g], dsem=dwg, cowrite=True)
                if isK:
                    cast_weight("bd", wb_bd, w_bd, 1024, D)
                    cast_weight("bm", wb_bm, w_bm, 1024, D)
                    cast_weight("out", wb_out, w_out, D, D)
                    cast_weight("gate", wb_gate, w_gate, D, FF)
                    cast_weight("up", wb_up, w_up, D, FF)
                    cast_weight("down", wb_down, w_down, FF, D)

                xR = Ring(S, [sb(f"{pf}x{i}", [128, D], F32) for i in range(2)])
                rtR = Ring(S, [sb(f"{pf}rt{i}", [128, 4, 160], F32) for i in range(2)])
                junk = sb(pf + "junk", [128, D], BF16)
                b_junk = Buf()
                hR = Ring(S, [sb(f"{pf}h{i}", [128, D], BF16) for i in range(2)], dma=False)
                hTR = Ring(S, [sb(f"{pf}hT{i}", [128, 16, 128], BF16) for i in range(2)], dma=False)
                stR = Ring(S, [sb(f"{pf}st{i}", [128, 8], F32) for i in range(3)], dma=False)
                t1 = sb(pf + "t1", [128, 8, 64], F32)
                t2 = sb(pf + "t2", [128, 8, 64], F32)
                b_t = Buf()
                tmA = sb(pf + "tmA", [128, 1024], BF16)
                lat = sb(pf + "lat", [128, 512], BF16)
                latT = sb(pf + "latT", [128, 4, 128], BF16)
                tmN = sb(pf + "tmN", [128, 8, 128], BF16)
                tmR = sb(pf + "tmR", [128, 8, 64], BF16)
                b_tmA, b_lat, b_latT, b_tmN, b_tmR = Buf(), Buf(), Buf(), Buf(), Buf()
                stA = sb(pf + "stA", [128, 8, 512], BF16)
                stN = sb(pf + "stN", [128, 8, 512], BF16)
                stRr = sb(pf + "stR", [128, 4, 512], BF16)
                b_stA, b_stN, b_stR = Buf(), Buf(), Buf()
                d_stA, d_stN, d_stR = S.dsem(), S.dsem(), S.dsem()
                if isK:
                    stV = sb("kstV", [128, 8, 4, 132], BF16)
                    stVm = sb("kstVm", [128, 8, 4, 132], BF16)
                    b_stV, b_stVm = Buf(), Buf()
                    S.op("dve", lambda e: e.memset(stV[:, :, :, 128:132], 1.0), (), [b_stV], cowrite=True)
                    S.op("dve", lambda e: e.memset(stVm[:, :, :, 128:132], 1.0), (), [b_stVm], cowrite=True)
                    d_stV, d_stVm = S.dsem(), S.dsem()
                tp = ps(pf + "tp", [128, 16, 128], BF16)
                pj = ps(pf + "pj", [128, 2, 512], F32)
                tk = ps(pf + "tk", [128, 2, 8, 128], BF16)
                tcp = ps(pf + "tc", [128, 4, 128], BF16)
                b_tp, b_tc = Buf(), Buf()
                pjR = Ring(S, [pj[:, i, :] for i in range(2)], dma=False)
                tkR = Ring(S, [tk[:, i, :, :] for i in range(2)], dma=False)

                ngrp = nt // 4
                for g in range(ngrp):
                    rtt_, b_rtt, d_rtt = rtR.next()
                    DMA("sp", rtt_[:], rt[:, t_off + g * 4:t_off + g * 4 + 4, :], [B_rt], [b_rtt], dsem=d_rtt)
                    for j in range(4):
                        ti = g * 4 + j
                        x_, b_x, d_x = xR.next()
                        DMA("sp", x_[:], xin[ti * 128:(ti + 1) * 128, :], (), [b_x], dsem=d_x)
                        st_, b_st, _ = stR.next()
                        ACT(junk[:], x_[:], AF.Square, [b_x], [b_junk, b_st], accum_out=st_[:, 0:1])
                        rstd_from_ss(st_[:, 0:1], 1, D, b_st, st_[:, 1:2], st_[:, 2:3], b_st)
                        h_, b_h, _ = hR.next()
                        STT(h_[:], x_[:], st_[:, 2:3], gat[:], ALU.mult, ALU.mult, [b_x, b_st, b_g], [b_h])
                        for kc in range(16):
                            TR(tp[:, kc, :], h_[:, kc * 128:(kc + 1) * 128], [b_h], [b_tp], cowrite=(kc > 0))
                        hT_, b_hT, _ = hTR.next()
                        CP("dve", hT_[:, 0:8, :], tp[:, 0:8, :], [b_tp], [b_hT])
                        CP("act", hT_[:, 8:16, :], tp[:, 8:16, :], [b_tp], [b_hT], cowrite=True)
                        tab = rtt_[:, j, :]
                        if DBG == 1:
                            continue

                        def proj_block(c0, ncol):
                            p_, b_p, _ = pjR.next()
                            for kc in range(16):
                                MM(p_[:, 0:ncol], hT_[:, kc, :], w[:, kc, c0:c0 + ncol], kc == 0, kc == 15,
                                   [b_hT, b_w], [b_p], cowrite=(kc > 0))
                            return p_, b_p

                        lat0 = 2048 if isK else 1024
                        p_, b_p = proj_block(lat0, 512)
                        st2, b_st2, _ = stR.next()
                        ACT(junk[:, 0:512], p_, AF.Square, [b_p], [b_junk, b_st2], accum_out=st2[:, 0:1])
                        rstd_from_ss(st2[:, 0:1], 1, 512, b_st2, st2[:, 1:2], st2[:, 2:3], b_st2)
                        STT(lat[:], p_, st2[:, 2:3], gl[:], ALU.mult, ALU.mult, [b_p, b_st2, b_g], [b_lat])
                        for blk in range(2):
                            p_, b_p = proj_block(blk * 512, 512)
                            if DBG == 5:
                                continue
                            rope(p_.rearrange("p (c w) -> p c w", w=64), 8, 64, 16, tab, b_rtt, b_p, t1, t2, b_t,
                                 tmA[:, blk * 512:(blk + 1) * 512].rearrange("p (c w) -> p c w", w=64), b_tmA)
                        if DBG == 5 or DBG == 6:
                            continue
                        if isK:
                            for blk in range(2):
                                p_, b_p = proj_block(1024 + blk * 512, 512)
                                CP("act", stV[:, blk * 4:(blk + 1) * 4, j, 0:128], p_.rearrange("p (h d) -> p h d", d=128),
                                   [b_p], [b_stV], cowrite=True)
                        if DBG == 3:
                            continue
                        if DBG == 9:
                            continue
                        for kc in range(4):
                            TR(tcp[:, kc, :], lat[:, kc * 128:(kc + 1) * 128], [b_lat], [b_tc], cowrite=(kc > 0))
                        CP("act", latT[:], tcp[:], [b_tc], [b_latT])
                        if DBG == 10:
                            continue
                        bw = 512 if isK else 384
                        for cb in range(4):
                            p_, b_p, _ = pjR.next()
                            for kc in range(4):
                                MM(p_[:, 0:bw], latT[:, kc, :], wb2[:, kc, cb * bw:(cb + 1) * bw], kc == 0, kc == 3,
                                   [b_latT, b_wb2], [b_p], cowrite=(kc > 0))
                            if DBG == 11:
                                continue
                            if isK:
                                pv = p_.rearrange("p (h t d) -> p h t d", t=2, d=128)
                                ceng = "dve" if cb % 2 == 0 else "act"
                                for h2 in range(2):
                                    CP(ceng, tmN[:, 2 * cb + h2, :], p_[:, h2 * 256:h2 * 256 + 128], [b_p], [b_tmN], cowrite=True)
                                for h2 in range(2):
                                    CP(ceng, stVm[:, 2 * cb + h2, j, 0:128], p_[:, h2 * 256 + 128:h2 * 256 + 256], [b_p], [b_stVm], cowrite=True)
                            else:
                                pv = p_[:, 0:384].rearrange("p (h d) -> p h d", d=192)
                                for h2 in range(2):
                                    CP("dve", tmN[:, 2 * cb + h2, :], p_[:, h2 * 192:h2 * 192 + 128], [b_p], [b_tmN], cowrite=True)
                                rope(pv[:, :, 128:192], 2, 64, 64, tab, b_rtt, b_p, t1, t2, b_t,
                                     tmR[:, 2 * cb:2 * cb + 2, :], b_tmR)
                        if DBG in (11, 12):
                            continue
                        tk_, b_tk, _ = tkR.next()
                        if DBG != 8:
                            for hh in range(8):
                                TR(tk_[:, hh, :], tmA[:, hh * 128:(hh + 1) * 128], [b_tmA], [b_tk], cowrite=(hh > 0))
                        if DBG != 7:
                            CP("dve", stA[:, :, j * 128:(j + 1) * 128], tk_, [b_tk], [b_stA], cowrite=True)
                        if DBG in (2, 7, 8):
                            continue
                        tk_, b_tk, _ = tkR.next()
                        for hh in range(8):
                            TR(tk_[:, hh, :], tmN[:, hh, :], [b_tmN], [b_tk], cowrite=(hh > 0))
                        CP("dve", stN[:, :, j * 128:(j + 1) * 128], tk_, [b_tk], [b_stN], cowrite=True)
                        if DBG == 4:
                            continue
                        if isK:
                            p_, b_p = proj_block(2560, 64)
                            p3 = p_[:, 0:64].unsqueeze(1)
                            rope(p3, 1, 64, 64, tab, b_rtt, b_p, t1, t2, b_t, tmR[:, 0:1, :], b_tmR)
                            CP("dve", tmR[:, 1:2, :], tmR[:, 0:1, :], [b_tmR], [b_tmR], cowrite=True)
                            tk_, b_tk, _ = tkR.next()
                            TR(tk_[:, 0, :], tmR[:, 0:2, :].rearrange("p a d -> p (a d)"), [b_tmR], [b_tk], cowrite=False)
                            CP("act", stRr[:, 0, j * 128:(j + 1) * 128], tk_[:, 0, :], [b_tk], [b_stR], cowrite=True)
                        else:
                            tk_, b_tk, _ = tkR.next()
                            for pr in range(4):
                                TR(tk_[:, pr, :], tmR[:, 2 * pr:2 * pr + 2, :].rearrange("p a d -> p (a d)"), [b_tmR], [b_tk],
                                   cowrite=(pr > 0))
                            CP("act", stRr[:, :, j * 128:(j + 1) * 128], tk_[:, 0:4, :], [b_tk], [b_stR], cowrite=True)
                    c0 = g * 512
                    if isK:
                        DMAS("sp", KTda[:, :, c0:c0 + 512].rearrange("h p t -> p h t"), stA[:], 2, [b_stA], [B_kv], dsem=d_stA)
                        DMAS("sp", KTn[:, :, c0:c0 + 512].rearrange("h p t -> p h t"), stN[:], 2, [b_stN], [B_kv], dsem=d_stN)
                        DMA("sp", KTr[:, c0:c0 + 512], stRr[:, 0, :], [b_stR], [B_kv], dsem=d_stR, cowrite=True)
                        DMAS("sp", Vda[:, :, g * 4:g * 4 + 4, :].rearrange("h p t d -> p h t d"), stV[:], 2, [b_stV], [B_kv], dsem=d_stV)
                        DMAS("sp", Vm[:, :, g * 4:g * 4 + 4, :].rearrange("h p t d -> p h t d"), stVm[:], 2, [b_stVm], [B_kv], dsem=d_stVm)
                    else:
                        DMAS("sp", QTda[:, :, c0:c0 + 512].rearrange("h p t -> p h t"), stA[:], 2, [b_stA], [B_q], dsem=d_stA)
                        DMAS("sp", QTn[:, :, c0:c0 + 512].rearrange("h p t -> p h t"), stN[:], 2, [b_stN], [B_q], dsem=d_stN)
                        DMA("sp", QTr[:, :, c0:c0 + 512].rearrange("h p t -> p h t"), stRr[:], [b_stR], [B_q], dsem=d_stR, cowrite=True)
                S.run_block(final=(stop == (2 if isK else 3)))
                for r in (xR, rtR):
                    r.free()
                for d_ in [dw, dwb, dwg, d_stA, d_stN, d_stR] + ([d_stV, d_stVm] if isK else []):
                    S.dsem_free(d_)

        proj_phase(True)
        if stop == 2:
            return nc
        proj_phase(False)
        if stop == 3:
            return nc

        with contextlib.ExitStack() as es:
            def sb(name, shape, dt):
                return es.enter_context(nc.sbuf_tensor(name, shape, dt))

            def ps(name, shape, dt):
                return es.enter_context(nc.psum_tensor(name, shape, dt))
            NKT = KC // 128
            ktR = Ring(S, [sb(f"a_kt{i}", [128, KC], BF16) for i in range(3)])
            krR = Ring(S, [sb(f"a_kr{i}", [128, KC], BF16) for i in range(2)])
            v_tiles = [sb(f"a_v{i}", [128, NKT, 132], BF16) for i in range(3)]
            vR = Ring(S, v_tiles)
            qtR = Ring(S, [sb(f"a_qt{i}", [128, 512], BF16) for i in range(5)])
            qrR = Ring(S, [sb(f"a_qr{i}", [128, 512], BF16) for i in range(2)])
            pTR = Ring(S, [sb(f"a_pT{i}", [128, 512], BF16) for i in range(4)], dma=False)
            o1 = sb("o1", [128, 4, 128], F32)
            o2 = sb("o2", [128, 4, 128], F32)
            o3 = sb("o3", [128, 4, 128], F32)
            osq = sb("osq", [128, 4, 128], F32)
            onb = sb("onb", [128, 4, 128], BF16)
            fst = sb("fst", [128, 16], F32)
            gsub = sb("gsub", [128, 128], F32)
            b_o1, b_o2, b_o3, b_osq, b_onb, b_fst, b_gsub = Buf(), Buf(), Buf(), Buf(), Buf(), Buf(), Buf()
            oT = sb("oT", [128, 16, 512], BF16)
            b_oT = Buf()
            x1 = sb("x1", [128, 4, D], F32)
            b_x1 = [Buf() for _ in range(4)]
            d_x1 = [S.dsem() for _ in range(4)]
            d_y = [S.dsem() for _ in range(4)]
            hT = sb("hT", [128, 16, 512], BF16)
            b_hT = Buf()
            hb = sb("hb", [128, D], BF16)
            b_hb = Buf()
            big = sb("big", [128, 22, 512], BF16)
            b_big = [Buf() for _ in range(22)]
            gbuf = sb("gbuf", [128, D], F32)
            b_gbuf = Buf()
            d_g = S.dsem()
            junk = sb("ajunk", [128, D], BF16)
            b_junk = Buf()
            pst = sb("pst", [128, 16], F32)
            b_pst = Buf()
            sg1 = sb("sg1", [128, 512], F32)
            sg2 = sb("sg2", [128, 512], F32)
            b_sg1, b_sg2 = Buf(), Buf()
            wR = Ring(S, [sb(f"wbuf{i}", [128, 8192], BF16) for i in range(3)])
            dg0 = S.dsem()
            DMA("sp", gsub[:], g_sub_d.partition_broadcast(128), (), [b_gsub], dsem=dg0)
            TS("dve", gsub[:], gsub[:], 1.0 - LAMBDA_INIT, ALU.mult, [b_gsub], [b_gsub], cowrite=True)

            stp = ps("stp", [128, 3, 512], F32)
            acc = ps("acc", [128, 4, 512], F32)
            tpo = ps("tpo", [128, 4, 128], BF16)
            stpR = Ring(S, [stp[:, i, :] for i in range(3)], dma=False)
            b_acc, b_tpo = Buf(), Buf()
            accB = [Buf() for _ in range(4)]

            SC_DA = 64 ** -0.5
            SC_MLA = 192 ** -0.5

            def load_w(view_fn):
                t_, b_, d_ = wR.next()
                view_fn(t_, b_, d_)
                return t_, b_

            for g in range(NG):
                prompt = g < NPo // 512
                k0 = 0 if prompt else SP
                klen = SP if prompt else SS
                q0 = g * 512
                nch = klen // KC
                steps = []
                for u in range(16):
                    for m in range(2 if u < 8 else 1):
                        for c in range(nch):
                            for t in range(NKT):
                                steps.append((u, m, c, t))
                cur = {"u": None, "chunk": None}
                rec = {}

                def emit_qk(i):
                    u, m, c, t = steps[i]
                    da = u < 8
                    h = u % 8
                    if cur["u"] != u:
                        cur["u"] = u
                        if da:
                            qs = []
                            for mm_ in range(2):
                                qt_, b_qt, d_qt = qtR.next()
                                DMA("sp", qt_[:], QTda[h, :, q0:q0 + 512], [B_q], [b_qt], dsem=d_qt)
                                oh = slice(64, 128) if mm_ == 0 else slice(0, 64)
                                S.op("pool", lambda e, qt_=qt_, oh=oh: e.memset(qt_[oh, :], 0.0), [b_qt], [b_qt], cowrite=True)
                                qs.append((qt_, b_qt))
                            cur["q"] = (qs, None, None, None)
                        else:
                            qt_, b_qt, d_qt = qtR.next()
                            DMA("sp", qt_[:], QTn[h, :, q0:q0 + 512], [B_q], [b_qt], dsem=d_qt)
                            qr_, b_qr, d_qr = qrR.next()
                            DMA("sp", qr_[:], QTr[h // 2, :, q0:q0 + 512], [B_q], [b_qr], dsem=d_qr)
                            oh = slice(64, 128) if h % 2 == 0 else slice(0, 64)
                            S.op("pool", lambda e, qr_=qr_, oh=oh: e.memset(qr_[oh, :], 0.0), [b_qr], [b_qr], cowrite=True)
                            cur["q"] = (qt_, b_qt, qr_, b_qr)
                    qt_, b_qt, qr_, b_qr = cur["q"]
                    if cur["chunk"] != (u, m, c):
                        cur["chunk"] = (u, m, c)
                        ks = k0 + c * KC
                        kt_, b_kt, d_kt = ktR.next()
                        v_, b_v, d_v = vR.next()
                        if da:
                            DMA("sp", kt_[:], KTda[h, :, ks:ks + KC], [B_kv], [b_kt], dsem=d_kt)
                            DMA("sp", v_[:], Vda[h, :, ks // 128:ks // 128 + NKT, :], [B_kv], [b_v], dsem=d_v)
                            cur["kv"] = (kt_, b_kt, None, None, v_, b_v)
                        else:
                            DMA("sp", kt_[:], KTn[h, :, ks:ks + KC], [B_kv], [b_kt], dsem=d_kt)
                            kr_, b_kr, d_kr = krR.next()
                            DMA("sp", kr_[:], KTr[:, ks:ks + KC], [B_kv], [b_kr], dsem=d_kr)
                            DMA("sp", v_[:], Vm[h, :, ks // 128:ks // 128 + NKT, :], [B_kv], [b_v], dsem=d_v)
                            cur["kv"] = (kt_, b_kt, kr_, b_kr, v_, b_v)
                    kt_, b_kt, kr_, b_kr, v_, b_v = cur["kv"]
                    s_, b_s, _ = stpR.next()
                    ksl = slice(t * 128, (t + 1) * 128)
                    if da:
                        qt_, b_qt = qt_[m]
                        MM(s_, kt_[:, ksl], qt_[:], True, True, [b_kt, b_qt], [b_s], cowrite=False)
                    else:
                        MM(s_, kt_[:, ksl], qt_[:], True, False, [b_kt, b_qt], [b_s], cowrite=False)
                        MM(s_, kr_[:, ksl], qr_[:], False, True, [b_kr, b_qr], [b_s], cowrite=True)
                    rec[i] = (s_, b_s, v_, b_v)

                def emit_exp_pv(i):
                    u, m, c, t = steps[i]
                    da = u < 8
                    s_, b_s, v_, b_v = rec.pop(i)
                    first = (c == 0 and t == 0)
                    last = (c == nch - 1 and t == NKT - 1)
                    p_, b_p, _ = pTR.next()
                    ACT(p_[:], s_, AF.Exp, [b_s], [b_p], scale=(SC_DA if da else SC_MLA))
                    for s4 in range(4):
                        MM(acc[:, s4, 0:129], p_[:, s4 * 128:(s4 + 1) * 128], v_[:, t, 0:129], first, last,
                           [b_p, b_v], [accB[s4]], cowrite=not first)
                    return last

                def finalize(u, m):
                    da = u < 8
                    rl = fst[:, 0:4]
                    S.op("dve", lambda e, rl=rl: e.reciprocal(out=rl.unsqueeze(2), in_=acc[:, :, 128:129]), accB, [b_fst])
                    if da and m == 0:
                        for s4 in range(4):
                            TS("dve", o1[:, s4, :], acc[:, s4, 0:128], rl[:, s4:s4 + 1], ALU.mult, accB + [b_fst], [b_o1], cowrite=True)
                        return
                    if da:
                        for s4 in range(4):
                            TS("dve", o2[:, s4, :], acc[:, s4, 0:128], rl[:, s4:s4 + 1], ALU.mult, accB + [b_fst], [b_o2], cowrite=True)
                        STT(o3[:], o2[:], sc[:, 1:2], o1[:], ALU.mult, ALU.add, [b_o2, b_o1, B_sc], [b_o3])
                        TT("dve", osq[:], o3[:], o3[:], ALU.mult, [b_o3], [b_osq])
                        S.op("dve", lambda e: e.tensor_reduce(out=fst[:, 4:8], in_=osq[:], axis=AX.X, op=ALU.add), [b_osq], [b_fst], cowrite=True)
                        rstd_from_ss(fst[:, 4:8], 4, 128, b_fst, fst[:, 8:12], fst[:, 12:16], b_fst)
                        for s4 in range(4):
                            STT(onb[:, s4, :], o3[:, s4, :], fst[:, 12 + s4:13 + s4], gsub[:], ALU.mult, ALU.mult,
                                [b_o3, b_fst, b_gsub], [b_onb], cowrite=True)
                    else:
                        for s4 in range(4):
                            TS("dve", onb[:, s4, :], acc[:, s4, 0:128], rl[:, s4:s4 + 1], ALU.mult, accB + [b_fst], [b_onb], cowrite=True)
                    for s4 in range(4):
                        TR(tpo[:, s4, :], onb[:, s4, :], [b_onb], [b_tpo], cowrite=(s4 > 0))
                    CP("dve", oT[:, u, :], tpo[:].rearrange("p a b -> p (a b)"), [b_tpo], [b_oT], cowrite=True)

                LOOK = 2
                nst = len(steps)
                for i in range(min(LOOK, nst)):
                    emit_qk(i)
                for i in range(nst):
                    if i + LOOK < nst:
                        emit_qk(i + LOOK)
                    if emit_exp_pv(i):
                        finalize(steps[i][0], steps[i][1])

                DMA("sp", gbuf[:], g_attn_d.partition_broadcast(128), (), [b_gbuf], dsem=d_g)
                for j in range(4):
                    r0 = q0 + j * 128
                    DMA("sp", x1[:, j, :], xq[r0:r0 + 128, :], (), [b_x1[j]], dsem=d_x1[j])

                def norm_T(j, dstT, b_dstT, first):
                    ACT(junk[:], x1[:, j, :], AF.Square, [b_x1[j]], [b_junk, b_pst], accum_out=pst[:, 0:1])
                    rstd_from_ss(pst[:, 0:1], 1, D, b_pst, pst[:, 1:2], pst[:, 2:3], b_pst)
                    STT(hb[:], x1[:, j, :], pst[:, 2:3], gbuf[:], ALU.mult, ALU.mult, [b_x1[j], b_pst, b_gbuf], [b_hb])
                    for q4 in range(4):
                        for kk in range(4):
                            kc = q4 * 4 + kk
                            TR(tpo[:, kk, :], hb[:, kc * 128:(kc + 1) * 128], [b_hb], [b_tpo], cowrite=(kk > 0))
                        CP("dve" if q4 % 2 == 0 else "act", dstT[:, q4 * 4:q4 * 4 + 4, j * 128:(j + 1) * 128], tpo[:], [b_tpo], [b_dstT],
                           cowrite=not (first and q4 == 0))
                for j in range(4):
                    norm_T(j, hT, b_hT, j == 0)
                win = wb_in.rearrange("(kc p) c -> p kc c", p=128)
                wbd_v = wb_bd.rearrange("(kc p) c -> p kc c", p=128)
                wbm_v = wb_bm.rearrange("(kc p) c -> p kc c", p=128)
                for fc in range(16):
                    wt, b_wt, d_wt = wR.next()
                    wv = wt[:, 0:48 * 128].rearrange("p (k c) -> p k c", c=128)
                    DMA("sp", wv[:, 0:4, :], win[:, 0:4, 4160 + fc * 128:4160 + (fc + 1) * 128], [wbufs["in"]], [b_wt], dsem=d_wt)
                    DMAS("sp", wv[:, 4:16, :], win[:, 4:16, 4160 + fc * 128:4160 + (fc + 1) * 128], 3, [wbufs["in"]], [b_wt], dsem=d_wt)
                    DMAS("sp", wv[:, 16:32, :], win[:, :, 4160 + D + fc * 128:4160 + D + (fc + 1) * 128], 4, [wbufs["in"]], [b_wt], dsem=d_wt)
                    DMAS("sp", wv[:, 32:40, :], wbd_v[:, :, fc * 128:(fc + 1) * 128], 2, [wbufs["bd"]], [b_wt], dsem=d_wt)
                    DMAS("sp", wv[:, 40:48, :], wbm_v[:, :, fc * 128:(fc + 1) * 128], 2, [wbufs["bm"]], [b_wt], dsem=d_wt)
                    for kc in range(16):
                        MM(acc[:, 0, :], wv[:, kc, :], hT[:, kc, :], kc == 0, kc == 15, [b_wt, b_hT], [accB[0]], cowrite=(kc > 0))
                    for kc in range(16):
                        MM(acc[:, 1, :], wv[:, 16 + kc, :], hT[:, kc, :], kc == 0, kc == 15, [b_wt, b_hT], [accB[1]], cowrite=(kc > 0))
                    for kc in range(8):
                        MM(acc[:, 2, :], wv[:, 32 + kc, :], oT[:, kc, :], kc == 0, kc == 7, [b_wt, b_oT], [accB[2]], cowrite=(kc > 0))
                    for kc in range(8):
                        MM(acc[:, 3, :], wv[:, 40 + kc, :], oT[:, 8 + kc, :], kc == 0, kc == 7, [b_wt, b_oT], [accB[3]], cowrite=(kc > 0))
                    ACT(sg1[:], acc[:, 0, :], AF.Sigmoid, [accB[0]], [b_sg1])
                    ACT(sg2[:], acc[:, 1, :], AF.Sigmoid, [accB[1]], [b_sg2])
                    TT("dve", sg1[:], sg1[:], acc[:, 2, :], ALU.mult, [b_sg1, accB[2]], [b_sg1])
                    TT("dve", sg2[:], sg2[:], acc[:, 3, :], ALU.mult, [b_sg2, accB[3]], [b_sg2])
                    TT("dve", big[:, fc, :], sg1[:], sg2[:], ALU.add, [b_sg1, b_sg2], [b_big[fc]])
                wout_v = wb_out.rearrange("(kc p) c -> p kc c", p=128)
                for cb in range(4):
                    wt, b_wt, d_wt = wR.next()
                    wv = wt[:].rearrange("p (k c) -> p k c", c=512)
                    DMA("sp", wv[:, 0:4, :], wout_v[:, 0:4, cb * 512:(cb + 1) * 512], [wbufs["out"]], [b_wt], dsem=d_wt)
                    DMAS("sp", wv[:, 4:16, :], wout_v[:, 4:16, cb * 512:(cb + 1) * 512], 3, [wbufs["out"]], [b_wt], dsem=d_wt)
                    for j in range(4):
                        for kc in range(16):
                            MM(acc[:, j, :], big[:, kc, j * 128:(j + 1) * 128], wv[:, kc, :], kc == 0, kc == 15, [b_wt, b_big[kc]], [accB[j]],
                               cowrite=(kc > 0))
                        xs = x1[:, j, cb * 512:(cb + 1) * 512]
                        TT("dve", xs, xs, acc[:, j, :], ALU.add, [b_x1[j], accB[j]], [b_x1[j]], cowrite=True)
                DMA("sp", gbuf[:], g_ffn_d.partition_broadcast(128), (), [b_gbuf], dsem=d_g)
                for j in range(4):
                    norm_T(j, hT, b_hT, j == 0)
                wg_v = wb_gate.rearrange("(kc p) c -> p kc c", p=128)
                wu_v = wb_up.rearrange("(kc p) c -> p kc c", p=128)
                wd_v = wb_down.rearrange("(hc p) c -> p hc c", p=128)
                for half in range(2):
                    for hp2 in range(11):
                        hc0 = half * 22 + hp2 * 2
                        wt, b_wt, d_wt = wR.next()
                        wv = wt[:].rearrange("p (s k c) -> p s k c", s=2, c=256)
                        DMA("sp", wv[:, 0, 0:4, :], wg_v[:, 0:4, hc0 * 128:(hc0 + 2) * 128], [wbufs["gate"]], [b_wt], dsem=d_wt)
                        DMAS("sp", wv[:, 0, 4:16, :], wg_v[:, 4:16, hc0 * 128:(hc0 + 2) * 128], 3, [wbufs["gate"]], [b_wt], dsem=d_wt)
                        DMAS("sp", wv[:, 1, :, :], wu_v[:, :, hc0 * 128:(hc0 + 2) * 128], 4, [wbufs["up"]], [b_wt], dsem=d_wt)
                        for i2 in range(2):
                            ga, ua = (0, 1) if i2 == 0 else (2, 3)
                            for kc in range(16):
                                MM(acc[:, ga, :], wv[:, 0, kc, i2 * 128:(i2 + 1) * 128], hT[:, kc, :], kc == 0, kc == 15, [b_wt, b_hT], [accB[ga]],
                                   cowrite=(kc > 0))
                            for kc in range(16):
                                MM(acc[:, ua, :], wv[:, 1, kc, i2 * 128:(i2 + 1) * 128], hT[:, kc, :], kc == 0, kc == 15, [b_wt, b_hT], [accB[ua]],
                                   cowrite=(kc > 0))
                            sg, b_sg = (sg1, b_sg1) if i2 == 0 else (sg2, b_sg2)
                            ACT(sg[:], acc[:, ga, :], AF.Silu, [accB[ga]], [b_sg])
                            ci = hp2 * 2 + i2
                            TT("dve", big[:, ci, :], sg[:], acc[:, ua, :], ALU.mult, [b_sg, accB[ua]], [b_big[ci]])
                    for cb in range(4):
                        wts = []
                        for part in range(2):
                            wt, b_wt, d_wt = wR.next()
                            wv = wt[:, 0:11 * 512].rearrange("p (k c) -> p k c", c=512)
                            hc0 = half * 22 + part * 11
                            DMA("sp", wv[:, 0:4, :], wd_v[:, hc0:hc0 + 4, cb * 512:(cb + 1) * 512], [wbufs["down"]], [b_wt], dsem=d_wt)
                            DMAS("sp", wv[:, 4:11, :], wd_v[:, hc0 + 4:hc0 + 11, cb * 512:(cb + 1) * 512], 2, [wbufs["down"]], [b_wt], dsem=d_wt)
                            wts.append((wv, b_wt))
                        for j in range(4):
                            for ci in range(22):
                                wv, b_wt = wts[ci // 11]
                                MM(acc[:, j, :], big[:, ci, j * 128:(j + 1) * 128], wv[:, ci % 11, :], ci == 0, ci == 21, [b_wt, b_big[ci]], [accB[j]],
                                   cowrite=(ci > 0))
                            xs = x1[:, j, cb * 512:(cb + 1) * 512]
                            TT("dve", xs, xs, acc[:, j, :], ALU.add, [b_x1[j], accB[j]], [b_x1[j]], cowrite=True)
                DMA("sp", gbuf[:], g_fin_d.partition_broadcast(128), (), [b_gbuf], dsem=d_g)
                for j in range(4):
                    ACT(junk[:], x1[:, j, :], AF.Square, [b_x1[j]], [b_junk, b_pst], accum_out=pst[:, 0:1])
                    rstd_from_ss(pst[:, 0:1], 1, D, b_pst, pst[:, 1:2], pst[:, 2:3], b_pst)
                    STT(x1[:, j, :], x1[:, j, :], pst[:, 2:3], gbuf[:], ALU.mult, ALU.mult, [b_x1[j], b_pst, b_gbuf], [b_x1[j]], cowrite=True)
                    r0 = q0 + j * 128
                    DMA("sp", y[r0:r0 + 128, :], x1[:, j, :], [b_x1[j]], (), dsem=d_y[j])
            S.run_block(final=True)
    return nc


_NC_CACHE = {}
PARAM_NAMES = ["attn_norm_g", "w_in", "da_lambda_q1", "da_lambda_k1", "da_lambda_q2", "da_lambda_k2", "da_subln_g",
               "mla_q_norm_g", "mla_w_q_b", "mla_kv_norm_g", "mla_w_kv_b", "w_branch_da", "w_branch_mla", "w_out",
               "ffn_norm_g", "w_gate", "w_up", "w_down"]


STOP = 9
NCORES = 8
DBG = 0


def kernel(x_prompt, x_sample, final_norm_g, **params):
    xp = np.asarray(x_prompt, dtype=np.float32)[0]
    xs = np.asarray(x_sample, dtype=np.float32)[0]
    SP, SS = xp.shape[0], xs.shape[0]
    NPo, NSo = SP // 8, SS // 8
    key = (NPo, NSo)
    if key not in _NC_CACHE:
        _NC_CACHE[key] = build_nc(NPo, NSo, STOP)
    nc = _NC_CACHE[key]
    xall = np.ascontiguousarray(np.concatenate([xp, xs], axis=0))
    shared = {"xall": xall, "ident": np.eye(128, dtype=np.float32)}
    for n in PARAM_NAMES:
        a = np.asarray(params[n], dtype=np.float32)
        a = a[0]
        if a.ndim == 1:
            a = a[None, :]
        shared[n] = np.ascontiguousarray(a)
    shared["final_norm_g"] = np.ascontiguousarray(np.asarray(final_norm_g, dtype=np.float32)[None, :])
    pk = np.concatenate([np.arange(SP), np.arange(SS)]).astype(np.float32)
    shared["posk"] = np.ascontiguousarray(pk.reshape(-1, 128).T)
    in_maps = []
    for c in range(8):
        m = dict(shared)
        m["xq"] = np.ascontiguousarray(np.concatenate([xp[c * NPo:(c + 1) * NPo], xs[c * NSo:(c + 1) * NSo]], axis=0))
        pq = np.concatenate([np.arange(c * NPo, (c + 1) * NPo), np.arange(c * NSo, (c + 1) * NSo)]).astype(np.float32)
        m["posq"] = np.ascontiguousarray(pq.reshape(-1, 128).T)
        in_maps.append(m)
    res = run_bass_kernel_spmd(nc, in_maps[:NCORES], core_ids=list(range(NCORES)))
    rr = [res.results[c]["y"] if c < NCORES else np.zeros((NPo + NSo, D), np.float32) for c in range(8)]
    yp = np.concatenate([rr[c][:NPo] for c in range(8)], axis=0)[None]
    ys = np.concatenate([rr[c][NPo:] for c in range(8)], axis=0)[None]
    return (yp.astype(np.float32), ys.astype(np.float32))
```

```python
import contextlib
import math
import numpy as np
import concourse.bass as bass
import concourse.mybir as mybir
from concourse.bass_utils import run_bass_kernel_spmd

F32 = mybir.dt.float32
BF16 = mybir.dt.bfloat16
I32 = mybir.dt.int32
AF = mybir.ActivationFunctionType
ALU = mybir.AluOpType
AX = mybir.AxisListType

D = 2048
NH = 8
FF = 5632
INC = 8256
EPS = 1e-6
THETA = 500000.0
LAMBDA_INIT = 0.8 - 0.6 * math.exp(0.0)


class LSem:
    def __init__(self, S, name, step):
        self.S, self.name, self.step = S, name, step
        self.epoch = 28000 // step
        self.n = 0
        self.phys = []
        self.persist = False
        self.base = 0
        self.rank = {}

    def target_v(self, v):
        e = (v - 1) // self.epoch
        return self._phys(e), ((v - 1) % self.epoch + 1) * self.step

    def _phys(self, e):
        while len(self.phys) <= e:
            self.phys.append(self.S.nc.alloc_semaphore(name=f"{self.name}_{len(self.phys)}"))
        return self.phys[e]

    def next(self):
        self.n += 1
        return self.n

    def inc_target(self, n):
        return self._phys((n - 1) // self.epoch), self.step

    def wait_target(self, n):
        e = (n - 1) // self.epoch
        return self._phys(e), ((n - 1) % self.epoch + 1) * self.step


class Buf:
    def __init__(self, name=""):
        self.name = name
        self.writers = {}
        self.readers = {}


ENGS = ("pe", "act", "dve", "pool", "sp")


class Sched:
    def __init__(self, nc):
        self.nc = nc
        self.es = {e: LSem(self, "e_" + e, 1) for e in ("pe", "act", "dve", "pool")}
        self.prog = {e: [] for e in ENGS}
        self.seen = {e: {} for e in ENGS}
        self.dma_pool = []
        self.dma_all = []
        self.nops = 0

    def dsem(self, persist=False):
        if self.dma_pool and not persist:
            return self.dma_pool.pop()
        s = LSem(self, f"d{len(self.dma_all)}", 16)
        s.persist = persist
        self.dma_all.append(s)
        return s

    def dsem_free(self, s):
        self.dma_pool.append(s)

    def _wait(self, eng, lsem, n):
        seen = self.seen[eng]
        if seen.get(lsem, 0) >= n:
            return
        seen[lsem] = n
        self.prog[eng].append(("w", lsem, n))

    def op(self, eng, fn, reads=(), writes=(), dsem=None, cowrite=False):
        own = self.es.get(eng) if dsem is None else None
        deps = []
        for b in reads:
            for ls, n in b.writers.items():
                if ls is own and eng == "pe":
                    continue
                deps.append((ls, n))
        for b in writes:
            if not cowrite:
                for ls, n in b.writers.items():
                    if ls is own:
                        continue
                    deps.append((ls, n))
            for ls, n in b.readers.items():
                if ls is own:
                    continue
                deps.append((ls, n))
        for ls, n in deps:
            self._wait(eng, ls, n)
        ls = dsem if dsem is not None else own
        n = ls.next()
        self.prog[eng].append(("o", fn, ls, n))
        self.nops += 1
        for b in reads:
            if b.readers.get(ls, 0) < n:
                b.readers[ls] = n
        for b in writes:
            if cowrite:
                b.writers[ls] = n
            else:
                b.writers = {ls: n}
                b.readers = {}
        return (ls, n)

    def run_block(self, final=False):
        for s in self.dma_all:
            if s.n and (final or not s.persist):
                self._wait("sp", s, s.n)
        for e, s in self.es.items():
            if s.n:
                self._wait("sp", s, s.n)
        nc = self.nc
        prog = self.prog
        comp = set(self.es.values())
        lazy = {self.es["pe"]}
        waited = {ls: set() for ls in comp}
        for e in ENGS:
            for it in prog[e]:
                if it[0] == "w" and it[1] in comp:
                    waited[it[1]].add(it[2])
        for ls in comp:
            if ls in lazy:
                ws = sorted(waited[ls])
            else:
                ws = sorted(it[-1] for e in ENGS for it in prog[e] if it[0] == "o" and it[-2] is ls)
            ls.rank = {n: ls.base + i + 1 for i, n in enumerate(ws)}

        def replay(lst, e):
            for it in lst:
                ls, n = it[-2], it[-1]
                if it[0] == "w":
                    if ls in comp:
                        ph, val = ls.target_v(ls.rank[n])
                    else:
                        ph, val = ls.wait_target(n)
                    e.wait_ge(ph, val)
                else:
                    ins = it[1](e)
                    if ls in comp:
                        if n in ls.rank:
                            ph, _ = ls.target_v(ls.rank[n])
                            ins.then_inc(ph, 1)
                    else:
                        ph, amt = ls.inc_target(n)
                        ins.then_inc(ph, amt)

        with nc.Block() as block:
            @block.sync
            def _(e):
                replay(prog["sp"], e)

            @block.tensor
            def _(e):
                replay(prog["pe"], e)

            @block.scalar
            def _(e):
                replay(prog["act"], e)

            @block.vector
            def _(e):
                replay(prog["dve"], e)

            @block.gpsimd
            def _(e):
                replay(prog["pool"], e)
        self.prog = {e: [] for e in ENGS}
        for ls in comp:
            ls.base += len(ls.rank)
            ls.rank = {}
        for e in ENGS:
            for s in self.dma_all:
                if not s.persist:
                    self.seen[e][s] = s.n
            for s in self.es.values():
                self.seen[e][s] = s.n


class Ring:
    def __init__(self, S, tiles, dma=True):
        self.S = S
        self.slots = [(t, Buf(), S.dsem() if dma else None) for t in tiles]
        self.i = -1

    def next(self):
        self.i = (self.i + 1) % len(self.slots)
        return self.slots[self.i]

    def free(self):
        for t, b, d in self.slots:
            if d is not None:
                self.S.dsem_free(d)


def build_nc(NPo, NSo, stop=9):
    NQ = NPo + NSo
    SP, SS = 8 * NPo, 8 * NSo
    STOT = SP + SS
    NTK = STOT // 128
    NTQ = NQ // 128
    NT = NTK + NTQ
    NG = NQ // 512
    KC = 1024
    nc = bass.Bass("TRN2", target_bir_lowering=False)

    def din(name, shape):
        return nc.dram_tensor(name, shape, F32, kind="ExternalInput").ap()

    def dscr(name, shape, dt=BF16):
        return nc.dram_tensor(name, shape, dt, kind="Internal").ap()

    xall = din("xall", [STOT, D])
    xq = din("xq", [NQ, D])
    posk = din("posk", [128, NTK])
    posq = din("posq", [128, NTQ])
    ident_d = din("ident", [128, 128])
    g_attn_d = din("attn_norm_g", [1, D])
    w_in = din("w_in", [D, INC])
    lam_d = [din(n, [1, 64]) for n in ("da_lambda_q1", "da_lambda_k1", "da_lambda_q2", "da_lambda_k2")]
    g_sub_d = din("da_subln_g", [1, 128])
    g_q_d = din("mla_q_norm_g", [1, 512])
    w_qb = din("mla_w_q_b", [512, 1536])
    g_kv_d = din("mla_kv_norm_g", [1, 512])
    w_kvb = din("mla_w_kv_b", [512, 2048])
    w_bd = din("w_branch_da", [1024, D])
    w_bm = din("w_branch_mla", [1024, D])
    w_out = din("w_out", [D, D])
    g_ffn_d = din("ffn_norm_g", [1, D])
    w_gate = din("w_gate", [D, FF])
    w_up = din("w_up", [D, FF])
    w_down = din("w_down", [FF, D])
    g_fin_d = din("final_norm_g", [1, D])
    y = nc.dram_tensor("y", [NQ, D], F32, kind="ExternalOutput").ap()

    wb_in = dscr("wb_in", [D, INC])
    wb_qb = dscr("wb_qb", [512, 1536])
    wb_kvb = dscr("wb_kvb", [512, 2048])
    wb_bd = dscr("wb_bd", [1024, D])
    wb_bm = dscr("wb_bm", [1024, D])
    wb_out = dscr("wb_out", [D, D])
    wb_gate = dscr("wb_gate", [D, FF])
    wb_up = dscr("wb_up", [D, FF])
    wb_down = dscr("wb_down", [FF, D])
    rt = dscr("rt", [128, NT, 160], F32)
    KTda = dscr("KTda", [NH, 128, STOT])
    KTn = dscr("KTn", [NH, 128, STOT])
    KTr = dscr("KTr", [128, STOT])
    Vda = dscr("Vda", [NH, 128, NTK, 132])
    Vm = dscr("Vm", [NH, 128, NTK, 132])
    QTda = dscr("QTda", [NH, 128, NQ])
    QTn = dscr("QTn", [NH, 128, NQ])
    QTr = dscr("QTr", [4, 128, NQ])

    S = Sched(nc)
    B_rt, B_kv, B_q = Buf("rt"), Buf("kv"), Buf("q")

    def DMA(eng, out, in_, reads=(), writes=(), dsem=None, cowrite=False):
        return S.op(eng, lambda e: e.dma_start(out=out, in_=in_), reads, writes, dsem=dsem, cowrite=cowrite)

    def DMAS(eng, out, in_, n, reads=(), writes=(), dsem=None):
        sz = out.shape[1]
        step = (sz + n - 1) // n
        for a in range(0, sz, step):
            b_ = min(sz, a + step)
            DMA(eng, out[:, a:b_], in_[:, a:b_], reads, writes, dsem=dsem, cowrite=True)

    def MM(out, lhsT, rhs, start, stop, reads, writes, cowrite=True):
        return S.op("pe", lambda e: e.matmul(out, lhsT=lhsT, rhs=rhs, start=start, stop=stop), reads, writes, cowrite=cowrite)

    def TR(out, in_, reads, writes, cowrite=True):
        return S.op("pe", lambda e: e.transpose(out, in_, ident_b[:]), list(reads) + [B_id], writes, cowrite=cowrite)

    def ACT(out, in_, func, reads, writes, scale=1.0, bias=0.0, accum_out=None, cowrite=False):
        return S.op("act", lambda e: e.activation(out=out, in_=in_, func=func, bias=bias, scale=scale, accum_out=accum_out),
                    reads, writes, cowrite=cowrite)

    def TT(eng, out, in0, in1, op, reads, writes, cowrite=False):
        return S.op(eng, lambda e: e.tensor_tensor(out=out, in0=in0, in1=in1, op=op), reads, writes, cowrite=cowrite)

    def TS(eng, out, in0, s1, op0, reads, writes, s2=None, op1=ALU.bypass, cowrite=False):
        return S.op(eng, lambda e: e.tensor_scalar(out=out, in0=in0, scalar1=s1, scalar2=s2, op0=op0, op1=op1), reads, writes, cowrite=cowrite)

    def STT(out, in0, scalar, in1, op0, op1, reads, writes, cowrite=False):
        return S.op("dve", lambda e: e.scalar_tensor_tensor(out=out, in0=in0, scalar=scalar, in1=in1, op0=op0, op1=op1),
                    reads, writes, cowrite=cowrite)

    def CP(eng, out, in_, reads, writes, cowrite=False):
        if eng == "act":
            return S.op("act", lambda e: e.copy(out=out, in_=in_), reads, writes, cowrite=cowrite)
        return S.op(eng, lambda e: e.tensor_copy(out=out, in_=in_), reads, writes, cowrite=cowrite)

    def rstd_from_ss(ss, n, dim, reads_b, tmp, out, out_b):
        ACT(tmp, ss, AF.Ln, [reads_b], [out_b], scale=1.0 / dim, bias=eps_t[:, 0:1], cowrite=True)
        ACT(out, tmp, AF.Exp, [out_b], [out_b], scale=-0.5, cowrite=True)

    with contextlib.ExitStack() as top:
        def tsb(name, shape, dt):
            return top.enter_context(nc.sbuf_tensor(name, shape, dt))
        ident_b = tsb("ident_b", [128, 128], BF16)
        sc = tsb("sc", [128, 4], F32)
        eps_t = tsb("eps_t", [128, 1], F32)
        B_id, B_sc = Buf("id"), Buf("sc")

        wbufs = {}

        def cast_weight(key, dst, src, rows, cols):
            b = Buf(key)
            ds = S.dsem(persist=True)
            wbufs[key] = b
            bw = cols
            while bw > 2048:
                for dv in (2, 3, 4, 5, 6, 7, 8, 11):
                    if cols % dv == 0 and cols // dv <= 2048:
                        bw = cols // dv
                        break
                break
            for r0 in range(0, rows, 128):
                DMA("pool", dst[r0:r0 + 128, :].rearrange("r (a b) -> r a b", b=bw),
                    src[r0:r0 + 128, :].rearrange("r (a b) -> r a b", b=bw), writes=[b], dsem=ds, cowrite=True)

        with contextlib.ExitStack() as es:
            def sb(name, shape, dt):
                return es.enter_context(nc.sbuf_tensor(name, shape, dt))
            cast_weight("in", wb_in, w_in, D, INC)
            cast_weight("kvb", wb_kvb, w_kvb, 512, 2048)
            cast_weight("qb", wb_qb, w_qb, 512, 1536)
            idf = sb("idf", [128, 128], F32)
            lq = sb("lq", [128, 4, 64], F32)
            prod = sb("prod", [128, 2, 64], F32)
            s12 = sb("s12", [128, 2], F32)
            e12 = sb("e12", [128, 2], F32)
            d0, d0b, d0c = S.dsem(), S.dsem(), S.dsem()
            b_idf, b_lq, b_pr, b_s12 = Buf(), Buf(), Buf(), Buf()
            DMA("sp", idf[:], ident_d[:, :], writes=[b_idf], dsem=d0)
            CP("dve", ident_b[:], idf[:], [b_idf], [B_id])
            S.op("dve", lambda e: e.memset(eps_t[:], EPS), (), [B_sc], cowrite=True)
            for i in range(4):
                DMA("sp", lq[:, i, :], lam_d[i].partition_broadcast(128), writes=[b_lq], dsem=d0b, cowrite=True)
            TT("dve", prod[:, 0, :], lq[:, 0, :], lq[:, 1, :], ALU.mult, [b_lq], [b_pr], cowrite=True)
            TT("dve", prod[:, 1, :], lq[:, 2, :], lq[:, 3, :], ALU.mult, [b_lq], [b_pr], cowrite=True)
            S.op("dve", lambda e: e.tensor_reduce(out=s12[:], in_=prod[:], axis=AX.X, op=ALU.add), [b_pr], [b_s12])
            ACT(e12[:], s12[:], AF.Exp, [b_s12], [b_s12], cowrite=True)
            TT("dve", sc[:, 0:1], e12[:, 0:1], e12[:, 1:2], ALU.subtract, [b_s12], [B_sc], cowrite=True)
            TS("dve", sc[:, 1:2], sc[:, 0:1], LAMBDA_INIT, ALU.add, [B_sc], [B_sc], s2=-1.0, op1=ALU.mult, cowrite=True)
            pos = sb("pos", [128, NT], F32)
            b_pos = Buf()
            DMA("sp", pos[:, 0:NTK], posk[:, :], writes=[b_pos], dsem=d0c, cowrite=True)
            DMA("sp", pos[:, NTK:NT], posq[:, :], writes=[b_pos], dsem=d0c, cowrite=True)
            invf = np.concatenate([
                (np.float32(THETA) ** (-np.arange(0, 16, 2, dtype=np.float32) / np.float32(16))).astype(np.float32),
                (np.float32(THETA) ** (-np.arange(0, 64, 2, dtype=np.float32) / np.float32(64))).astype(np.float32)])
            CH = 24
            PI = math.pi
            C1 = 6.28125
            C2 = 2.0 * math.pi - C1
            PIC = 3.1415925
            tA = sb("tA", [128, CH, 40], F32)
            tB = sb("tB", [128, CH, 40], F32)
            tC = sb("tC", [128, CH, 40], F32)
            tD = sb("tD", [128, CH, 40], F32)
            tI = sb("tI", [128, CH, 40], I32)
            rtt = sb("rtt", [128, CH, 160], F32)
            bA, bB, bC, bD, bI, bR = Buf(), Buf(), Buf(), Buf(), Buf(), Buf()
            drt = S.dsem()
            for c0 in range(0, NT, CH):
                n = min(CH, NT - c0)
                A, Bt, C, Dt, It = tA[:, 0:n, :], tB[:, 0:n, :], tC[:, 0:n, :], tD[:, 0:n, :], tI[:, 0:n, :]
                for j in range(40):
                    TS("dve", tA[:, 0:n, j], pos[:, c0:c0 + n], float(invf[j]), ALU.mult, [b_pos], [bA], cowrite=(j > 0))
                TS("dve", It, A, 1.0 / (2.0 * PI), ALU.mult, [bA], [bI])
                CP("dve", C, It, [bI], [bC])
                STT(Dt, C, -C1, A, ALU.mult, ALU.add, [bC, bA], [bD])
                STT(A, C, -C2, Dt, ALU.mult, ALU.add, [bC, bD], [bA])

                def wrap(src, bs, tmp, bt, dst, bd_):
                    TS("dve", tmp, src, PI, ALU.is_gt, [bs], [bt], s2=-2.0 * PI, op1=ALU.mult)
                    TT("dve", dst, src, tmp, ALU.add, [bs, bt], [bd_])
                    TS("dve", tmp, dst, -PI, ALU.is_lt, [bd_], [bt], s2=2.0 * PI, op1=ALU.mult)
                    TT("dve", src, dst, tmp, ALU.add, [bd_, bt], [bs])
                    TS("dve", src, src, -PIC, ALU.max, [bs], [bs], s2=PIC, op1=ALU.min)
                wrap(A, bA, Bt, bB, Dt, bD)
                ACT(C, A, AF.Sin, [bA], [bC])
                TS("dve", Dt, A, PI / 2.0, ALU.add, [bA], [bD])
                wrap(Dt, bD, Bt, bB, A, bA)
                ACT(A, Dt, AF.Sin, [bD], [bA])
                R = rtt[:, 0:n, :]
                CP("dve", rtt[:, 0:n, 0:8], tA[:, 0:n, 0:8], [bA], [bR])
                CP("dve", rtt[:, 0:n, 8:16], tA[:, 0:n, 0:8], [bA], [bR], cowrite=True)
                TS("dve", rtt[:, 0:n, 16:24], tC[:, 0:n, 0:8], -1.0, ALU.mult, [bC], [bR], cowrite=True)
                CP("dve", rtt[:, 0:n, 24:32], tC[:, 0:n, 0:8], [bC], [bR], cowrite=True)
                CP("dve", rtt[:, 0:n, 32:64], tA[:, 0:n, 8:40], [bA], [bR], cowrite=True)
                CP("dve", rtt[:, 0:n, 64:96], tA[:, 0:n, 8:40], [bA], [bR], cowrite=True)
                TS("dve", rtt[:, 0:n, 96:128], tC[:, 0:n, 8:40], -1.0, ALU.mult, [bC], [bR], cowrite=True)
                CP("dve", rtt[:, 0:n, 128:160], tC[:, 0:n, 8:40], [bC], [bR], cowrite=True)
                DMA("sp", rt[:, c0:c0 + n, :], R, [bR], [B_rt], dsem=drt, cowrite=True)
            S.run_block(final=(stop == 1))
            for d_ in (d0, d0b, d0c, drt):
                S.dsem_free(d_)
        if stop == 1:
            return nc

        def rope(src3, C, W, rot, tab, tab_b, src_b, t1, t2, tb, dst3, dst_b):
            o = 0 if rot == 16 else 32
            hh = rot // 2
            cosf = tab[:, o:o + rot]
            s_a = tab[:, o + rot:o + rot + hh]
            s_b = tab[:, o + rot + hh:o + 2 * rot]
            for c in range(C):
                TT("dve", t1[:, c, 0:rot], src3[:, c, 0:rot], cosf, ALU.mult, [src_b, tab_b], [tb], cowrite=True)
                TT("dve", t2[:, c, 0:hh], src3[:, c, hh:rot], s_a, ALU.mult, [src_b, tab_b], [tb], cowrite=True)
                TT("dve", t2[:, c, hh:rot], src3[:, c, 0:hh], s_b, ALU.mult, [src_b, tab_b], [tb], cowrite=True)
            TT("dve", dst3[:, :, 0:rot], t1[:, 0:C, 0:rot], t2[:, 0:C, 0:rot], ALU.add, [tb], [dst_b], cowrite=True)
            if W > rot:
                CP("dve", dst3[:, :, rot:W], src3[:, :, rot:W], [src_b], [dst_b], cowrite=True)

        def proj_phase(isK):
            nt = NTK if isK else NTQ
            xin = xall if isK else xq
            t_off = 0 if isK else NTK
            NA = 2624 if isK else 1536
            NB = 2048 if isK else 1536
            with contextlib.ExitStack() as es:
                def sb(name, shape, dt):
                    return es.enter_context(nc.sbuf_tensor(name, shape, dt))

                def ps(name, shape, dt):
                    return es.enter_context(nc.psum_tensor(name, shape, dt))
                pf = "k" if isK else "q"
                w = sb(pf + "w", [128, 16, NA], BF16)
                wb2 = sb(pf + "wb2", [128, 4, NB], BF16)
                gat = sb(pf + "gat", [128, D], F32)
                gl = sb(pf + "gl", [128, 512], F32)
                b_w, b_wb2, b_g = Buf(), Buf(), Buf()
                dw, dwb, dwg = S.dsem(), S.dsem(), S.dsem()
                win = wb_in.rearrange("(kc p) c -> p kc c", p=128)
                if isK:
                    DMAS("sp", w[:, :, 0:2048], win[:, :, 1024:3072], 16, [wbufs["in"]], [b_w], dsem=dw)
                    DMAS("sp", w[:, :, 2048:2624], win[:, :, 3584:4160], 4, [wbufs["in"]], [b_w], dsem=dw)
                    DMAS("sp", wb2[:], wb_kvb.rearrange("(kc p) c -> p kc c", p=128), 4, [wbufs["kvb"]], [b_wb2], dsem=dwb)
                    DMA("sp", gl[:], g_kv_d.partition_broadcast(128), (), [b_g], dsem=dwg, cowrite=True)
                else:
                    DMAS("sp", w[:, :, 0:1024], win[:, :, 0:1024], 8, [wbufs["in"]], [b_w], dsem=dw)
                    DMAS("sp", w[:, :, 1024:1536], win[:, :, 3072:3584], 4, [wbufs["in"]], [b_w], dsem=dw)
                    DMAS("sp", wb2[:], wb_qb.rearrange("(kc p) c -> p kc c", p=128), 4, [wbufs["qb"]], [b_wb2], dsem=dwb)
                    DMA("sp", gl[:], g_q_d.partition_broadcast(128), (), [b_g], dsem=dwg, cowrite=True)
                DMA("sp", gat[:], g_attn_d.partition_broadcast(128), (), [b_g], dsem=dwg, cowrite=True)
                if isK:
                    cast_weight("bd", wb_bd, w_bd, 1024, D)
                    cast_weight("bm", wb_bm, w_bm, 1024, D)
                    cast_weight("out", wb_out, w_out, D, D)
                    cast_weight("gate", wb_gate, w_gate, D, FF)
                    cast_weight("up", wb_up, w_up, D, FF)
                    cast_weight("down", wb_down, w_down, FF, D)

                xR = Ring(S, [sb(f"{pf}x{i}", [128, D], F32) for i in range(2)])
                rtR = Ring(S, [sb(f"{pf}rt{i}", [128, 4, 160], F32) for i in range(2)])
                junk = sb(pf + "junk", [128, D], BF16)
                b_junk = Buf()
                hR = Ring(S, [sb(f"{pf}h{i}", [128, D], BF16) for i in range(2)], dma=False)
                hTR = Ring(S, [sb(f"{pf}hT{i}", [128, 16, 128], BF16) for i in range(2)], dma=False)
                stR = Ring(S, [sb(f"{pf}st{i}", [128, 8], F32) for i in range(3)], dma=False)
                t1 = sb(pf + "t1", [128, 8, 64], F32)
                t2 = sb(pf + "t2", [128, 8, 64], F32)
                b_t = Buf()
                tmA = sb(pf + "tmA", [128, 1024], BF16)
                lat = sb(pf + "lat", [128, 512], BF16)
                latT = sb(pf + "latT", [128, 4, 128], BF16)
                tmN = sb(pf + "tmN", [128, 8, 128], BF16)
                tmR = sb(pf + "tmR", [128, 8, 64], BF16)
                b_tmA, b_lat, b_latT, b_tmN, b_tmR = Buf(), Buf(), Buf(), Buf(), Buf()
                stA = sb(pf + "stA", [128, 8, 512], BF16)
                stN = sb(pf + "stN", [128, 8, 512], BF16)
                stRr = sb(pf + "stR", [128, 4, 512], BF16)
                b_stA, b_stN, b_stR = Buf(), Buf(), Buf()
                d_stA, d_stN, d_stR = S.dsem(), S.dsem(), S.dsem()
                if isK:
                    stV = sb("kstV", [128, 8, 4, 132], BF16)
                    stVm = sb("kstVm", [128, 8, 4, 132], BF16)
                    b_stV, b_stVm = Buf(), Buf()
                    S.op("dve", lambda e: e.memset(stV[:, :, :, 128:132], 1.0), (), [b_stV], cowrite=True)
                    S.op("dve", lambda e: e.memset(stVm[:, :, :, 128:132], 1.0), (), [b_stVm], cowrite=True)
                    d_stV, d_stVm = S.dsem(), S.dsem()
                tp = ps(pf + "tp", [128, 16, 128], BF16)
                pj = ps(pf + "pj", [128, 2, 512], F32)
                tk = ps(pf + "tk", [128, 2, 8, 128], BF16)
                tcp = ps(pf + "tc", [128, 4, 128], BF16)
                b_tp, b_tc = Buf(), Buf()
                pjR = Ring(S, [pj[:, i, :] for i in range(2)], dma=False)
                tkR = Ring(S, [tk[:, i, :, :] for i in range(2)], dma=False)

                ngrp = nt // 4

                def head_norm(ti):
                    x_, b_x, d_x = xR.next()
                    DMA("sp", x_[:], xin[ti * 128:(ti + 1) * 128, :], (), [b_x], dsem=d_x)
                    st_, b_st, _ = stR.next()
                    ACT(junk[:], x_[:], AF.Square, [b_x], [b_junk, b_st], accum_out=st_[:, 0:1])
                    rstd_from_ss(st_[:, 0:1], 1, D, b_st, st_[:, 1:2], st_[:, 2:3], b_st)
                    h_, b_h, _ = hR.next()
                    STT(h_[:], x_[:], st_[:, 2:3], gat[:], ALU.mult, ALU.mult, [b_x, b_st, b_g], [b_h])
                    return h_, b_h

                def head_tr(hh):
                    h_, b_h = hh
                    for kc in range(16):
                        TR(tp[:, kc, :], h_[:, kc * 128:(kc + 1) * 128], [b_h], [b_tp], cowrite=(kc > 0))
                    hT_, b_hT, _ = hTR.next()
                    CP("dve", hT_[:, 0:8, :], tp[:, 0:8, :], [b_tp], [b_hT])
                    CP("act", hT_[:, 8:16, :], tp[:, 8:16, :], [b_tp], [b_hT], cowrite=True)
                    return hT_, b_hT

                pend = head_tr(head_norm(0))
                pend_h = None
                for g in range(ngrp):
                    rtt_, b_rtt, d_rtt = rtR.next()
                    DMA("sp", rtt_[:], rt[:, t_off + g * 4:t_off + g * 4 + 4, :], [B_rt], [b_rtt], dsem=d_rtt)
                    for j in range(4):
                        ti = g * 4 + j
                        hT_, b_hT = pend
                        tab = rtt_[:, j, :]

                        def proj_block(c0, ncol, hT_=hT_, b_hT=b_hT):
                            p_, b_p, _ = pjR.next()
                            for kc in range(16):
                                MM(p_[:, 0:ncol], hT_[:, kc, :], w[:, kc, c0:c0 + ncol], kc == 0, kc == 15,
                                   [b_hT, b_w], [b_p], cowrite=(kc > 0))
                            return p_, b_p

                        lat0 = 2048 if isK else 1024
                        p_, b_p = proj_block(lat0, 512)
                        st2, b_st2, _ = stR.next()
                        ACT(junk[:, 0:512], p_, AF.Square, [b_p], [b_junk, b_st2], accum_out=st2[:, 0:1])
                        rstd_from_ss(st2[:, 0:1], 1, 512, b_st2, st2[:, 1:2], st2[:, 2:3], b_st2)
                        STT(lat[:], p_, st2[:, 2:3], gl[:], ALU.mult, ALU.mult, [b_p, b_st2, b_g], [b_lat])
                        if ti + 1 < nt:
                            pend_h = head_norm(ti + 1)
                        for blk in range(2):
                            p_, b_p = proj_block(blk * 512, 512)
                            if DBG == 5:
                                continue
                            rope(p_.rearrange("p (c w) -> p c w", w=64), 8, 64, 16, tab, b_rtt, b_p, t1, t2, b_t,
                                 tmA[:, blk * 512:(blk + 1) * 512].rearrange("p (c w) -> p c w", w=64), b_tmA)
                        if DBG == 5 or DBG == 6:
                            continue
                        if isK:
                            for blk in range(2):
                                p_, b_p = proj_block(1024 + blk * 512, 512)
                                CP("act", stV[:, blk * 4:(blk + 1) * 4, j, 0:128], p_.rearrange("p (h d) -> p h d", d=128),
                                   [b_p], [b_stV], cowrite=True)
                        if DBG == 3:
                            continue
                        if DBG == 9:
                            continue
                        for kc in range(4):
                            TR(tcp[:, kc, :], lat[:, kc * 128:(kc + 1) * 128], [b_lat], [b_tc], cowrite=(kc > 0))
                        CP("act", latT[:], tcp[:], [b_tc], [b_latT])
                        if DBG == 10:
                            continue
                        bw = 512 if isK else 384
                        for cb in range(4):
                            p_, b_p, _ = pjR.next()
                            for kc in range(4):
                                MM(p_[:, 0:bw], latT[:, kc, :], wb2[:, kc, cb * bw:(cb + 1) * bw], kc == 0, kc == 3,
                                   [b_latT, b_wb2], [b_p], cowrite=(kc > 0))
                            if DBG == 11:
                                continue
                            if isK:
                                pv = p_.rearrange("p (h t d) -> p h t d", t=2, d=128)
                                ceng = "dve" if cb % 2 == 0 else "act"
                                for h2 in range(2):
                                    CP(ceng, tmN[:, 2 * cb + h2, :], p_[:, h2 * 256:h2 * 256 + 128], [b_p], [b_tmN], cowrite=True)
                                for h2 in range(2):
                                    CP(ceng, stVm[:, 2 * cb + h2, j, 0:128], p_[:, h2 * 256 + 128:h2 * 256 + 256], [b_p], [b_stVm], cowrite=True)
                            else:
                                pv = p_[:, 0:384].rearrange("p (h d) -> p h d", d=192)
                                for h2 in range(2):
                                    CP("dve", tmN[:, 2 * cb + h2, :], p_[:, h2 * 192:h2 * 192 + 128], [b_p], [b_tmN], cowrite=True)
                                rope(pv[:, :, 128:192], 2, 64, 64, tab, b_rtt, b_p, t1, t2, b_t,
                                     tmR[:, 2 * cb:2 * cb + 2, :], b_tmR)
                        if DBG in (11, 12):
                            continue
                        if ti + 1 < nt:
                            pend = head_tr(pend_h)
                        tk_, b_tk, _ = tkR.next()
                        if DBG != 8:
                            for hh in range(8):
                                TR(tk_[:, hh, :], tmA[:, hh * 128:(hh + 1) * 128], [b_tmA], [b_tk], cowrite=(hh > 0))
                        if DBG != 7:
                            CP("dve", stA[:, :, j * 128:(j + 1) * 128], tk_, [b_tk], [b_stA], cowrite=True)
                        if DBG in (2, 7, 8):
                            continue
                        tk_, b_tk, _ = tkR.next()
                        for hh in range(8):
                            TR(tk_[:, hh, :], tmN[:, hh, :], [b_tmN], [b_tk], cowrite=(hh > 0))
                        CP("dve", stN[:, :, j * 128:(j + 1) * 128], tk_, [b_tk], [b_stN], cowrite=True)
                        if DBG == 4:
                            continue
                        if isK:
                            p_, b_p = proj_block(2560, 64)
                            p3 = p_[:, 0:64].unsqueeze(1)
                            rope(p3, 1, 64, 64, tab, b_rtt, b_p, t1, t2, b_t, tmR[:, 0:1, :], b_tmR)
                            CP("dve", tmR[:, 1:2, :], tmR[:, 0:1, :], [b_tmR], [b_tmR], cowrite=True)
                            tk_, b_tk, _ = tkR.next()
                            TR(tk_[:, 0, :], tmR[:, 0:2, :].rearrange("p a d -> p (a d)"), [b_tmR], [b_tk], cowrite=False)
                            CP("act", stRr[:, 0, j * 128:(j + 1) * 128], tk_[:, 0, :], [b_tk], [b_stR], cowrite=True)
                        else:
                            tk_, b_tk, _ = tkR.next()
                            for pr in range(4):
                                TR(tk_[:, pr, :], tmR[:, 2 * pr:2 * pr + 2, :].rearrange("p a d -> p (a d)"), [b_tmR], [b_tk],
                                   cowrite=(pr > 0))
                            CP("act", stRr[:, :, j * 128:(j + 1) * 128], tk_[:, 0:4, :], [b_tk], [b_stR], cowrite=True)
                    c0 = g * 512
                    if isK:
                        DMAS("sp", KTda[:, :, c0:c0 + 512].rearrange("h p t -> p h t"), stA[:], 2, [b_stA], [B_kv], dsem=d_stA)
                        DMAS("sp", KTn[:, :, c0:c0 + 512].rearrange("h p t -> p h t"), stN[:], 2, [b_stN], [B_kv], dsem=d_stN)
                        DMA("sp", KTr[:, c0:c0 + 512], stRr[:, 0, :], [b_stR], [B_kv], dsem=d_stR, cowrite=True)
                        DMAS("sp", Vda[:, :, g * 4:g * 4 + 4, :].rearrange("h p t d -> p h t d"), stV[:], 2, [b_stV], [B_kv], dsem=d_stV)
                        DMAS("sp", Vm[:, :, g * 4:g * 4 + 4, :].rearrange("h p t d -> p h t d"), stVm[:], 2, [b_stVm], [B_kv], dsem=d_stVm)
                    else:
                        DMAS("sp", QTda[:, :, c0:c0 + 512].rearrange("h p t -> p h t"), stA[:], 2, [b_stA], [B_q], dsem=d_stA)
                        DMAS("sp", QTn[:, :, c0:c0 + 512].rearrange("h p t -> p h t"), stN[:], 2, [b_stN], [B_q], dsem=d_stN)
                        DMA("sp", QTr[:, :, c0:c0 + 512].rearrange("h p t -> p h t"), stRr[:], [b_stR], [B_q], dsem=d_stR, cowrite=True)
                S.run_block(final=(stop == (2 if isK else 3)))
                for r in (xR, rtR):
                    r.free()
                for d_ in [dw, dwb, dwg, d_stA, d_stN, d_stR] + ([d_stV, d_stVm] if isK else []):
                    S.dsem_free(d_)

        proj_phase(True)
        if stop == 2:
            return nc
        proj_phase(False)
        if stop == 3:
            return nc

        with contextlib.ExitStack() as es:
            def sb(name, shape, dt):
                return es.enter_context(nc.sbuf_tensor(name, shape, dt))

            def ps(name, shape, dt):
                return es.enter_context(nc.psum_tensor(name, shape, dt))
            NKT = KC // 128
            ktR = Ring(S, [sb(f"a_kt{i}", [128, KC], BF16) for i in range(3)])
            krR = Ring(S, [sb(f"a_kr{i}", [128, KC], BF16) for i in range(2)])
            v_tiles = [sb(f"a_v{i}", [128, NKT, 132], BF16) for i in range(3)]
            vR = Ring(S, v_tiles)
            qtR = Ring(S, [sb(f"a_qt{i}", [128, 512], BF16) for i in range(5)])
            qrR = Ring(S, [sb(f"a_qr{i}", [128, 512], BF16) for i in range(2)])
            pTR = Ring(S, [sb(f"a_pT{i}", [128, 512], BF16) for i in range(4)], dma=False)
            o1 = sb("o1", [128, 4, 128], F32)
            o2 = sb("o2", [128, 4, 128], F32)
            o3 = sb("o3", [128, 4, 128], F32)
            osq = sb("osq", [128, 4, 128], F32)
            onb = sb("onb", [128, 4, 128], BF16)
            fst = sb("fst", [128, 16], F32)
            gsub = sb("gsub", [128, 128], F32)
            b_o1, b_o2, b_o3, b_osq, b_onb, b_fst, b_gsub = Buf(), Buf(), Buf(), Buf(), Buf(), Buf(), Buf()
            oT = sb("oT", [128, 16, 512], BF16)
            b_oT = Buf()
            x1 = sb("x1", [128, 4, D], F32)
            b_x1 = [Buf() for _ in range(4)]
            d_x1 = [S.dsem() for _ in range(4)]
            d_y = [S.dsem() for _ in range(4)]
            hT = sb("hT", [128, 16, 512], BF16)
            b_hT = Buf()
            hb = sb("hb", [128, D], BF16)
            b_hb = Buf()
            big = sb("big", [128, 22, 512], BF16)
            b_big = [Buf() for _ in range(22)]
            gbuf = sb("gbuf", [128, D], F32)
            b_gbuf = Buf()
            d_g = S.dsem()
            junk = sb("ajunk", [128, D], BF16)
            b_junk = Buf()
            pst = sb("pst", [128, 16], F32)
            b_pst = Buf()
            sg1 = sb("sg1", [128, 512], F32)
            sg2 = sb("sg2", [128, 512], F32)
            b_sg1, b_sg2 = Buf(), Buf()
            wR = Ring(S, [sb(f"wbuf{i}", [128, 8192], BF16) for i in range(3)])
            dg0 = S.dsem()
            DMA("sp", gsub[:], g_sub_d.partition_broadcast(128), (), [b_gsub], dsem=dg0)
            TS("dve", gsub[:], gsub[:], 1.0 - LAMBDA_INIT, ALU.mult, [b_gsub], [b_gsub], cowrite=True)

            stp = ps("stp", [128, 3, 512], F32)
            acc = ps("acc", [128, 4, 512], F32)
            tpo = ps("tpo", [128, 4, 128], BF16)
            stpR = Ring(S, [stp[:, i, :] for i in range(3)], dma=False)
            b_acc, b_tpo = Buf(), Buf()
            accB = [Buf() for _ in range(4)]

            SC_DA = 64 ** -0.5
            SC_MLA = 192 ** -0.5

            def load_w(view_fn):
                t_, b_, d_ = wR.next()
                view_fn(t_, b_, d_)
                return t_, b_

            for g in range(NG):
                prompt = g < NPo // 512
                k0 = 0 if prompt else SP
                klen = SP if prompt else SS
                q0 = g * 512
                nch = klen // KC
                steps = []
                for u in range(16):
                    for m in range(2 if u < 8 else 1):
                        for c in range(nch):
                            for t in range(NKT):
                                steps.append((u, m, c, t))
                cur = {"u": None, "chunk": None}
                rec = {}

                def emit_qk(i):
                    u, m, c, t = steps[i]
                    da = u < 8
                    h = u % 8
                    if cur["u"] != u:
                        cur["u"] = u
                        if da:
                            qs = []
                            for mm_ in range(2):
                                qt_, b_qt, d_qt = qtR.next()
                                DMA("sp", qt_[:], QTda[h, :, q0:q0 + 512], [B_q], [b_qt], dsem=d_qt)
                                oh = slice(64, 128) if mm_ == 0 else slice(0, 64)
                                S.op("pool", lambda e, qt_=qt_, oh=oh: e.memset(qt_[oh, :], 0.0), [b_qt], [b_qt], cowrite=True)
                                qs.append((qt_, b_qt))
                            cur["q"] = (qs, None, None, None)
                        else:
                            qt_, b_qt, d_qt = qtR.next()
                            DMA("sp", qt_[:], QTn[h, :, q0:q0 + 512], [B_q], [b_qt], dsem=d_qt)
                            qr_, b_qr, d_qr = qrR.next()
                            DMA("sp", qr_[:], QTr[h // 2, :, q0:q0 + 512], [B_q], [b_qr], dsem=d_qr)
                            oh = slice(64, 128) if h % 2 == 0 else slice(0, 64)
                            S.op("pool", lambda e, qr_=qr_, oh=oh: e.memset(qr_[oh, :], 0.0), [b_qr], [b_qr], cowrite=True)
                            cur["q"] = (qt_, b_qt, qr_, b_qr)
                    qt_, b_qt, qr_, b_qr = cur["q"]
                    if cur["chunk"] != (u, m, c):
                        cur["chunk"] = (u, m, c)
                        ks = k0 + c * KC
                        kt_, b_kt, d_kt = ktR.next()
                        v_, b_v, d_v = vR.next()
                        if da:
                            DMA("sp", kt_[:], KTda[h, :, ks:ks + KC], [B_kv], [b_kt], dsem=d_kt)
                            DMA("sp", v_[:], Vda[h, :, ks // 128:ks // 128 + NKT, :], [B_kv], [b_v], dsem=d_v)
                            cur["kv"] = (kt_, b_kt, None, None, v_, b_v)
                        else:
                            DMA("sp", kt_[:], KTn[h, :, ks:ks + KC], [B_kv], [b_kt], dsem=d_kt)
                            kr_, b_kr, d_kr = krR.next()
                            DMA("sp", kr_[:], KTr[:, ks:ks + KC], [B_kv], [b_kr], dsem=d_kr)
                            DMA("sp", v_[:], Vm[h, :, ks // 128:ks // 128 + NKT, :], [B_kv], [b_v], dsem=d_v)
                            cur["kv"] = (kt_, b_kt, kr_, b_kr, v_, b_v)
                    kt_, b_kt, kr_, b_kr, v_, b_v = cur["kv"]
                    s_, b_s, _ = stpR.next()
                    ksl = slice(t * 128, (t + 1) * 128)
                    if da:
                        qt_, b_qt = qt_[m]
                        MM(s_, kt_[:, ksl], qt_[:], True, True, [b_kt, b_qt], [b_s], cowrite=False)
                    else:
                        MM(s_, kt_[:, ksl], qt_[:], True, False, [b_kt, b_qt], [b_s], cowrite=False)
                        MM(s_, kr_[:, ksl], qr_[:], False, True, [b_kr, b_qr], [b_s], cowrite=True)
                    rec[i] = (s_, b_s, v_, b_v)

                def emit_exp_pv(i):
                    u, m, c, t = steps[i]
                    da = u < 8
                    s_, b_s, v_, b_v = rec.pop(i)
                    first = (c == 0 and t == 0)
                    last = (c == nch - 1 and t == NKT - 1)
                    p_, b_p, _ = pTR.next()
                    ACT(p_[:], s_, AF.Exp, [b_s], [b_p], scale=(SC_DA if da else SC_MLA))
                    for s4 in range(4):
                        MM(acc[:, s4, 0:129], p_[:, s4 * 128:(s4 + 1) * 128], v_[:, t, 0:129], first, last,
                           [b_p, b_v], [accB[s4]], cowrite=not first)
                    return last

                def finalize(u, m):
                    da = u < 8
                    rl = fst[:, 0:4]
                    S.op("dve", lambda e, rl=rl: e.reciprocal(out=rl.unsqueeze(2), in_=acc[:, :, 128:129]), accB, [b_fst])
                    if da and m == 0:
                        for s4 in range(4):
                            TS("dve", o1[:, s4, :], acc[:, s4, 0:128], rl[:, s4:s4 + 1], ALU.mult, accB + [b_fst], [b_o1], cowrite=True)
                        return
                    if da:
                        for s4 in range(4):
                            TS("dve", o2[:, s4, :], acc[:, s4, 0:128], rl[:, s4:s4 + 1], ALU.mult, accB + [b_fst], [b_o2], cowrite=True)
                        STT(o3[:], o2[:], sc[:, 1:2], o1[:], ALU.mult, ALU.add, [b_o2, b_o1, B_sc], [b_o3])
                        TT("dve", osq[:], o3[:], o3[:], ALU.mult, [b_o3], [b_osq])
                        S.op("dve", lambda e: e.tensor_reduce(out=fst[:, 4:8], in_=osq[:], axis=AX.X, op=ALU.add), [b_osq], [b_fst], cowrite=True)
                        rstd_from_ss(fst[:, 4:8], 4, 128, b_fst, fst[:, 8:12], fst[:, 12:16], b_fst)
                        for s4 in range(4):
                            STT(onb[:, s4, :], o3[:, s4, :], fst[:, 12 + s4:13 + s4], gsub[:], ALU.mult, ALU.mult,
                                [b_o3, b_fst, b_gsub], [b_onb], cowrite=True)
                    else:
                        for s4 in range(4):
                            TS("dve", onb[:, s4, :], acc[:, s4, 0:128], rl[:, s4:s4 + 1], ALU.mult, accB + [b_fst], [b_onb], cowrite=True)
                    for s4 in range(4):
                        TR(tpo[:, s4, :], onb[:, s4, :], [b_onb], [b_tpo], cowrite=(s4 > 0))
                    CP("dve", oT[:, u, :], tpo[:].rearrange("p a b -> p (a b)"), [b_tpo], [b_oT], cowrite=True)

                LOOK = 2
                nst = len(steps)
                for i in range(min(LOOK, nst)):
                    emit_qk(i)
                for i in range(nst):
                    if i + LOOK < nst:
                        emit_qk(i + LOOK)
                    if emit_exp_pv(i):
                        finalize(steps[i][0], steps[i][1])

                DMA("sp", gbuf[:], g_attn_d.partition_broadcast(128), (), [b_gbuf], dsem=d_g)
                for j in range(4):
                    r0 = q0 + j * 128
                    DMA("sp", x1[:, j, :], xq[r0:r0 + 128, :], (), [b_x1[j]], dsem=d_x1[j])

                def norm_T(j, dstT, b_dstT, first):
                    ACT(junk[:], x1[:, j, :], AF.Square, [b_x1[j]], [b_junk, b_pst], accum_out=pst[:, 0:1])
                    rstd_from_ss(pst[:, 0:1], 1, D, b_pst, pst[:, 1:2], pst[:, 2:3], b_pst)
                    STT(hb[:], x1[:, j, :], pst[:, 2:3], gbuf[:], ALU.mult, ALU.mult, [b_x1[j], b_pst, b_gbuf], [b_hb])
                    for q4 in range(4):
                        for kk in range(4):
                            kc = q4 * 4 + kk
                            TR(tpo[:, kk, :], hb[:, kc * 128:(kc + 1) * 128], [b_hb], [b_tpo], cowrite=(kk > 0))
                        CP("dve" if q4 % 2 == 0 else "act", dstT[:, q4 * 4:q4 * 4 + 4, j * 128:(j + 1) * 128], tpo[:], [b_tpo], [b_dstT],
                           cowrite=not (first and q4 == 0))
                for j in range(4):
                    norm_T(j, hT, b_hT, j == 0)
                win = wb_in.rearrange("(kc p) c -> p kc c", p=128)
                wbd_v = wb_bd.rearrange("(kc p) c -> p kc c", p=128)
                wbm_v = wb_bm.rearrange("(kc p) c -> p kc c", p=128)
                for fc in range(16):
                    wt, b_wt, d_wt = wR.next()
                    wv = wt[:, 0:48 * 128].rearrange("p (k c) -> p k c", c=128)
                    DMA("sp", wv[:, 0:4, :], win[:, 0:4, 4160 + fc * 128:4160 + (fc + 1) * 128], [wbufs["in"]], [b_wt], dsem=d_wt)
                    DMAS("sp", wv[:, 4:16, :], win[:, 4:16, 4160 + fc * 128:4160 + (fc + 1) * 128], 3, [wbufs["in"]], [b_wt], dsem=d_wt)
                    DMAS("sp", wv[:, 16:32, :], win[:, :, 4160 + D + fc * 128:4160 + D + (fc + 1) * 128], 4, [wbufs["in"]], [b_wt], dsem=d_wt)
                    DMAS("sp", wv[:, 32:40, :], wbd_v[:, :, fc * 128:(fc + 1) * 128], 2, [wbufs["bd"]], [b_wt], dsem=d_wt)
                    DMAS("sp", wv[:, 40:48, :], wbm_v[:, :, fc * 128:(fc + 1) * 128], 2, [wbufs["bm"]], [b_wt], dsem=d_wt)
                    for kc in range(16):
                        MM(acc[:, 0, :], wv[:, kc, :], hT[:, kc, :], kc == 0, kc == 15, [b_wt, b_hT], [accB[0]], cowrite=(kc > 0))
                    for kc in range(16):
                        MM(acc[:, 1, :], wv[:, 16 + kc, :], hT[:, kc, :], kc == 0, kc == 15, [b_wt, b_hT], [accB[1]], cowrite=(kc > 0))
                    for kc in range(8):
                        MM(acc[:, 2, :], wv[:, 32 + kc, :], oT[:, kc, :], kc == 0, kc == 7, [b_wt, b_oT], [accB[2]], cowrite=(kc > 0))
                    for kc in range(8):
                        MM(acc[:, 3, :], wv[:, 40 + kc, :], oT[:, 8 + kc, :], kc == 0, kc == 7, [b_wt, b_oT], [accB[3]], cowrite=(kc > 0))
                    ACT(sg1[:], acc[:, 0, :], AF.Sigmoid, [accB[0]], [b_sg1])
                    ACT(sg2[:], acc[:, 1, :], AF.Sigmoid, [accB[1]], [b_sg2])
                    TT("dve", sg1[:], sg1[:], acc[:, 2, :], ALU.mult, [b_sg1, accB[2]], [b_sg1])
                    TT("dve", sg2[:], sg2[:], acc[:, 3, :], ALU.mult, [b_sg2, accB[3]], [b_sg2])
                    TT("dve", big[:, fc, :], sg1[:], sg2[:], ALU.add, [b_sg1, b_sg2], [b_big[fc]])
                wout_v = wb_out.rearrange("(kc p) c -> p kc c", p=128)
                for cb in range(4):
                    wt, b_wt, d_wt = wR.next()
                    wv = wt[:].rearrange("p (k c) -> p k c", c=512)
                    DMA("sp", wv[:, 0:4, :], wout_v[:, 0:4, cb * 512:(cb + 1) * 512], [wbufs["out"]], [b_wt], dsem=d_wt)
                    DMAS("sp", wv[:, 4:16, :], wout_v[:, 4:16, cb * 512:(cb + 1) * 512], 3, [wbufs["out"]], [b_wt], dsem=d_wt)
                    for j in range(4):
                        for kc in range(16):
                            MM(acc[:, j, :], big[:, kc, j * 128:(j + 1) * 128], wv[:, kc, :], kc == 0, kc == 15, [b_wt, b_big[kc]], [accB[j]],
                               cowrite=(kc > 0))
                        xs = x1[:, j, cb * 512:(cb + 1) * 512]
                        TT("dve", xs, xs, acc[:, j, :], ALU.add, [b_x1[j], accB[j]], [b_x1[j]], cowrite=True)
                DMA("sp", gbuf[:], g_ffn_d.partition_broadcast(128), (), [b_gbuf], dsem=d_g)
                for j in range(4):
                    norm_T(j, hT, b_hT, j == 0)
                wg_v = wb_gate.rearrange("(kc p) c -> p kc c", p=128)
                wu_v = wb_up.rearrange("(kc p) c -> p kc c", p=128)
                wd_v = wb_down.rearrange("(hc p) c -> p hc c", p=128)
                for half in range(2):
                    for hp2 in range(11):
                        hc0 = half * 22 + hp2 * 2
                        wt, b_wt, d_wt = wR.next()
                        wv = wt[:].rearrange("p (s k c) -> p s k c", s=2, c=256)
                        DMA("sp", wv[:, 0, 0:4, :], wg_v[:, 0:4, hc0 * 128:(hc0 + 2) * 128], [wbufs["gate"]], [b_wt], dsem=d_wt)
                        DMAS("sp", wv[:, 0, 4:16, :], wg_v[:, 4:16, hc0 * 128:(hc0 + 2) * 128], 3, [wbufs["gate"]], [b_wt], dsem=d_wt)
                        DMAS("sp", wv[:, 1, :, :], wu_v[:, :, hc0 * 128:(hc0 + 2) * 128], 4, [wbufs["up"]], [b_wt], dsem=d_wt)
                        for i2 in range(2):
                            ga, ua = (0, 1) if i2 == 0 else (2, 3)
                            for kc in range(16):
                                MM(acc[:, ga, :], wv[:, 0, kc, i2 * 128:(i2 + 1) * 128], hT[:, kc, :], kc == 0, kc == 15, [b_wt, b_hT], [accB[ga]],
                                   cowrite=(kc > 0))
                            for kc in range(16):
                                MM(acc[:, ua, :], wv[:, 1, kc, i2 * 128:(i2 + 1) * 128], hT[:, kc, :], kc == 0, kc == 15, [b_wt, b_hT], [accB[ua]],
                                   cowrite=(kc > 0))
                            sg, b_sg = (sg1, b_sg1) if i2 == 0 else (sg2, b_sg2)
                            ACT(sg[:], acc[:, ga, :], AF.Silu, [accB[ga]], [b_sg])
                            ci = hp2 * 2 + i2
                            TT("dve", big[:, ci, :], sg[:], acc[:, ua, :], ALU.mult, [b_sg, accB[ua]], [b_big[ci]])
                    for cb in range(4):
                        wts = []
                        for part in range(2):
                            wt, b_wt, d_wt = wR.next()
                            wv = wt[:, 0:11 * 512].rearrange("p (k c) -> p k c", c=512)
                            hc0 = half * 22 + part * 11
                            DMA("sp", wv[:, 0:4, :], wd_v[:, hc0:hc0 + 4, cb * 512:(cb + 1) * 512], [wbufs["down"]], [b_wt], dsem=d_wt)
                            DMAS("sp", wv[:, 4:11, :], wd_v[:, hc0 + 4:hc0 + 11, cb * 512:(cb + 1) * 512], 2, [wbufs["down"]], [b_wt], dsem=d_wt)
                            wts.append((wv, b_wt))
                        for j in range(4):
                            for ci in range(22):
                                wv, b_wt = wts[ci // 11]
                                MM(acc[:, j, :], big[:, ci, j * 128:(j + 1) * 128], wv[:, ci % 11, :], ci == 0, ci == 21, [b_wt, b_big[ci]], [accB[j]],
                                   cowrite=(ci > 0))
                            xs = x1[:, j, cb * 512:(cb + 1) * 512]
                            TT("dve", xs, xs, acc[:, j, :], ALU.add, [b_x1[j], accB[j]], [b_x1[j]], cowrite=True)
                DMA("sp", gbuf[:], g_fin_d.partition_broadcast(128), (), [b_gbuf], dsem=d_g)
                for j in range(4):
                    ACT(junk[:], x1[:, j, :], AF.Square, [b_x1[j]], [b_junk, b_pst], accum_out=pst[:, 0:1])
                    rstd_from_ss(pst[:, 0:1], 1, D, b_pst, pst[:, 1:2], pst[:, 2:3], b_pst)
                    STT(x1[:, j, :], x1[:, j, :], pst[:, 2:3], gbuf[:], ALU.mult, ALU.mult, [b_x1[j], b_pst, b_gbuf], [b_x1[j]], cowrite=True)
                    r0 = q0 + j * 128
                    DMA("sp", y[r0:r0 + 128, :], x1[:, j, :], [b_x1[j]], (), dsem=d_y[j])
            S.run_block(final=True)
    return nc


_NC_CACHE = {}
PARAM_NAMES = ["attn_norm_g", "w_in", "da_lambda_q1", "da_lambda_k1", "da_lambda_q2", "da_lambda_k2", "da_subln_g",
               "mla_q_norm_g", "mla_w_q_b", "mla_kv_norm_g", "mla_w_kv_b", "w_branch_da", "w_branch_mla", "w_out",
               "ffn_norm_g", "w_gate", "w_up", "w_down"]


STOP = 9
NCORES = 8
DBG = 0


def kernel(x_prompt, x_sample, final_norm_g, **params):
    xp = np.asarray(x_prompt, dtype=np.float32)[0]
    xs = np.asarray(x_sample, dtype=np.float32)[0]
    SP, SS = xp.shape[0], xs.shape[0]
    NPo, NSo = SP // 8, SS // 8
    key = (NPo, NSo)
    if key not in _NC_CACHE:
        _NC_CACHE[key] = build_nc(NPo, NSo, STOP)
    nc = _NC_CACHE[key]
    xall = np.ascontiguousarray(np.concatenate([xp, xs], axis=0))
    shared = {"xall": xall, "ident": np.eye(128, dtype=np.float32)}
    for n in PARAM_NAMES:
        a = np.asarray(params[n], dtype=np.float32)
        a = a[0]
        if a.ndim == 1:
            a = a[None, :]
        shared[n] = np.ascontiguousarray(a)
    shared["final_norm_g"] = np.ascontiguousarray(np.asarray(final_norm_g, dtype=np.float32)[None, :])
    pk = np.concatenate([np.arange(SP), np.arange(SS)]).astype(np.float32)
    shared["posk"] = np.ascontiguousarray(pk.reshape(-1, 128).T)
    in_maps = []
    for c in range(8):
        m = dict(shared)
        m["xq"] = np.ascontiguousarray(np.concatenate([xp[c * NPo:(c + 1) * NPo], xs[c * NSo:(c + 1) * NSo]], axis=0))
        pq = np.concatenate([np.arange(c * NPo, (c + 1) * NPo), np.arange(c * NSo, (c + 1) * NSo)]).astype(np.float32)
        m["posq"] = np.ascontiguousarray(pq.reshape(-1, 128).T)
        in_maps.append(m)
    res = run_bass_kernel_spmd(nc, in_maps[:NCORES], core_ids=list(range(NCORES)))
    rr = [res.results[c]["y"] if c < NCORES else np.zeros((NPo + NSo, D), np.float32) for c in range(8)]
    yp = np.concatenate([rr[c][:NPo] for c in range(8)], axis=0)[None]
    ys = np.concatenate([rr[c][NPo:] for c in range(8)], axis=0)[None]
    return (yp.astype(np.float32), ys.astype(np.float32))
```
